# Optimizing a Trainium2 kernel written in Bass

```python
import jax, jax.numpy as jnp
from jax import lax
import numpy as np

D_MODEL = 1024
BATCH = 4
SEQ = 4096
DEPTH = 1

D_FF = 2816
A_HEADS = 8
A_HEAD_DIM = 64
A_WIDTH = A_HEADS * A_HEAD_DIM
MOBA_BLOCK = 256
MOBA_TOPK = 3
Q_CHUNK = 64
G_GROUPS = 8
G_CHUNK = 128
G_WIDTH = 512
G_GROUP_DIM = G_WIDTH // G_GROUPS
IN_WIDTH = 3 * A_WIDTH + 2 * G_WIDTH
N_BRANCHES = 2
EPS = 1e-6

kernel_name = "hybrid_moba_gmlp_macaron_block"


def rmsnorm(x, g):
    xf = x.astype(jnp.float32)
    y = xf * lax.rsqrt(jnp.mean(xf * xf, axis=-1, keepdims=True) + EPS)
    return (y * g.astype(jnp.float32)).astype(x.dtype)


def layernorm(x, g, b):
    xf = x.astype(jnp.float32)
    mu = jnp.mean(xf, axis=-1, keepdims=True)
    var = jnp.mean(jnp.square(xf - mu), axis=-1, keepdims=True)
    y = (xf - mu) * lax.rsqrt(var + EPS)
    return (y * g.astype(jnp.float32) + b.astype(jnp.float32)).astype(x.dtype)


def swiglu(x, w_gate, w_up, w_down):
    return (jax.nn.silu(x @ w_gate) * (x @ w_up)) @ w_down


def moba_attention(q, k, v):
    B, H, S, Dh = q.shape
    n_blocks = -(-S // MOBA_BLOCK)
    s_pad = n_blocks * MOBA_BLOCK
    k_sel = min(MOBA_TOPK, n_blocks)
    pad = ((0, 0), (0, 0), (0, s_pad - S), (0, 0))
    kp = jnp.pad(k, pad)
    vp = jnp.pad(v, pad)
    kb = kp.reshape(B, H, n_blocks, MOBA_BLOCK, Dh)
    vb = vp.reshape(B, H, n_blocks, MOBA_BLOCK, Dh)
    k_mean = jnp.mean(kb.astype(jnp.float32), axis=3)
    scale = Dh ** -0.5
    b_idx = jnp.arange(B)[:, None, None, None]
    h_idx = jnp.arange(H)[None, :, None, None]
    block_ids = jnp.arange(n_blocks)

    def one_chunk(c):
        start = c * Q_CHUNK
        qc = lax.dynamic_slice_in_dim(q, start, Q_CHUNK, axis=2) * scale
        own = start // MOBA_BLOCK
        gate = jnp.einsum('bhqd,bhnd->bhqn', qc.astype(jnp.float32), k_mean)
        gate = jnp.where(block_ids < own, gate, -jnp.inf)
        top_s, top_i = lax.top_k(gate, k_sel)
        valid = jnp.isfinite(top_s)
        kg = kb[b_idx, h_idx, top_i]
        vg = vb[b_idx, h_idx, top_i]
        s_sel = jnp.einsum('bhqd,bhqkpd->bhqkp', qc, kg).astype(jnp.float32)
        s_sel = jnp.where(valid[..., None], s_sel, -jnp.inf)
        k_own = lax.dynamic_slice_in_dim(kp, own * MOBA_BLOCK, MOBA_BLOCK, axis=2)
        v_own = lax.dynamic_slice_in_dim(vp, own * MOBA_BLOCK, MOBA_BLOCK, axis=2)
        s_own = jnp.einsum('bhqd,bhpd->bhqp', qc, k_own).astype(jnp.float32)
        q_pos = start + jnp.arange(Q_CHUNK)
        k_pos = own * MOBA_BLOCK + jnp.arange(MOBA_BLOCK)
        s_own = jnp.where(k_pos[None, :] <= q_pos[:, None], s_own, -jnp.inf)
        logits = jnp.concatenate([s_sel.reshape(B, H, Q_CHUNK, k_sel * MOBA_BLOCK), s_own], axis=-1)
        p = jax.nn.softmax(logits, axis=-1)
        p_sel = p[..., :k_sel * MOBA_BLOCK].reshape(B, H, Q_CHUNK, k_sel, MOBA_BLOCK).astype(v.dtype)
        p_own = p[..., k_sel * MOBA_BLOCK:].astype(v.dtype)
        return (jnp.einsum('bhqkp,bhqkpd->bhqd', p_sel, vg)
                + jnp.einsum('bhqp,bhpd->bhqd', p_own, v_own))

    outs = lax.map(one_chunk, jnp.arange(S // Q_CHUNK))
    return outs.transpose(1, 2, 0, 3, 4).reshape(B, H, S, Dh)


def chunked_spatial_gating(u, v, ln_g, ln_b, w_s, b_s):
    B, S, _ = v.shape
    v = layernorm(v, ln_g, ln_b)
    n_chunks = S // G_CHUNK
    vc = v.reshape(B, n_chunks, G_CHUNK, G_GROUPS, G_GROUP_DIM)
    causal = jnp.tril(jnp.ones((G_CHUNK, G_CHUNK), dtype=bool))
    w = jnp.where(causal[None], w_s, jnp.zeros((), w_s.dtype))
    mixed = jnp.einsum('gij,bcjgd->bcigd', w, vc) + b_s.T[None, None, :, :, None]
    return u * mixed.reshape(B, S, G_WIDTH)


def setup_inputs(seed: int = 0) -> dict:
    key = jax.random.key(seed)
    ks = jax.random.split(key, 24)
    L, D = DEPTH, D_MODEL

    def w(k, shape, fan_in, mult=1.0):
        return jax.random.normal(k, shape, jnp.float32) * (mult * fan_in ** -0.5)

    def gain(k, shape):
        return 1.0 + 0.05 * jax.random.normal(k, shape, jnp.float32)

    def bias(k, shape, s=0.02):
        return s * jax.random.normal(k, shape, jnp.float32)

    return {
        "x": jax.random.normal(ks[0], (BATCH, SEQ, D), jnp.float32),
        "ffn1_norm": gain(ks[1], (L, D)),
        "ffn1_w_gate": w(ks[2], (L, D, D_FF), D),
        "ffn1_w_up": w(ks[3], (L, D, D_FF), D),
        "ffn1_w_down": w(ks[4], (L, D_FF, D), D_FF),
        "mix_norm": gain(ks[5], (L, D)),
        "w_in": w(ks[6], (L, D, IN_WIDTH), D),
        "gmlp_ln_g": gain(ks[7], (L, G_WIDTH)),
        "gmlp_ln_b": bias(ks[8], (L, G_WIDTH)),
        "gmlp_w_s": w(ks[9], (L, G_GROUPS, G_CHUNK, G_CHUNK), G_CHUNK, 0.5),
        "gmlp_b_s": 1.0 + bias(ks[10], (L, G_GROUPS, G_CHUNK), 0.05),
        "w_branch_attn": w(ks[11], (L, A_WIDTH, D), A_WIDTH),
        "w_branch_gmlp": w(ks[12], (L, G_WIDTH, D), G_WIDTH),
        "w_gate": w(ks[13], (L, D, N_BRANCHES * D), D),
        "b_gate": bias(ks[14], (L, N_BRANCHES * D)),
        "w_out": w(ks[15], (L, D, D), D),
        "ffn2_norm": gain(ks[16], (L, D)),
        "ffn2_w_gate": w(ks[17], (L, D, D_FF), D),
        "ffn2_w_up": w(ks[18], (L, D, D_FF), D),
        "ffn2_w_down": w(ks[19], (L, D_FF, D), D_FF),
        "final_norm": gain(ks[20], (D,)),
    }


def reference(x, ffn1_norm, ffn1_w_gate, ffn1_w_up, ffn1_w_down, mix_norm, w_in,
              gmlp_ln_g, gmlp_ln_b, gmlp_w_s, gmlp_b_s, w_branch_attn, w_branch_gmlp,
              w_gate, b_gate, w_out, ffn2_norm, ffn2_w_gate, ffn2_w_up, ffn2_w_down,
              final_norm):
    B, S, D = x.shape
    h = x
    for l in range(DEPTH):
        h = h + 0.5 * swiglu(rmsnorm(h, ffn1_norm[l]), ffn1_w_gate[l], ffn1_w_up[l], ffn1_w_down[l])

        n = rmsnorm(h, mix_norm[l])
        z = n @ w_in[l]
        q = z[..., :A_WIDTH]
        k = z[..., A_WIDTH:2 * A_WIDTH]
        v = z[..., 2 * A_WIDTH:3 * A_WIDTH]
        to_heads = lambda t: t.reshape(B, S, A_HEADS, A_HEAD_DIM).transpose(0, 2, 1, 3)
        attn = moba_attention(to_heads(q), to_heads(k), to_heads(v))
        attn = attn.transpose(0, 2, 1, 3).reshape(B, S, A_WIDTH)

        zg = jax.nn.gelu(z[..., 3 * A_WIDTH:])
        gm = chunked_spatial_gating(zg[..., :G_WIDTH], zg[..., G_WIDTH:],
                                    gmlp_ln_g[l], gmlp_ln_b[l], gmlp_w_s[l], gmlp_b_s[l])

        y_attn = attn @ w_branch_attn[l]
        y_gmlp = gm @ w_branch_gmlp[l]
        gates = jax.nn.sigmoid(n @ w_gate[l] + b_gate[l])
        merged = gates[..., :D] * y_attn + gates[..., D:] * y_gmlp
        h = h + merged @ w_out[l]

        h = h + 0.5 * swiglu(rmsnorm(h, ffn2_norm[l]), ffn2_w_gate[l], ffn2_w_up[l], ffn2_w_down[l])
    return rmsnorm(h, final_norm)
```

```python
import numpy as np
from collections import deque
from contextlib import ExitStack
import concourse.bass as bass
import concourse.mybir as mybir
from concourse.bass_utils import run_bass_kernel_spmd

F32 = mybir.dt.float32
BF16 = mybir.dt.bfloat16
AF = mybir.ActivationFunctionType
ALU = mybir.AluOpType
AX = mybir.AxisListType

D = 1024
FF = 2816
NFC = FF // 128
NH = 8
DH = 64
INW = 2560
SEQ = 4096
BATCH = 4
BLK = 256
TOK = 2048
EPS = 1e-6
OWN = [[0, 3, 4, 7, 8, 11, 12, 15], [1, 2, 5, 6, 9, 10, 13, 14]]
NEG = -30000.0
RING = 7
NPT = 4
FPH = [[0, 1, 2], [3, 4, 5], [6, 7, 8], [9, 10]]


class Sched:
    ENG = ("pe", "act", "dve", "pool", "sp")

    def __init__(self, nc, stack):
        self.nc = nc
        self.stack = stack
        self.dry = False
        self.prog = {e: [] for e in self.ENG}
        self.sem = {}
        self.cnt = {}
        for e in ("pe", "act", "dve", "pool"):
            self.sem[e] = stack.enter_context(nc.semaphore("sem_" + e))
            self.cnt[e] = 0
        self.waited = {}
        self.last_w = {}
        self.readers = {}
        self.dma_sems = {}

    def dma_sem(self, name):
        if name not in self.dma_sems:
            s = self.stack.enter_context(self.nc.semaphore("dsem_" + name))
            self.dma_sems[name] = [s, 0]
            self.sem[name] = s
        return self.dma_sems[name]

    def _wait(self, eng, semkey, value):
        if value <= 0:
            return
        if eng == "pe" and semkey == "pe":
            return
        k = (eng, semkey)
        if self.waited.get(k, 0) >= value:
            return
        self.waited[k] = value
        s = self.sem[semkey]
        self.prog[eng].append(lambda E: E.wait_ge(s, value))

    def _deps(self, eng, reads, writes):
        for r in reads:
            lw = self.last_w.get(r)
            if lw is not None:
                self._wait(eng, lw[0], lw[1])
        for w in writes:
            lw = self.last_w.get(w)
            if lw is not None:
                self._wait(eng, lw[0], lw[1])
            for rd in self.readers.get(w, ()):
                self._wait(eng, rd[0], rd[1])

    def _record(self, token, reads, writes):
        for w in writes:
            self.last_w[w] = token
            self.readers[w] = []
        for r in reads:
            if r in writes:
                continue
            lst = self.readers.setdefault(r, [])
            if not lst or lst[-1] != token:
                lst.append(token)
                if len(lst) > 24:
                    best = {}
                    for t in lst:
                        if t[0] not in best or best[t[0]][1] < t[1]:
                            best[t[0]] = t
                    lst[:] = list(best.values())

    PSUM_KEYS = ("pA", "pB", "pY", "pT")

    def op(self, eng, fn, reads=(), writes=(), mark=True):
        if self.dry:
            return
        pr = [r for r in reads if isinstance(r, tuple) and r[0] in self.PSUM_KEYS and r not in writes]
        if pr:
            reads = [r for r in reads if r not in pr]
            writes = list(writes) + pr
        self._deps(eng, reads, writes)
        if mark:
            self.cnt[eng] += 1
            s = self.sem[eng]
            self.prog[eng].append(lambda E: fn(E).then_inc(s, 1))
            tok = (eng, self.cnt[eng])
        else:
            self.prog[eng].append(fn)
            tok = (eng, self.cnt[eng] + 1)
        self._record(tok, reads, writes)

    def dma(self, queue, fn, semname, reads=(), writes=()):
        if self.dry:
            return
        self._deps(queue, reads, writes)
        ds = self.dma_sem(semname)
        ds[1] += 16
        s = ds[0]
        self.prog[queue].append(lambda E: fn(E).then_inc(s, 16))
        self._record((semname, ds[1]), reads, writes)

    def coll(self, fn, semname, reads=(), writes=()):
        if self.dry:
            return
        self._deps("pool", reads, writes)
        ds = self.dma_sem(semname)
        ds[1] += 1
        s = ds[0]
        self.prog["pool"].append(lambda E: fn(E).then_inc(s, 1))
        self._record((semname, ds[1]), reads, writes)

    def barrier(self):
        if self.dry:
            return
        toks = {}
        for d in (self.last_w,):
            for t in d.values():
                if t[0] not in toks or toks[t[0]] < t[1]:
                    toks[t[0]] = t[1]
        for lst in self.readers.values():
            for t in lst:
                if t[0] not in toks or toks[t[0]] < t[1]:
                    toks[t[0]] = t[1]
        for e in ("pe", "act", "dve", "pool", "sp"):
            for k, v in toks.items():
                if k == "pe" and v > self.cnt["pe"]:
                    continue
                self._wait(e, k, v)

    def final_wait(self, eng, regions):
        for r in regions:
            lw = self.last_w.get(r)
            if lw is not None:
                self._wait(eng, lw[0], lw[1])

    def mm(self, out, lhsT, rhs, start, stop, reads, writes, mark=None):
        if mark is None:
            mark = stop
        self.op("pe", lambda E: E.matmul(out, lhsT=lhsT, rhs=rhs, start=start, stop=stop),
                reads, writes, mark)

    def tr(self, out, in_, ident, reads, writes, mark=True):
        self.op("pe", lambda E: E.transpose(out=out, in_=in_, identity=ident), reads, writes, mark)

    def act(self, out, in_, func, reads, writes, bias=None, scale=None, accum_out=None):
        kw = {}
        if bias is not None:
            kw["bias"] = bias
        if scale is not None:
            kw["scale"] = scale
        if accum_out is not None:
            kw["accum_out"] = accum_out
        self.op("act", lambda E: E.activation(out=out, in_=in_, func=func, **kw), reads, writes)

    def tt(self, eng, out, in0, in1, op, reads, writes):
        self.op(eng, lambda E: E.tensor_tensor(out=out, in0=in0, in1=in1, op=op), reads, writes)

    def ts(self, eng, out, in0, s1, s2, op0, op1, reads, writes):
        if op1 is None:
            self.op(eng, lambda E: E.tensor_scalar(out=out, in0=in0, scalar1=s1, scalar2=None, op0=op0),
                    reads, writes)
        else:
            self.op(eng, lambda E: E.tensor_scalar(out=out, in0=in0, scalar1=s1, scalar2=s2, op0=op0, op1=op1),
                    reads, writes)

    def stt(self, out, in0, scalar, in1, op0, op1, reads, writes):
        self.op("dve", lambda E: E.scalar_tensor_tensor(out=out, in0=in0, scalar=scalar, in1=in1, op0=op0, op1=op1),
                reads, writes)

    def copy(self, eng, out, in_, reads, writes):
        if eng == "act":
            self.op("act", lambda E: E.copy(out=out, in_=in_), reads, writes)
        else:
            self.op(eng, lambda E: E.tensor_copy(out=out, in_=in_), reads, writes)

    def emit(self):
        nc = self.nc
        prog = self.prog
        with nc.Block() as block:
            @block.tensor
            def _(E):
                for f in prog["pe"]:
                    f(E)

            @block.scalar
            def _(E):
                for f in prog["act"]:
                    f(E)

            @block.vector
            def _(E):
                for f in prog["dve"]:
                    f(E)

            @block.gpsimd
            def _(E):
                for f in prog["pool"]:
                    f(E)

            @block.sync
            def _(E):
                for f in prog["sp"]:
                    f(E)


class WStream:
    def __init__(self, S, ring, R):
        self.S = S
        self.ring = ring
        self.R = R
        self.plan = []
        self.reset()

    def reset(self):
        self.idx = 0
        self.next_load = 0
        self.live = deque()

    def view(self, slot, shape):
        P, a, b = shape
        return self.ring[0:P, slot, 0:a * b].rearrange("p (a b) -> p a b", a=a)

    def acquire(self, src, shape):
        if self.S.dry:
            self.plan.append((src, shape))
            return self.view(0, shape), ("wr", 0), -1
        i = self.idx
        self.idx += 1
        assert self.plan[i][1] == shape
        self.live.append(i)
        self.pump()
        assert i < self.next_load
        return self.view(i % self.R, shape), ("wr", i % self.R), i

    def release(self, i):
        if self.S.dry:
            return
        self.live.remove(i)
        self.pump()

    def pump(self):
        oldest = self.live[0] if self.live else self.idx
        while self.next_load < len(self.plan) and self.next_load < oldest + self.R:
            j = self.next_load
            slot = j % self.R
            src, shape = self.plan[j]
            dst = self.view(slot, shape)
            self.S.dma("pool", lambda E, dst=dst, src=src: E.dma_start(out=dst, in_=src),
                       "wr%d" % slot, writes=[("wr", slot)])
            self.next_load += 1


def build_program(NPAIR=4):
    nc = bass.Bass("TRN2", target_bir_lowering=False)

    def din(name, shape):
        return nc.dram_tensor(name, list(shape), F32, kind="ExternalInput").ap()

    x_own = din("x_own", [TOK, D])
    w1g = din("ffn1_w_gate", [D, FF]); w1u = din("ffn1_w_up", [D, FF]); w1d = din("ffn1_w_down", [FF, D])
    w2g = din("ffn2_w_gate", [D, FF]); w2u = din("ffn2_w_up", [D, FF]); w2d = din("ffn2_w_down", [FF, D])
    w_in = din("w_in", [D, INW])
    w_ba = din("w_branch_attn", [512, D]); w_bg = din("w_branch_gmlp", [512, D])
    w_gt = din("w_gate", [D, 2 * D]); w_out = din("w_out", [D, D])
    norms = din("norms", [128, 4, D])
    lngb = din("lngb", [128, 2, 512])
    wsT = din("wsT", [128, 8, 128])
    trilT = din("trilT", [128, 128])
    bsT = din("bsT", [128, 8])
    bgate = din("bgate", [128, 16])
    ident_d = din("ident", [128, 128])
    shodd_d = din("shodd", [64, 128])
    ej_d = din("ej", [16, 16 * 128])
    cmask_d = din("cmask", [128, 2 * 2 * 256])
    past_d = din("past", [128, 8 * 16])
    nb_d = din("nb", [128, 8 * 16])
    out_d = nc.dram_tensor("out", [TOK, D], F32, kind="ExternalOutput").ap()

    def wview(w):
        return w.rearrange("(dc p) f -> p dc f", p=128)

    w1g_v, w1u_v, w2g_v, w2u_v = wview(w1g), wview(w1u), wview(w2g), wview(w2u)
    w1d_v = w1d.rearrange("(fc p) d -> p fc d", p=128)
    w2d_v = w2d.rearrange("(fc p) d -> p fc d", p=128)
    win_v = wview(w_in)
    wgt_v = wview(w_gt)
    wout_v = wview(w_out)
    wba_v = w_ba.rearrange("(kc p) d -> p kc d", p=128)
    wbg_v = w_bg.rearrange("(kc p) d -> p kc d", p=128)
    xown_v = x_own.rearrange("(g t p) d -> g p t d", p=128, t=4)
    NGRP = [[2 * i, 2 * i + 1] for i in range(NPAIR)]
    xsrc = [nc.dram_tensor("xsrc%d" % g, [128, 4096], BF16) for g in range(4)]
    xdst = [nc.dram_tensor("xdst%d" % g, [256, 4096], BF16) for g in range(4)]
    msrc = [nc.dram_tensor("msrc%d" % g, [128, 8], F32) for g in range(4)]
    mdst = [nc.dram_tensor("mdst%d" % g, [256, 8], F32) for g in range(4)]
    out_v = out_d.rearrange("(g t p) d -> g p t d", p=128, t=4)

    with ExitStack() as st:
        S = Sched(nc, st)

        def sb(name, shape, dt):
            return st.enter_context(nc.sbuf_tensor(name, list(shape), dt))

        def ps(name, shape, dt):
            return st.enter_context(nc.psum_tensor(name, list(shape), dt))

        ring = sb("wring", [128, RING, 2048], BF16)
        W = WStream(S, ring, RING)
        KT = sb("KT", [128, 4, SEQ], BF16)
        VA = sb("VA", [128, 32, NH, DH + 1], BF16)
        kmT = sb("kmT", [128, 4, 16], F32)
        normg = sb("normg", [128, 4, D], F32)
        lng = sb("lng", [128, 2, 512], F32)
        wsm = sb("wsm", [128, 8, 128], BF16)
        trl = sb("trl", [128, 128], BF16)
        bst = sb("bst", [128, 8], F32)
        bgt = sb("bgt", [128, 16], F32)
        idb = sb("idb", [128, 128], BF16)
        ejb = sb("ejb", [128, 16, 128], BF16)
        cmb = sb("cmb", [128, 2, 2, 256], BF16)
        kms = sb("kms", [128, 4, 2], F32)
        pastt = sb("pastt", [128, 8, 16], F32)
        nbt = sb("nbt", [128, 8, 16], F32)
        xh = sb("xh", [128, 4, D], F32)
        nT = sb("nT", [128, 8, 512], BF16)
        HT = sb("HT", [128, 6, 512], BF16)
        nbuf = sb("nbuf", [128, 2, D], BF16)
        sg = sb("sg", [128, 2, 512], F32)
        st6 = sb("st6", [128, 12], F32)
        mv = sb("mv", [128, 8, 2], F32)
        kmv = sb("kmv", [128, 2], F32)
        stat = sb("stat", [128, 4, 8], F32)
        QTz = sb("QTz", [128, 8, 512], BF16)
        kmz = sb("kmz", [128, 8, 16], BF16)
        qa = sb("qa", [128, 8, 512], BF16)
        QTlo = qa[:, 0:4, :]
        kmh = sb("kmh", [128, 2, 4, 16], BF16)
        kmd = sb("kmd", [128, 4, 16], F32)
        attnT = qa[:, 0:4, :]
        gmT = qa[:, 4:8, :]
        ug = sb("ug", [128, 4, 512], BF16)
        vgm = sb("vgm", [128, 8, 512], BF16)
        vg = vgm[:].rearrange("p a b -> p (a b)").bitcast(F32).rearrange("p (a b) -> p a b", a=4)
        mrgT = vgm
        gtmp = sb("gtmp", [128, 2, 512], F32)
        gmb = sb("gmb", [128, 8, 16], F32)
        mx8 = sb("mx8", [128, 8, 8], F32)
        thr = sb("thr", [128, 8], F32)
        biasq = sb("biasq", [128, 4, 8, 16], BF16)
        biasT = sb("biasT", [128, 8, 512], BF16)
        PT = sb("PT", [128, NPT, 512], BF16)
        shodd = sb("shodd_sb", [128, 128], BF16)
        ones_b = sb("ones_b", [128, 64], BF16)
        gm = PT
        lnst = sb("lnst", [128, 8], F32)
        lnmv4 = sb("lnmv4", [128, 4, 4], F32)
        pA = ps("pA", [128, 2, 512], F32)
        pB = ps("pB", [128, 2, 512], F32)
        pY = ps("pY", [128, 2, 512], F32)
        pT = ps("pT", [128, 2, 1024], BF16)
        cnt = {"pA": 0, "pB": 0, "pY": 0, "pT": 0}

        def nxt(name):
            cnt[name] += 1
            return cnt[name] % 2

        def setup():
            S.dma("sp", lambda E: E.dma_start(out=normg[:], in_=norms), "c0", writes=["normg"])
            S.dma("sp", lambda E: E.dma_start(out=lng[:], in_=lngb), "c1", writes=["lng"])
            S.dma("pool", lambda E: E.dma_start(out=wsm[:], in_=wsT), "c2", writes=["wsm"])
            S.dma("pool", lambda E: E.dma_start(out=trl[:], in_=trilT), "c3", writes=["trl"])
            S.dma("sp", lambda E: E.dma_start(out=bst[:], in_=bsT), "c4", writes=["bst"])
            S.dma("sp", lambda E: E.dma_start(out=bgt[:], in_=bgate), "c5", writes=["bgt"])
            S.dma("pool", lambda E: E.dma_start(out=idb[:], in_=ident_d), "c6", writes=["idb"])
            S.dma("pool", lambda E: E.dma_start(out=shodd[0:64, :], in_=shodd_d), "c11", writes=["shodd"])
            S.op("dve", lambda E: E.memset(ones_b[:], 1.0), [], ["ones_b"])
            S.op("dve", lambda E: E.memset(ejb[:], 0.0), [], ["ejb"])
            S.op("dve", lambda E: E.memset(biasT[:], 0.0), [], [("biasT", hp) for hp in range(4)])
            S.op("dve", lambda E: E.memset(QTz[:], 0.0), [], [("QTz", pc) for pc in range(4)])
            S.op("dve", lambda E: E.memset(kmz[:], 0.0), [], ["kmz"])
            S.dma("pool", lambda E: E.dma_start(out=ejb[0:16, :, :].rearrange("p a b -> p (a b)"), in_=ej_d), "c7", writes=["ejb"])
            S.dma("pool", lambda E: E.dma_start(out=cmb[:].rearrange("p m a b -> p (m a b)"), in_=cmask_d), "c8", writes=["cmb"])
            S.dma("sp", lambda E: E.dma_start(out=pastt[:].rearrange("p a b -> p (a b)"), in_=past_d), "c9", writes=["pastt"])
            S.dma("sp", lambda E: E.dma_start(out=nbt[:].rearrange("p a b -> p (a b)"), in_=nb_d), "c10", writes=["nbt"])
            S.tt("dve", wsm[:], wsm[:], trl[:, None, :].to_broadcast([128, 8, 128]), ALU.mult, ["trl"], ["wsm"])
            S.op("dve", lambda E: E.memset(VA[:, :, :, DH:DH + 1], 1.0), [], ["VAones"])
            S.op("dve", lambda E: E.memset(kmT[:], 0.0), [], [("kmT", pc, i) for pc in range(4) for i in range(8)])

        def src_xh(t):
            return xh[:, t, :], ("xh", t)

        xp0 = vgm[:].rearrange("p a b -> p (a b)").bitcast(F32).rearrange("p (t d) -> p t d", t=2)
        xp1 = qa[:].rearrange("p a b -> p (a b)").bitcast(F32).rearrange("p (t d) -> p t d", t=2)

        def src_xpre(t):
            return (xp0[:, t, :], "VGM") if t < 2 else (xp1[:, t - 2, :], "QA")

        def rms_stats_tile(t, src=src_xh):
            xa, xk = src(t)
            S.op("dve", lambda E, xa=xa: E.bn_stats(out=st6[:, 0:6], in_=xa[:, 0:512]), [xk], ["st6a"])
            S.op("dve", lambda E, xa=xa: E.bn_stats(out=st6[:, 6:12], in_=xa[:, 512:1024]), [xk], ["st6b"])
            S.op("dve", lambda E, t=t: E.bn_aggr(out=mv[:, t, :], in_=st6[:, 0:12]), ["st6a", "st6b"], [("mv", t)])

        def rms_rstd(NT, src=src_xh, tiles_done=False):
            if not tiles_done:
                for t in range(NT):
                    rms_stats_tile(t, src)
            rd = [("mv", t) for t in range(NT)]
            S.tt("dve", stat[:, 0, 0:NT], mv[:, 0:NT, 0], mv[:, 0:NT, 0], ALU.mult, rd, [("stat", 0)])
            S.tt("dve", stat[:, 1, 0:NT], stat[:, 0, 0:NT], mv[:, 0:NT, 1], ALU.add, rd + [("stat", 0)], [("stat", 1)])
            S.ts("dve", stat[:, 1, 0:NT], stat[:, 1, 0:NT], EPS, None, ALU.add, None, [("stat", 1)], [("stat", 1)])
            S.act(stat[:, 2, 0:NT], stat[:, 1, 0:NT], AF.Sqrt, [("stat", 1)], [("stat", 2)])
            S.op("dve", lambda E: E.reciprocal(out=stat[:, 3, 0:NT], in_=stat[:, 2, 0:NT]), [("stat", 2)], [("stat", 3)])

        def rms_to_nT(NT, which, src=src_xh, do_stats=True):
            if do_stats:
                rms_rstd(NT, src)
            for t in range(NT):
                b = t % 2
                xa, xk = src(t)
                S.stt(nbuf[:, b, :], xa, stat[:, 3, t:t + 1], normg[:, which, :], ALU.mult, ALU.mult,
                      [xk, ("stat", 3), "normg"], [("nbuf", b)])
                pb = nxt("pT")
                for dc in range(8):
                    S.tr(pT[:, pb, dc * 128:(dc + 1) * 128], nbuf[:, b, dc * 128:(dc + 1) * 128], idb[:],
                         [("nbuf", b), "idb"], [("pT", pb)], mark=(dc == 7))
                S.copy("act", nT[:, :, t * 128:(t + 1) * 128], pT[:, pb, :].rearrange("p (dc k) -> p dc k", k=128),
                       [("pT", pb)], [("nT", t)])

        def ffn(NT, which, wg_v, wu_v, wd_v, src=src_xh, do_norm=True, hook=None, do_stats=True, tail_stats=False):
            if do_norm:
                rms_to_nT(NT, which, src, do_stats=do_stats)
            nhalf = NT // 4
            for phi, ph in enumerate(FPH):
                if hook is not None and phi == len(FPH) - 1:
                    hook()
                for pi, c in enumerate(ph):
                    wg, kg, ig = W.acquire(wg_v[:, :, c * 256:(c + 1) * 256], (128, 8, 256))
                    wu, ku, iu = W.acquire(wu_v[:, :, c * 256:(c + 1) * 256], (128, 8, 256))
                    for half in range(nhalf):
                        nrd = [("nT", half * 4 + i) for i in range(4)]
                        for j in range(2):
                            fl = pi * 2 + j
                            ba = nxt("pA")
                            for dc in range(8):
                                S.mm(pA[:, ba, :], wg[:, dc, j * 128:(j + 1) * 128], nT[:, dc, half * 512:(half + 1) * 512],
                                     dc == 0, dc == 7, [kg] + nrd, [("pA", ba)])
                            bb = nxt("pB")
                            for dc in range(8):
                                S.mm(pB[:, bb, :], wu[:, dc, j * 128:(j + 1) * 128], nT[:, dc, half * 512:(half + 1) * 512],
                                     dc == 0, dc == 7, [ku] + nrd, [("pB", bb)])
                            S.act(sg[:, ba, :], pA[:, ba, :], AF.Silu, [("pA", ba)], [("sg", ba)])
                            S.tt("dve", HT[:, fl, half * 512:(half + 1) * 512], sg[:, ba, :], pB[:, bb, :], ALU.mult,
                                 [("sg", ba), ("pB", bb)], [("HT", fl, half)])
                    W.release(ig)
                    W.release(iu)
                wds = [W.acquire(wd_v[:, 2 * c:2 * c + 2, :], (128, 2, 1024)) for c in ph]
                nf = 2 * len(ph)
                for t in range(NT):
                    for hf in range(2):
                        by = nxt("pY")
                        for fl in range(nf):
                            wd, kd, _ = wds[fl // 2]
                            S.mm(pY[:, by, :], HT[:, fl, t * 128:(t + 1) * 128], wd[:, fl % 2, hf * 512:(hf + 1) * 512],
                                 fl == 0, fl == nf - 1, [kd, ("HT", fl, t // 4)], [("pY", by)])
                        if phi == 0:
                            xa, xk = src(t)
                        else:
                            xa, xk = src_xh(t)
                        S.stt(xh[:, t, hf * 512:(hf + 1) * 512], pY[:, by, :], 0.5, xa[:, hf * 512:(hf + 1) * 512],
                              ALU.mult, ALU.add, [("pY", by), xk], [("xh", t)])
                    if tail_stats and phi == len(FPH) - 1:
                        rms_stats_tile(t)
                for _, _, i in wds:
                    W.release(i)
            if tail_stats:
                rms_rstd(NT, tiles_done=True)

        def kv_proj(g):
            kst = gtmp[:].rearrange("p a b -> p (a b)").bitcast(BF16).rearrange("p (a b) -> p a b", a=4)
            vst = sg[:].rearrange("p a b -> p (a b)").bitcast(BF16).rearrange("p (t h d) -> p t h d", t=4, h=8)
            nrd = [("nT", i) for i in range(4)]
            for c in (2, 3):
                w, kw, iw = W.acquire(win_v[:, :, c * 256:(c + 1) * 256], (128, 8, 256))
                for j in range(2):
                    pc = (c - 2) * 2 + j
                    ba = nxt("pA")
                    for dc in range(8):
                        S.mm(pA[:, ba, :], w[:, dc, j * 128:(j + 1) * 128], nT[:, dc, 0:512],
                             dc == 0, dc == 7, [kw] + nrd, [("pA", ba)])
                    S.copy("act", kst[:, pc, :], pA[:, ba, :], [("pA", ba)], [("gtmp", pc // 2)])
                    for bl in range(2):
                        S.op("dve", lambda E, ba=ba, bl=bl: E.bn_stats(out=st6[:, 0:6], in_=pA[:, ba, bl * 256:(bl + 1) * 256]),
                             [("pA", ba)], ["st6a"])
                        S.op("dve", lambda E: E.bn_aggr(out=kmv[:, 0:2], in_=st6[:, 0:6]), ["st6a"], ["kmv"])
                        S.copy("dve", kms[:, pc, bl:bl + 1], kmv[:, 0:1], ["kmv"], ["kms"])
                W.release(iw)
            S.dma("sp", lambda E: E.dma_start(out=msrc[g][:, :], in_=kms[:].rearrange("p a b -> p (a b)")), "stm%d" % g,
                  reads=["kms"], writes=[("msrc", g)])
            S.coll(lambda E: E.collective_compute("AllGather", ALU.bypass, replica_groups=NGRP,
                                                  ins=[msrc[g].ap().opt()], outs=[mdst[g].ap().opt()]),
                   "ccm%d" % g, reads=[("msrc", g)], writes=[("mdst", g)])
            for c in (4, 5):
                w, kw, iw = W.acquire(win_v[:, :, c * 256:(c + 1) * 256], (128, 8, 256))
                for t in range(4):
                    by = nxt("pY")
                    for dc in range(8):
                        S.mm(pY[:, by, 0:256], nT[:, dc, t * 128:(t + 1) * 128], w[:, dc, :],
                             dc == 0, dc == 7, [kw, ("nT", t)], [("pY", by)])
                    h0 = (c - 4) * 4
                    S.copy("act", vst[:, t, h0:h0 + 4, :], pY[:, by, 0:256].rearrange("p (h d) -> p h d", d=DH),
                           [("pY", by)], [("sg", t // 2)])
                W.release(iw)
            S.dma("sp", lambda E: E.dma_start(out=xsrc[g][:, 0:2048], in_=kst.rearrange("p a b -> p (a b)")), "stk%d" % g,
                  reads=[("gtmp", 0), ("gtmp", 1)], writes=[("xsrc", g, 0)])
            S.dma("sp", lambda E: E.dma_start(out=xsrc[g][:, 2048:4096], in_=vst.rearrange("p t h d -> p (t h d)")), "stv%d" % g,
                  reads=[("sg", 0), ("sg", 1)], writes=[("xsrc", g, 1)])
            for m in range(2):
                tb = m * 8 + 2 * g
                S.dma("sp", lambda E, m=m, tb=tb: E.dma_start(
                    out=kmT[:, :, tb:tb + 2],
                    in_=mdst[g][m * 128:(m + 1) * 128, :].rearrange("p (a b) -> p a b", a=4)),
                    "rbm%d%d" % (g, m), reads=[("mdst", g)], writes=[("kmT", pc, m * 4 + g) for pc in range(4)])
            S.coll(lambda E: E.collective_compute("AllGather", ALU.bypass, replica_groups=NGRP,
                                                  ins=[xsrc[g].ap().opt()], outs=[xdst[g].ap().opt()]),
                   "cck%d" % g, reads=[("xsrc", g, 0), ("xsrc", g, 1)], writes=[("xdst", g)])
            for m in range(2):
                tb = m * 8 + 2 * g
                S.dma("sp", lambda E, m=m, tb=tb: E.dma_start(
                    out=KT[:, :, tb * 256:tb * 256 + 512],
                    in_=xdst[g][m * 128:(m + 1) * 128, 0:2048].rearrange("p (a b) -> p a b", a=4)),
                    "rbk%d%d" % (g, m), reads=[("xdst", g)], writes=[("KT", pc, m * 4 + g) for pc in range(4)])
                S.dma("sp", lambda E, m=m, tb=tb: E.dma_start(
                    out=VA[:, tb * 2:tb * 2 + 4, :, 0:DH],
                    in_=xdst[g][m * 128:(m + 1) * 128, 2048:4096].rearrange("p (t h d) -> p t h d", t=4, h=8)),
                    "rbv%d%d" % (g, m), reads=[("xdst", g)],
                    writes=[("VA", tb * 2 + i, hh) for i in range(4) for hh in range(2)])

        def gelu_from_psum(src, rkey, dst, wkey):
            S.act(dst, src, AF.Gelu_apprx_tanh, [rkey], [wkey])

        def qug_proj(g):
            kv_proj(g)
            for c in (0, 1):
                w, kw, iw = W.acquire(win_v[:, :, c * 256:(c + 1) * 256], (128, 8, 256))
                nrd = [("nT", i) for i in range(4)]
                for j in range(2):
                    pc = c * 2 + j
                    ba = nxt("pA")
                    for dc in range(8):
                        S.mm(pA[:, ba, :], w[:, dc, j * 128:(j + 1) * 128], nT[:, dc, 0:512],
                             dc == 0, dc == 7, [kw] + nrd, [("pA", ba)])
                    S.op("act", lambda E, pc=pc, ba=ba: E.mul(out=QTz[0:64, 2 * pc, :], in_=pA[0:64, ba, :], mul=0.125),
                         [("pA", ba)], [("QTz", pc)])
                    S.op("act", lambda E, pc=pc, ba=ba: E.mul(out=QTz[64:128, 2 * pc + 1, :], in_=pA[64:128, ba, :], mul=0.125),
                         [("pA", ba)], [("QTz", pc)])
                    S.stt(QTlo[0:64, pc, :], pA[0:64, ba, :], 0.125, QTz[0:64, 2 * pc, :], ALU.mult, ALU.subtract,
                          [("pA", ba), ("QTz", pc)], ["QA"])
                    S.stt(QTlo[64:128, pc, :], pA[64:128, ba, :], 0.125, QTz[64:128, 2 * pc + 1, :], ALU.mult, ALU.subtract,
                          [("pA", ba), ("QTz", pc)], ["QA"])
                W.release(iw)
            for c in (6, 7, 8, 9):
                w, kw, iw = W.acquire(win_v[:, :, c * 256:(c + 1) * 256], (128, 8, 256))
                for t in range(4):
                    by = nxt("pY")
                    for dc in range(8):
                        S.mm(pY[:, by, 0:256], nT[:, dc, t * 128:(t + 1) * 128], w[:, dc, :],
                             dc == 0, dc == 7, [kw, ("nT", t)], [("pY", by)])
                    if c < 8:
                        dst = ug[:, t, (c - 6) * 256:(c - 5) * 256]
                        wk = ("ug", t, c - 6)
                    else:
                        dst = vg[:, t, (c - 8) * 256:(c - 7) * 256]
                        wk = "VGM"
                    gelu_from_psum(pY[:, by, 0:256], ("pY", by), dst, wk)
                W.release(iw)

        def gate_part1(g):
            kmk = [("kmT", pc, i) for pc in range(4) for i in range(8)]
            S.copy("dve", kmh[:, 0, :, :], kmT[:], kmk, ["kmh0"])
            S.tt("dve", kmd[:], kmT[:], kmh[:, 0, :, :], ALU.subtract, kmk + ["kmh0"], ["kmd"])
            S.copy("dve", kmh[:, 1, :, :], kmd[:], ["kmd"], ["kmh1"])
            for pc in range(4):
                S.copy("dve", kmz[0:64, 2 * pc, :], kmh[0:64, 0, pc, :], ["kmh0"], ["kmz"])
                S.copy("dve", kmz[64:128, 2 * pc + 1, :], kmh[64:128, 0, pc, :], ["kmh0"], ["kmz"])
            for sub in range(4):
                s = 2 * g + sub // 2
                ba = nxt("pA")
                for h in range(NH):
                    pc = h // 2
                    qs = slice(sub * 128, (sub + 1) * 128)
                    terms = [(QTz[:, h, qs], kmh[:, 0, pc, :]), (QTz[:, h, qs], kmh[:, 1, pc, :]), (QTlo[:, pc, qs], kmz[:, h, :])]
                    for ti, (qq, kk) in enumerate(terms):
                        S.mm(pA[:, ba, h * 16:(h + 1) * 16], qq, kk,
                             ti == 0, ti == 2, [("QTz", pc), "QA", "kmh0", "kmh1", "kmz"], [("pA", ba)],
                             mark=(h == NH - 1 and ti == 2))
                S.tt("dve", gmb[:], pA[:, ba, 0:128].rearrange("p (h t) -> p h t", t=16),
                     pastt[:, s:s + 1, :].to_broadcast([128, 8, 16]), ALU.mult, [("pA", ba), "pastt"], ["gmb"])
                S.tt("dve", gmb[:], gmb[:], nbt[:, s:s + 1, :].to_broadcast([128, 8, 16]), ALU.add, ["gmb", "nbt"], ["gmb"])
                for h in range(NH):
                    S.op("dve", lambda E, h=h: E.max(out=mx8[:, h, :], in_=gmb[:, h, :]), ["gmb"], [("mx8", h)])
                S.ts("dve", thr[:], mx8[:, :, 3], -1.0e8, None, ALU.max, None, [("mx8", h) for h in range(NH)], ["thr"])
                S.tt("dve", gmb[:], gmb[:], thr[:, :, None].to_broadcast([128, 8, 16]), ALU.is_ge, ["gmb", "thr"], ["gmb"])
                S.ts("dve", biasq[:, sub, :, :], gmb[:], -NEG, NEG, ALU.mult, ALU.add, ["gmb"], [("biasq", sub)])
        def attention(g):
            for hp in range(4):
                pb = nxt("pT")
                for hh in range(2):
                    h = hp * 2 + hh
                    for sub in range(4):
                        S.tr(pT[0:16, pb, hh * 512 + sub * 128: hh * 512 + (sub + 1) * 128], biasq[:, sub, h, :], idb[:],
                             [("biasq", sub), "idb"], [("pT", pb)], mark=(hh == 1 and sub == 3))
                S.copy("dve", biasT[0:16, hp * 2:hp * 2 + 2, :], pT[0:16, pb, :].rearrange("p (h q) -> p h q", q=512),
                       [("pT", pb)], [("biasT", hp)])
            nblk = 2 * g + 2
            visits = [(m * 8 + t, kt) for t in range(2 * g) for m in range(2) for kt in range(2)] + \
                     [(m * 8 + t, kt) for t in (2 * g, 2 * g + 1) for m in range(2) for kt in range(2)]
            nv = len(visits)
            sbufs = [(pA, 0), (pA, 1), (pB, 0), (pB, 1)]
            skeys = [("pA", 0), ("pA", 1), ("pB", 0), ("pB", 1)]
            bc = pT[:, 1, :].bitcast(F32)
            pk = pT[:, 0, :].bitcast(F32)
            recf = gtmp[:, 0, :]
            rhl = gtmp[:, 1, :].bitcast(BF16)
            ah = nbuf[:, :, 0:512]
            st_ = {"si": 0, "pti": 0}

            def epi_a(h):
                hb = h % 2
                S.copy("act", sg[0:65, hb, :], pY[0:65, hb, :], [("pY", hb)], [("sg", hb)])
                S.op("dve", lambda E: E.reciprocal(out=recf[64:65, :], in_=sg[64:65, hb, :]), [("sg", hb)], [("gtmp", 0)])
                S.copy("dve", rhl[64:65, 0:512], recf[64:65, :], [("gtmp", 0)], [("gtmp", 1)])
                S.tt("dve", recf[64:65, :], recf[64:65, :], rhl[64:65, 0:512], ALU.subtract, [("gtmp", 1)], [("gtmp", 0)])
                S.copy("dve", rhl[64:65, 512:1024], recf[64:65, :], [("gtmp", 0)], [("gtmp", 1)])

            def epi_b(h):
                hb = h % 2
                S.mm(bc[0:64, :], ones_b[64:65, :], rhl[64:65, 0:512], True, False, ["ones_b", ("gtmp", 1)], [("pT", 1)], mark=False)
                S.mm(bc[0:64, :], ones_b[64:65, :], rhl[64:65, 512:1024], False, True, ["ones_b", ("gtmp", 1)], [("pT", 1)], mark=True)
                S.tt("dve", ah[0:64, hb, :], sg[0:64, hb, :], bc[0:64, :], ALU.mult, [("sg", hb), ("pT", 1)], [("nbuf", hb)])

            def epi_c(h):
                hb = h % 2
                if hb == 0:
                    S.mm(pk, idb[0:64, :], ah[0:64, 0, :], True, False, ["idb", ("nbuf", 0)], [("pT", 0)], mark=True)
                else:
                    S.mm(pk, shodd[0:64, :], ah[0:64, 1, :], False, True, ["shodd", ("nbuf", 1)], [("pT", 0)], mark=True)
                    S.copy("act", attnT[:, h // 2, :], pk, [("pT", 0)], ["QA"])

            items = [(h, vi, t, kt) for h in range(NH) for vi, (t, kt) in enumerate(visits)]
            n_it = len(items)
            LA = 3
            pending = []

            def cols(t):
                return slice(256, 512) if t in (2 * g + 1, 8 + 2 * g + 1) else slice(0, 512)

            def score(i):
                h, vi, t, kt = items[i]
                pc = h // 2
                k0 = t * 256 + kt * 128
                ten, bi = sbufs[i % 4]
                sk = skeys[i % 4]
                cs = cols(t)
                diag = (t % 8) in (2 * g, 2 * g + 1)
                S.mm(ten[:, bi, cs], KT[:, pc, k0:k0 + 128], QTz[:, h, cs], True, False,
                     [("KT", pc, k0 // 512), ("QTz", pc)], [sk], mark=False)
                S.mm(ten[:, bi, cs], ejb[:, t, :], biasT[:, h, cs], False, not diag,
                     ["ejb", ("biasT", h // 2)], [sk], mark=(not diag))
                if diag:
                    qo = (t % 8 - 2 * g) * 256
                    S.mm(ten[:, bi, qo:qo + 256], idb[:], cmb[:, t // 8, kt, :], False, True, ["idb", "cmb"], [sk], mark=True)
                S.act(PT[:, i % NPT, cs], ten[:, bi, cs], AF.Exp, [sk], [("PT", i % NPT)])

            def pv(i):
                h, vi, t, kt = items[i]
                hb = h % 2
                cs = cols(t)
                S.mm(pY[0:65, hb, cs], VA[:, t * 2 + kt, h, :], PT[:, i % NPT, cs], vi == 0, vi == nv - 1,
                     [("PT", i % NPT), ("VA", t * 2 + kt, h // 4), "VAones"], [("pY", hb)], mark=True)
                if vi == nv - 1:
                    epi_a(h)
                    pending.append((i + min(10, nv - 2), epi_b, h))
                    pending.append((i + min(16, 2 * nv - 2), epi_c, h))

            for i in range(n_it + LA):
                if i < n_it:
                    score(i)
                j = i - LA
                if j >= 0:
                    for item in [x for x in pending if x[0] <= j]:
                        item[1](item[2])
                        pending.remove(item)
                    pv(j)
            for item in list(pending):
                item[1](item[2])

        def gmlp_and_transposes():
            for t in range(4):
                S.op("dve", lambda E, t=t: E.bn_stats(out=lnst[:, 0:6], in_=vg[:, t, :]), ["VGM"], ["lnst"])
                S.op("dve", lambda E, t=t: E.bn_aggr(out=lnmv4[:, t, 0:2], in_=lnst[:, 0:6]), ["lnst"], ["lnmv"])
            S.ts("dve", lnmv4[:, :, 2], lnmv4[:, :, 1], EPS, None, ALU.add, None, ["lnmv"], ["lnmv2"])
            S.act(lnmv4[:, :, 3], lnmv4[:, :, 2], AF.Sqrt, ["lnmv2"], ["lnmv3"])
            S.op("dve", lambda E: E.reciprocal(out=lnmv4[:, :, 2], in_=lnmv4[:, :, 3]), ["lnmv3"], ["lnmv2"])

            def b1(t):
                b = t % 2
                S.ts("dve", gtmp[:, b, :], vg[:, t, :], lnmv4[:, t, 0:1], lnmv4[:, t, 2:3], ALU.subtract, ALU.mult,
                     ["VGM", "lnmv", "lnmv2"], [("gtmp", b)])
                S.tt("dve", gtmp[:, b, :], gtmp[:, b, :], lng[:, 0, :], ALU.mult, [("gtmp", b), "lng"], [("gtmp", b)])
                S.tt("dve", nbuf[:, b, 0:512], gtmp[:, b, :], lng[:, 1, :], ALU.add, [("gtmp", b), "lng"], [("nbuf", b)])
                by = t % 2
                for gi in range(8):
                    S.mm(pY[:, by, gi * 64:(gi + 1) * 64], wsm[:, gi, :], nbuf[:, b, gi * 64:(gi + 1) * 64], True, True,
                         ["wsm", ("nbuf", b)], [("pY", by)], mark=(gi == 7))

            def b2(t):
                b = t % 2
                by = t % 2
                S.tt("dve", gtmp[:, b, :].rearrange("p (g d) -> p g d", d=64), pY[:, by, :].rearrange("p (g d) -> p g d", d=64),
                     bst[:, :, None].to_broadcast([128, 8, 64]), ALU.add, [("pY", by), "bst"], [("gtmp", b)])
                S.tt("dve", gm[:, t, :], gtmp[:, b, :], ug[:, t, :], ALU.mult, [("gtmp", b), ("ug", t, 0), ("ug", t, 1)], [("PT", t)])
                pb = nxt("pT")
                for kc in range(4):
                    S.tr(pT[:, pb, 512 + kc * 128:512 + (kc + 1) * 128], gm[:, t, kc * 128:(kc + 1) * 128], idb[:],
                         [("PT", t), "idb"], [("pT", pb)], mark=(kc == 3))
                S.copy("act", gmT[:, :, t * 128:(t + 1) * 128], pT[:, pb, 512:1024].rearrange("p (c k) -> p c k", k=128),
                       [("pT", pb)], ["QA"])

            b1(0)
            b1(1)
            b2(0)
            b1(2)
            b2(1)
            b1(3)
            b2(2)
            b2(3)

        def sig1_slot(d):
            if d < 6:
                return HT[:, d, :], [("HT", d, 0)]
            return nbuf[:, d - 6, 512:1024], [("nbuf", d - 6)]

        def early_g1():
            nrd = [("nT", i) for i in range(4)]
            for cpair in range(4):
                w1, k1, i1 = W.acquire(wgt_v[:, :, cpair * 256:(cpair + 1) * 256], (128, 8, 256))
                for j in range(2):
                    dchunk = cpair * 2 + j
                    for dc in range(8):
                        S.mm(pY[:, j, :], w1[:, dc, j * 128:(j + 1) * 128], nT[:, dc, 0:512], dc == 0, dc == 7, [k1] + nrd, [("pY", j)])
                    s1, s1k = sig1_slot(dchunk)
                    S.act(s1, pY[:, j, :], AF.Sigmoid, [("pY", j), "bgt"], s1k, bias=bgt[:, dchunk:dchunk + 1])
                W.release(i1)

        def hoisted_gates():
            nrd = [("nT", i) for i in range(4)]
            grd = ["QA"]
            for dq in range(2):
                wb, kb, ib = W.acquire(wbg_v[:, :, dq * 512:(dq + 1) * 512], (128, 4, 512))
                for dp in range(2):
                    cpair = dq * 2 + dp
                    w2, k2, i2 = W.acquire(wgt_v[:, :, D + cpair * 256:D + (cpair + 1) * 256], (128, 8, 256))
                    for j in range(2):
                        dchunk = cpair * 2 + j
                        lo = (dp * 2 + j) * 128
                        for dc in range(8):
                            S.mm(pY[:, j, :], w2[:, dc, j * 128:(j + 1) * 128], nT[:, dc, 0:512], dc == 0, dc == 7, [k2] + nrd, [("pY", j)])
                        bb = nxt("pB")
                        for kc in range(4):
                            S.mm(pB[:, bb, :], wb[:, kc, lo:lo + 128], gmT[:, kc, :], kc == 0, kc == 3, [kb] + grd, [("pB", bb)])
                        S.act(sg[:, j, :], pY[:, j, :], AF.Sigmoid, [("pY", j), "bgt"], [("sg", j)], bias=bgt[:, 8 + dchunk:9 + dchunk])
                        S.tt("dve", mrgT[:, dchunk, :], sg[:, j, :], pB[:, bb, :], ALU.mult, [("sg", j), ("pB", bb)], ["VGM"])
                    W.release(i2)
                W.release(ib)

        def merge_and_out():
            ard = ["QA"]
            for dq in range(2):
                wa, ka, ia = W.acquire(wba_v[:, :, dq * 512:(dq + 1) * 512], (128, 4, 512))
                for dd in range(4):
                    dchunk = dq * 4 + dd
                    lo = dd * 128
                    ba = nxt("pA")
                    for kc in range(4):
                        S.mm(pA[:, ba, :], wa[:, kc, lo:lo + 128], attnT[:, kc, :], kc == 0, kc == 3, [ka] + ard, [("pA", ba)])
                    s1, s1k = sig1_slot(dchunk)
                    gb = dchunk % 2
                    S.tt("dve", gtmp[:, gb, :], s1, pA[:, ba, :], ALU.mult, s1k + [("pA", ba)], [("gtmp", gb)])
                    S.tt("dve", mrgT[:, dchunk, :], gtmp[:, gb, :], mrgT[:, dchunk, :], ALU.add, [("gtmp", gb)], ["VGM"])
                W.release(ia)
            mrd = ["VGM"]
            for ch in range(4):
                w, kw, iw = W.acquire(wout_v[:, :, ch * 256:(ch + 1) * 256], (128, 8, 256))
                for t in range(4):
                    ba = nxt("pA")
                    for dc in range(8):
                        S.mm(pA[:, ba, 0:256], mrgT[:, dc, t * 128:(t + 1) * 128], w[:, dc, :], dc == 0, dc == 7, [kw] + mrd, [("pA", ba)])
                    S.tt("dve", xh[:, t, ch * 256:(ch + 1) * 256], pA[:, ba, 0:256], xh[:, t, ch * 256:(ch + 1) * 256], ALU.add,
                         [("pA", ba)], [("xh", t)])
                    if ch == 3:
                        rms_stats_tile(t)
                W.release(iw)
            rms_rstd(4, tiles_done=True)

        def final_norm_store(g):
            rms_rstd(4)
            for t in range(4):
                S.stt(xh[:, t, :], xh[:, t, :], stat[:, 3, t:t + 1], normg[:, 3, :], ALU.mult, ALU.mult,
                      [("stat", 3), "normg"], [("xh", t)])
            S.dma("sp", lambda E: E.dma_start(out=out_v[g], in_=xh[:]), "out%d" % g,
                  reads=[("xh", t) for t in range(4)], writes=[("out", g)])

        def load_x(g):
            S.dma("sp", lambda E: E.dma_start(out=xp0, in_=xown_v[g][:, 0:2, :]), "xin0", writes=["VGM"])
            S.dma("sp", lambda E: E.dma_start(out=xp1, in_=xown_v[g][:, 2:4, :]), "xin1", writes=["QA"])

        def body():
            load_x(0)
            rms_rstd(4, src_xpre)
            for g in range(4):
                rms_to_nT(4, 0, src_xpre, do_stats=False)
                if g > 0:
                    final_norm_store(g - 1)
                ffn(4, 0, w1g_v, w1u_v, w1d_v, src=src_xpre, do_norm=False, tail_stats=True)
                rms_to_nT(4, 1, do_stats=False)
                qug_proj(g)
                gate_part1(g)
                early_g1()
                gmlp_and_transposes()
                hoisted_gates()
                attention(g)
                merge_and_out()
                nxt_hook = None
                if g < 3:
                    load_x(g + 1)
                    nxt_hook = lambda: rms_rstd(4, src_xpre)
                ffn(4, 2, w2g_v, w2u_v, w2d_v, hook=nxt_hook, do_stats=False)
            final_norm_store(3)

        S.dry = True
        body()
        S.dry = False
        for k in cnt:
            cnt[k] = 0
        W.reset()
        setup()
        body()
        S.final_wait("sp", [("out", g) for g in range(4)])
        S.emit()
    return nc


def _tables(p):
    seqblk = OWN[0] + OWN[1]
    past = np.zeros((8, 16), np.float32)
    nb = np.zeros((8, 16), np.float32)
    for s in range(8):
        i = OWN[p][s]
        for t in range(16):
            if seqblk[t] < i:
                past[s, t] = 1.0
            elif t == p * 8 + s:
                nb[s, t] = 1.0e9
            else:
                nb[s, t] = -1.0e9
    past = np.ascontiguousarray(np.broadcast_to(past.reshape(1, 128), (128, 128)))
    nb = np.ascontiguousarray(np.broadcast_to(nb.reshape(1, 128), (128, 128)))
    return past, nb


def make_in_maps(inputs):
    f = lambda a: np.ascontiguousarray(np.asarray(a, dtype=np.float32))
    x = f(inputs["x"])
    rep = lambda v, n: np.ascontiguousarray(np.broadcast_to(f(v).reshape(1, n), (128, n)))
    norms = np.stack([rep(inputs["ffn1_norm"], D), rep(inputs["mix_norm"], D),
                      rep(inputs["ffn2_norm"], D), rep(inputs["final_norm"], D)], axis=1)
    lngb = np.stack([rep(inputs["gmlp_ln_g"], 512), rep(inputs["gmlp_ln_b"], 512)], axis=1)
    ws = f(inputs["gmlp_w_s"]).reshape(8, 128, 128)
    wsT = np.ascontiguousarray(ws.transpose(2, 0, 1))
    jj, ii = np.meshgrid(np.arange(128), np.arange(128), indexing="ij")
    trilT = (jj <= ii).astype(np.float32)
    bsT = np.ascontiguousarray(f(inputs["gmlp_b_s"]).reshape(8, 128).T)
    bgate = np.ascontiguousarray(f(inputs["b_gate"]).reshape(16, 128).T)
    ident = np.eye(128, dtype=np.float32)
    shodd = np.zeros((64, 128), np.float32)
    shodd[np.arange(64), 64 + np.arange(64)] = 1.0
    ej = np.zeros((16, 16, 128), np.float32)
    for j in range(16):
        ej[j, j, :] = 1.0
    ej = ej.reshape(16, 16 * 128)
    cm = np.zeros((128, 2, 256), np.float32)
    for kt in range(2):
        kpos = kt * 128 + np.arange(128)[:, None]
        qpos = np.arange(256)[None, :]
        cm[:, kt, :] = np.where(kpos <= qpos, 0.0, NEG)
    shared = {
        "ffn1_w_gate": f(inputs["ffn1_w_gate"]).reshape(D, FF), "ffn1_w_up": f(inputs["ffn1_w_up"]).reshape(D, FF),
        "ffn1_w_down": f(inputs["ffn1_w_down"]).reshape(FF, D),
        "ffn2_w_gate": f(inputs["ffn2_w_gate"]).reshape(D, FF), "ffn2_w_up": f(inputs["ffn2_w_up"]).reshape(D, FF),
        "ffn2_w_down": f(inputs["ffn2_w_down"]).reshape(FF, D),
        "w_in": f(inputs["w_in"]).reshape(D, INW),
        "w_branch_attn": f(inputs["w_branch_attn"]).reshape(512, D),
        "w_branch_gmlp": f(inputs["w_branch_gmlp"]).reshape(512, D),
        "w_gate": f(inputs["w_gate"]).reshape(D, 2 * D), "w_out": f(inputs["w_out"]).reshape(D, D),
        "norms": np.ascontiguousarray(norms), "lngb": np.ascontiguousarray(lngb), "wsT": wsT, "trilT": trilT,
        "bsT": bsT, "bgate": bgate, "ident": ident, "shodd": shodd, "ej": ej,
    }
    in_maps = []
    for c in range(8):
        b, p = c // 2, c % 2
        xb = x[b].reshape(16, BLK, D)
        m = dict(shared)
        m["x_own"] = np.ascontiguousarray(xb[OWN[p]].reshape(TOK, D))
        cmm = np.zeros((128, 2, 2, 256), np.float32)
        cmm[:, p] = cm
        m["cmask"] = cmm.reshape(128, 1024)
        m["past"], m["nb"] = _tables(p)
        in_maps.append(m)
    return in_maps


def assemble(outs):
    y = np.zeros((BATCH, 16, BLK, D), np.float32)
    for c in range(8):
        b, p = c // 2, c % 2
        y[b, OWN[p]] = np.asarray(outs[c], dtype=np.float32).reshape(8, BLK, D)
    return y.reshape(BATCH, SEQ, D)


_NC = None


def kernel(**inputs):
    global _NC
    if _NC is None:
        _NC = build_program()
    in_maps = make_in_maps(inputs)
    res = run_bass_kernel_spmd(_NC, in_maps, core_ids=list(range(8)))
    return assemble([r["out"] for r in res.results])
```

```python
import numpy as np
from collections import deque
from contextlib import ExitStack
import concourse.bass as bass
import concourse.mybir as mybir
from concourse.bass_utils import run_bass_kernel_spmd

F32 = mybir.dt.float32
BF16 = mybir.dt.bfloat16
AF = mybir.ActivationFunctionType
ALU = mybir.AluOpType
AX = mybir.AxisListType

D = 1024
FF = 2816
NFC = FF // 128
NH = 8
DH = 64
INW = 2560
SEQ = 4096
BATCH = 4
BLK = 256
TOK = 2048
EPS = 1e-6
OWN = [[0, 3, 4, 7, 8, 11, 12, 15], [1, 2, 5, 6, 9, 10, 13, 14]]
NEG = -30000.0
RING = 7
NPT = 4
FPH = [[0, 1, 2], [3, 4, 5], [6, 7, 8], [9, 10]]


class Sched:
    ENG = ("pe", "act", "dve", "pool", "sp")

    def __init__(self, nc, stack):
        self.nc = nc
        self.stack = stack
        self.dry = False
        self.prog = {e: [] for e in self.ENG}
        self.sem = {}
        self.cnt = {}
        for e in ("pe", "act", "dve", "pool"):
            self.sem[e] = stack.enter_context(nc.semaphore("sem_" + e))
            self.cnt[e] = 0
        self.waited = {}
        self.last_w = {}
        self.readers = {}
        self.dma_sems = {}

    def dma_sem(self, name):
        if name not in self.dma_sems:
            s = self.stack.enter_context(self.nc.semaphore("dsem_" + name))
            self.dma_sems[name] = [s, 0]
            self.sem[name] = s
        return self.dma_sems[name]

    def _wait(self, eng, semkey, value):
        if value <= 0:
            return
        if eng == "pe" and semkey == "pe":
            return
        k = (eng, semkey)
        if self.waited.get(k, 0) >= value:
            return
        self.waited[k] = value
        s = self.sem[semkey]
        self.prog[eng].append(lambda E: E.wait_ge(s, value))

    def _deps(self, eng, reads, writes):
        for r in reads:
            lw = self.last_w.get(r)
            if lw is not None:
                self._wait(eng, lw[0], lw[1])
        for w in writes:
            lw = self.last_w.get(w)
            if lw is not None:
                self._wait(eng, lw[0], lw[1])
            for rd in self.readers.get(w, ()):
                self._wait(eng, rd[0], rd[1])

    def _record(self, token, reads, writes):
        for w in writes:
            self.last_w[w] = token
            self.readers[w] = []
        for r in reads:
            if r in writes:
                continue
            lst = self.readers.setdefault(r, [])
            if not lst or lst[-1] != token:
                lst.append(token)
                if len(lst) > 24:
                    best = {}
                    for t in lst:
                        if t[0] not in best or best[t[0]][1] < t[1]:
                            best[t[0]] = t
                    lst[:] = list(best.values())

    PSUM_KEYS = ("pA", "pB", "pY", "pT")

    def op(self, eng, fn, reads=(), writes=(), mark=True):
        if self.dry:
            return
        pr = [r for r in reads if isinstance(r, tuple) and r[0] in self.PSUM_KEYS and r not in writes]
        if pr:
            reads = [r for r in reads if r not in pr]
            writes = list(writes) + pr
        self._deps(eng, reads, writes)
        if mark:
            self.cnt[eng] += 1
            s = self.sem[eng]
            self.prog[eng].append(lambda E: fn(E).then_inc(s, 1))
            tok = (eng, self.cnt[eng])
        else:
            self.prog[eng].append(fn)
            tok = (eng, self.cnt[eng] + 1)
        self._record(tok, reads, writes)

    def dma(self, queue, fn, semname, reads=(), writes=()):
        if self.dry:
            return
        self._deps(queue, reads, writes)
        ds = self.dma_sem(semname)
        ds[1] += 16
        s = ds[0]
        self.prog[queue].append(lambda E: fn(E).then_inc(s, 16))
        self._record((semname, ds[1]), reads, writes)

    def coll(self, fn, semname, reads=(), writes=()):
        if self.dry:
            return
        self._deps("pool", reads, writes)
        ds = self.dma_sem(semname)
        ds[1] += 1
        s = ds[0]
        self.prog["pool"].append(lambda E: fn(E).then_inc(s, 1))
        self._record((semname, ds[1]), reads, writes)

    def barrier(self):
        if self.dry:
            return
        toks = {}
        for d in (self.last_w,):
            for t in d.values():
                if t[0] not in toks or toks[t[0]] < t[1]:
                    toks[t[0]] = t[1]
        for lst in self.readers.values():
            for t in lst:
                if t[0] not in toks or toks[t[0]] < t[1]:
                    toks[t[0]] = t[1]
        for e in ("pe", "act", "dve", "pool", "sp"):
            for k, v in toks.items():
                if k == "pe" and v > self.cnt["pe"]:
                    continue
                self._wait(e, k, v)

    def final_wait(self, eng, regions):
        for r in regions:
            lw = self.last_w.get(r)
            if lw is not None:
                self._wait(eng, lw[0], lw[1])

    def mm(self, out, lhsT, rhs, start, stop, reads, writes, mark=None):
        if mark is None:
            mark = stop
        self.op("pe", lambda E: E.matmul(out, lhsT=lhsT, rhs=rhs, start=start, stop=stop),
                reads, writes, mark)

    def tr(self, out, in_, ident, reads, writes, mark=True):
        self.op("pe", lambda E: E.transpose(out=out, in_=in_, identity=ident), reads, writes, mark)

    def act(self, out, in_, func, reads, writes, bias=None, scale=None, accum_out=None):
        kw = {}
        if bias is not None:
            kw["bias"] = bias
        if scale is not None:
            kw["scale"] = scale
        if accum_out is not None:
            kw["accum_out"] = accum_out
        self.op("act", lambda E: E.activation(out=out, in_=in_, func=func, **kw), reads, writes)

    def tt(self, eng, out, in0, in1, op, reads, writes):
        self.op(eng, lambda E: E.tensor_tensor(out=out, in0=in0, in1=in1, op=op), reads, writes)

    def ts(self, eng, out, in0, s1, s2, op0, op1, reads, writes):
        if op1 is None:
            self.op(eng, lambda E: E.tensor_scalar(out=out, in0=in0, scalar1=s1, scalar2=None, op0=op0),
                    reads, writes)
        else:
            self.op(eng, lambda E: E.tensor_scalar(out=out, in0=in0, scalar1=s1, scalar2=s2, op0=op0, op1=op1),
                    reads, writes)

    def stt(self, out, in0, scalar, in1, op0, op1, reads, writes):
        self.op("dve", lambda E: E.scalar_tensor_tensor(out=out, in0=in0, scalar=scalar, in1=in1, op0=op0, op1=op1),
                reads, writes)

    def copy(self, eng, out, in_, reads, writes):
        if eng == "act":
            self.op("act", lambda E: E.copy(out=out, in_=in_), reads, writes)
        else:
            self.op(eng, lambda E: E.tensor_copy(out=out, in_=in_), reads, writes)

    def emit(self):
        nc = self.nc
        prog = self.prog
        with nc.Block() as block:
            @block.tensor
            def _(E):
                for f in prog["pe"]:
                    f(E)

            @block.scalar
            def _(E):
                for f in prog["act"]:
                    f(E)

            @block.vector
            def _(E):
                for f in prog["dve"]:
                    f(E)

            @block.gpsimd
            def _(E):
                for f in prog["pool"]:
                    f(E)

            @block.sync
            def _(E):
                for f in prog["sp"]:
                    f(E)


class WStream:
    def __init__(self, S, ring, R):
        self.S = S
        self.ring = ring
        self.R = R
        self.plan = []
        self.reset()

    def reset(self):
        self.idx = 0
        self.next_load = 0
        self.live = deque()

    def view(self, slot, shape):
        P, a, b = shape
        return self.ring[0:P, slot, 0:a * b].rearrange("p (a b) -> p a b", a=a)

    def acquire(self, src, shape):
        if self.S.dry:
            self.plan.append((src, shape))
            return self.view(0, shape), ("wr", 0), -1
        i = self.idx
        self.idx += 1
        assert self.plan[i][1] == shape
        self.live.append(i)
        self.pump()
        assert i < self.next_load
        return self.view(i % self.R, shape), ("wr", i % self.R), i

    def release(self, i):
        if self.S.dry:
            return
        self.live.remove(i)
        self.pump()

    def pump(self):
        oldest = self.live[0] if self.live else self.idx
        while self.next_load < len(self.plan) and self.next_load < oldest + self.R:
            j = self.next_load
            slot = j % self.R
            src, shape = self.plan[j]
            dst = self.view(slot, shape)
            self.S.dma("pool", lambda E, dst=dst, src=src: E.dma_start(out=dst, in_=src),
                       "wr%d" % slot, writes=[("wr", slot)])
            self.next_load += 1


def build_program(NPAIR=4):
    nc = bass.Bass("TRN2", target_bir_lowering=False)

    def din(name, shape):
        return nc.dram_tensor(name, list(shape), F32, kind="ExternalInput").ap()

    x_own = din("x_own", [TOK, D])
    w1g = din("ffn1_w_gate", [D, FF]); w1u = din("ffn1_w_up", [D, FF]); w1d = din("ffn1_w_down", [FF, D])
    w2g = din("ffn2_w_gate", [D, FF]); w2u = din("ffn2_w_up", [D, FF]); w2d = din("ffn2_w_down", [FF, D])
    w_in = din("w_in", [D, INW])
    w_ba = din("w_branch_attn", [512, D]); w_bg = din("w_branch_gmlp", [512, D])
    w_gt = din("w_gate", [D, 2 * D]); w_out = din("w_out", [D, D])
    norms = din("norms", [128, 4, D])
    lngb = din("lngb", [128, 2, 512])
    wsT = din("wsT", [128, 8, 128])
    trilT = din("trilT", [128, 128])
    bsT = din("bsT", [128, 8])
    bgate = din("bgate", [128, 16])
    ident_d = din("ident", [128, 128])
    shodd_d = din("shodd", [64, 128])
    ej_d = din("ej", [16, 16 * 128])
    cmask_d = din("cmask", [128, 2 * 2 * 256])
    past_d = din("past", [128, 8 * 16])
    nb_d = din("nb", [128, 8 * 16])
    out_d = nc.dram_tensor("out", [TOK, D], F32, kind="ExternalOutput").ap()

    def wview(w):
        return w.rearrange("(dc p) f -> p dc f", p=128)

    w1g_v, w1u_v, w2g_v, w2u_v = wview(w1g), wview(w1u), wview(w2g), wview(w2u)
    w1d_v = w1d.rearrange("(fc p) d -> p fc d", p=128)
    w2d_v = w2d.rearrange("(fc p) d -> p fc d", p=128)
    win_v = wview(w_in)
    wgt_v = wview(w_gt)
    wout_v = wview(w_out)
    wba_v = w_ba.rearrange("(kc p) d -> p kc d", p=128)
    wbg_v = w_bg.rearrange("(kc p) d -> p kc d", p=128)
    xown_v = x_own.rearrange("(g t p) d -> g p t d", p=128, t=4)
    NGRP = [[2 * i, 2 * i + 1] for i in range(NPAIR)]
    xsrc = [nc.dram_tensor("xsrc%d" % g, [128, 4096], BF16) for g in range(4)]
    xdst = [nc.dram_tensor("xdst%d" % g, [256, 4096], BF16) for g in range(4)]
    msrc = [nc.dram_tensor("msrc%d" % g, [128, 8], F32) for g in range(4)]
    mdst = [nc.dram_tensor("mdst%d" % g, [256, 8], F32) for g in range(4)]
    out_v = out_d.rearrange("(g t p) d -> g p t d", p=128, t=4)

    with ExitStack() as st:
        S = Sched(nc, st)

        def sb(name, shape, dt):
            return st.enter_context(nc.sbuf_tensor(name, list(shape), dt))

        def ps(name, shape, dt):
            return st.enter_context(nc.psum_tensor(name, list(shape), dt))

        ring = sb("wring", [128, RING, 2048], BF16)
        W = WStream(S, ring, RING)
        KT = sb("KT", [128, 4, SEQ], BF16)
        VA = sb("VA", [128, 32, NH, DH + 1], BF16)
        kmT = sb("kmT", [128, 4, 16], F32)
        normg = sb("normg", [128, 4, D], F32)
        lng = sb("lng", [128, 2, 512], F32)
        wsm = sb("wsm", [128, 8, 128], BF16)
        trl = sb("trl", [128, 128], BF16)
        bst = sb("bst", [128, 8], F32)
        bgt = sb("bgt", [128, 16], F32)
        idb = sb("idb", [128, 128], BF16)
        ejb = sb("ejb", [128, 16, 128], BF16)
        cmb = sb("cmb", [128, 2, 2, 256], BF16)
        kms = sb("kms", [128, 4, 2], F32)
        pastt = sb("pastt", [128, 8, 16], F32)
        nbt = sb("nbt", [128, 8, 16], F32)
        xh = sb("xh", [128, 4, D], F32)
        nT = sb("nT", [128, 8, 512], BF16)
        HT = sb("HT", [128, 6, 512], BF16)
        nbuf = sb("nbuf", [128, 2, D], BF16)
        sg = sb("sg", [128, 2, 512], F32)
        st6 = sb("st6", [128, 12], F32)
        mv = sb("mv", [128, 8, 2], F32)
        kmv = sb("kmv", [128, 2], F32)
        stat = sb("stat", [128, 4, 8], F32)
        QTz = sb("QTz", [128, 8, 512], BF16)
        kmz = sb("kmz", [128, 8, 16], BF16)
        qa = sb("qa", [128, 8, 512], BF16)
        QTlo = qa[:, 0:4, :]
        kmh = sb("kmh", [128, 2, 4, 16], BF16)
        kmd = sb("kmd", [128, 4, 16], F32)
        attnT = qa[:, 0:4, :]
        gmT = qa[:, 4:8, :]
        ug = sb("ug", [128, 4, 512], BF16)
        vgm = sb("vgm", [128, 8, 512], BF16)
        vg = vgm[:].rearrange("p a b -> p (a b)").bitcast(F32).rearrange("p (a b) -> p a b", a=4)
        mrgT = vgm
        gtmp = sb("gtmp", [128, 2, 512], F32)
        gmb = sb("gmb", [128, 8, 16], F32)
        mx8 = sb("mx8", [128, 8, 8], F32)
        thr = sb("thr", [128, 8], F32)
        biasq = sb("biasq", [128, 4, 8, 16], BF16)
        biasT = sb("biasT", [128, 8, 512], BF16)
        PT = sb("PT", [128, NPT, 512], BF16)
        shodd = sb("shodd_sb", [128, 128], BF16)
        ones_b = sb("ones_b", [128, 64], BF16)
        gm = PT
        lnst = sb("lnst", [128, 8], F32)
        lnmv4 = sb("lnmv4", [128, 4, 4], F32)
        pA = ps("pA", [128, 2, 512], F32)
        pB = ps("pB", [128, 2, 512], F32)
        pY = ps("pY", [128, 2, 512], F32)
        pT = ps("pT", [128, 2, 1024], BF16)
        cnt = {"pA": 0, "pB": 0, "pY": 0, "pT": 0}

        def nxt(name):
            cnt[name] += 1
            return cnt[name] % 2

        def setup():
            S.dma("sp", lambda E: E.dma_start(out=normg[:], in_=norms), "c0", writes=["normg"])
            S.dma("sp", lambda E: E.dma_start(out=lng[:], in_=lngb), "c1", writes=["lng"])
            S.dma("pool", lambda E: E.dma_start(out=wsm[:], in_=wsT), "c2", writes=["wsm"])
            S.dma("pool", lambda E: E.dma_start(out=trl[:], in_=trilT), "c3", writes=["trl"])
            S.dma("sp", lambda E: E.dma_start(out=bst[:], in_=bsT), "c4", writes=["bst"])
            S.dma("sp", lambda E: E.dma_start(out=bgt[:], in_=bgate), "c5", writes=["bgt"])
            S.dma("pool", lambda E: E.dma_start(out=idb[:], in_=ident_d), "c6", writes=["idb"])
            S.dma("pool", lambda E: E.dma_start(out=shodd[0:64, :], in_=shodd_d), "c11", writes=["shodd"])
            S.op("dve", lambda E: E.memset(ones_b[:], 1.0), [], ["ones_b"])
            S.op("dve", lambda E: E.memset(ejb[:], 0.0), [], ["ejb"])
            S.op("dve", lambda E: E.memset(biasT[:], 0.0), [], [("biasT", hp) for hp in range(4)])
            S.op("dve", lambda E: E.memset(QTz[:], 0.0), [], [("QTz", pc) for pc in range(4)])
            S.op("dve", lambda E: E.memset(kmz[:], 0.0), [], ["kmz"])
            S.dma("pool", lambda E: E.dma_start(out=ejb[0:16, :, :].rearrange("p a b -> p (a b)"), in_=ej_d), "c7", writes=["ejb"])
            S.dma("pool", lambda E: E.dma_start(out=cmb[:].rearrange("p m a b -> p (m a b)"), in_=cmask_d), "c8", writes=["cmb"])
            S.dma("sp", lambda E: E.dma_start(out=pastt[:].rearrange("p a b -> p (a b)"), in_=past_d), "c9", writes=["pastt"])
            S.dma("sp", lambda E: E.dma_start(out=nbt[:].rearrange("p a b -> p (a b)"), in_=nb_d), "c10", writes=["nbt"])
            S.tt("dve", wsm[:], wsm[:], trl[:, None, :].to_broadcast([128, 8, 128]), ALU.mult, ["trl"], ["wsm"])
            S.op("dve", lambda E: E.memset(VA[:, :, :, DH:DH + 1], 1.0), [], ["VAones"])
            S.op("dve", lambda E: E.memset(kmT[:], 0.0), [], [("kmT", pc, i) for pc in range(4) for i in range(8)])

        def src_xh(t):
            return xh[:, t, :], ("xh", t)

        xp0 = vgm[:].rearrange("p a b -> p (a b)").bitcast(F32).rearrange("p (t d) -> p t d", t=2)
        xp1 = qa[:].rearrange("p a b -> p (a b)").bitcast(F32).rearrange("p (t d) -> p t d", t=2)

        def src_xpre(t):
            return (xp0[:, t, :], "VGM") if t < 2 else (xp1[:, t - 2, :], "QA")

        def rms_stats_tile(t, src=src_xh):
            xa, xk = src(t)
            S.op("dve", lambda E, xa=xa: E.bn_stats(out=st6[:, 0:6], in_=xa[:, 0:512]), [xk], ["st6a"])
            S.op("dve", lambda E, xa=xa: E.bn_stats(out=st6[:, 6:12], in_=xa[:, 512:1024]), [xk], ["st6b"])
            S.op("dve", lambda E, t=t: E.bn_aggr(out=mv[:, t, :], in_=st6[:, 0:12]), ["st6a", "st6b"], [("mv", t)])

        def rms_rstd(NT, src=src_xh, tiles_done=False):
            if not tiles_done:
                for t in range(NT):
                    rms_stats_tile(t, src)
            rd = [("mv", t) for t in range(NT)]
            S.tt("dve", stat[:, 0, 0:NT], mv[:, 0:NT, 0], mv[:, 0:NT, 0], ALU.mult, rd, [("stat", 0)])
            S.tt("dve", stat[:, 1, 0:NT], stat[:, 0, 0:NT], mv[:, 0:NT, 1], ALU.add, rd + [("stat", 0)], [("stat", 1)])
            S.ts("dve", stat[:, 1, 0:NT], stat[:, 1, 0:NT], EPS, None, ALU.add, None, [("stat", 1)], [("stat", 1)])
            S.act(stat[:, 2, 0:NT], stat[:, 1, 0:NT], AF.Sqrt, [("stat", 1)], [("stat", 2)])
            S.op("dve", lambda E: E.reciprocal(out=stat[:, 3, 0:NT], in_=stat[:, 2, 0:NT]), [("stat", 2)], [("stat", 3)])

        def rms_to_nT(NT, which, src=src_xh, do_stats=True):
            if do_stats:
                rms_rstd(NT, src)
            for t in range(NT):
                b = t % 2
                xa, xk = src(t)
                S.stt(nbuf[:, b, :], xa, stat[:, 3, t:t + 1], normg[:, which, :], ALU.mult, ALU.mult,
                      [xk, ("stat", 3), "normg"], [("nbuf", b)])
                pb = nxt("pT")
                for dc in range(8):
                    S.tr(pT[:, pb, dc * 128:(dc + 1) * 128], nbuf[:, b, dc * 128:(dc + 1) * 128], idb[:],
                         [("nbuf", b), "idb"], [("pT", pb)], mark=(dc == 7))
                S.copy("act", nT[:, :, t * 128:(t + 1) * 128], pT[:, pb, :].rearrange("p (dc k) -> p dc k", k=128),
                       [("pT", pb)], [("nT", t)])

        def ffn(NT, which, wg_v, wu_v, wd_v, src=src_xh, do_norm=True, hook=None, do_stats=True, tail_stats=False):
            if do_norm:
                rms_to_nT(NT, which, src, do_stats=do_stats)
            nhalf = NT // 4
            for phi, ph in enumerate(FPH):
                if hook is not None and phi == len(FPH) - 1:
                    hook()
                for pi, c in enumerate(ph):
                    wg, kg, ig = W.acquire(wg_v[:, :, c * 256:(c + 1) * 256], (128, 8, 256))
                    wu, ku, iu = W.acquire(wu_v[:, :, c * 256:(c + 1) * 256], (128, 8, 256))
                    for half in range(nhalf):
                        nrd = [("nT", half * 4 + i) for i in range(4)]
                        for j in range(2):
                            fl = pi * 2 + j
                            ba = nxt("pA")
                            for dc in range(8):
                                S.mm(pA[:, ba, :], wg[:, dc, j * 128:(j + 1) * 128], nT[:, dc, half * 512:(half + 1) * 512],
                                     dc == 0, dc == 7, [kg] + nrd, [("pA", ba)])
                            bb = nxt("pB")
                            for dc in range(8):
                                S.mm(pB[:, bb, :], wu[:, dc, j * 128:(j + 1) * 128], nT[:, dc, half * 512:(half + 1) * 512],
                                     dc == 0, dc == 7, [ku] + nrd, [("pB", bb)])
                            S.act(sg[:, ba, :], pA[:, ba, :], AF.Silu, [("pA", ba)], [("sg", ba)])
                            S.tt("dve", HT[:, fl, half * 512:(half + 1) * 512], sg[:, ba, :], pB[:, bb, :], ALU.mult,
                                 [("sg", ba), ("pB", bb)], [("HT", fl, half)])
                    W.release(ig)
                    W.release(iu)
                wds = [W.acquire(wd_v[:, 2 * c:2 * c + 2, :], (128, 2, 1024)) for c in ph]
                nf = 2 * len(ph)
                for t in range(NT):
                    for hf in range(2):
                        by = nxt("pY")
                        for fl in range(nf):
                            wd, kd, _ = wds[fl // 2]
                            S.mm(pY[:, by, :], HT[:, fl, t * 128:(t + 1) * 128], wd[:, fl % 2, hf * 512:(hf + 1) * 512],
                                 fl == 0, fl == nf - 1, [kd, ("HT", fl, t // 4)], [("pY", by)])
                        if phi == 0:
                            xa, xk = src(t)
                        else:
                            xa, xk = src_xh(t)
                        S.stt(xh[:, t, hf * 512:(hf + 1) * 512], pY[:, by, :], 0.5, xa[:, hf * 512:(hf + 1) * 512],
                              ALU.mult, ALU.add, [("pY", by), xk], [("xh", t)])
                    if tail_stats and phi == len(FPH) - 1:
                        rms_stats_tile(t)
                for _, _, i in wds:
                    W.release(i)
            if tail_stats:
                rms_rstd(NT, tiles_done=True)

        def kv_proj(g):
            kst = gtmp[:].rearrange("p a b -> p (a b)").bitcast(BF16).rearrange("p (a b) -> p a b", a=4)
            vst = sg[:].rearrange("p a b -> p (a b)").bitcast(BF16).rearrange("p (t h d) -> p t h d", t=4, h=8)
            nrd = [("nT", i) for i in range(4)]
            for c in (2, 3):
                w, kw, iw = W.acquire(win_v[:, :, c * 256:(c + 1) * 256], (128, 8, 256))
                for j in range(2):
                    pc = (c - 2) * 2 + j
                    ba = nxt("pA")
                    for dc in range(8):
                        S.mm(pA[:, ba, :], w[:, dc, j * 128:(j + 1) * 128], nT[:, dc, 0:512],
                             dc == 0, dc == 7, [kw] + nrd, [("pA", ba)])
                    S.copy("act", kst[:, pc, :], pA[:, ba, :], [("pA", ba)], [("gtmp", pc // 2)])
                    for bl in range(2):
                        S.op("dve", lambda E, ba=ba, bl=bl: E.bn_stats(out=st6[:, 0:6], in_=pA[:, ba, bl * 256:(bl + 1) * 256]),
                             [("pA", ba)], ["st6a"])
                        S.op("dve", lambda E: E.bn_aggr(out=kmv[:, 0:2], in_=st6[:, 0:6]), ["st6a"], ["kmv"])
                        S.copy("dve", kms[:, pc, bl:bl + 1], kmv[:, 0:1], ["kmv"], ["kms"])
                W.release(iw)
            S.dma("sp", lambda E: E.dma_start(out=msrc[g][:, :], in_=kms[:].rearrange("p a b -> p (a b)")), "stm%d" % g,
                  reads=["kms"], writes=[("msrc", g)])
            S.coll(lambda E: E.collective_compute("AllGather", ALU.bypass, replica_groups=NGRP,
                                                  ins=[msrc[g].ap().opt()], outs=[mdst[g].ap().opt()]),
                   "ccm%d" % g, reads=[("msrc", g)], writes=[("mdst", g)])
            for c in (4, 5):
                w, kw, iw = W.acquire(win_v[:, :, c * 256:(c + 1) * 256], (128, 8, 256))
                for t in range(4):
                    by = nxt("pY")
                    for dc in range(8):
                        S.mm(pY[:, by, 0:256], nT[:, dc, t * 128:(t + 1) * 128], w[:, dc, :],
                             dc == 0, dc == 7, [kw, ("nT", t)], [("pY", by)])
                    h0 = (c - 4) * 4
                    S.copy("act", vst[:, t, h0:h0 + 4, :], pY[:, by, 0:256].rearrange("p (h d) -> p h d", d=DH),
                           [("pY", by)], [("sg", t // 2)])
                W.release(iw)
            S.dma("sp", lambda E: E.dma_start(out=xsrc[g][:, 0:2048], in_=kst.rearrange("p a b -> p (a b)")), "stk%d" % g,
                  reads=[("gtmp", 0), ("gtmp", 1)], writes=[("xsrc", g, 0)])
            S.dma("sp", lambda E: E.dma_start(out=xsrc[g][:, 2048:4096], in_=vst.rearrange("p t h d -> p (t h d)")), "stv%d" % g,
                  reads=[("sg", 0), ("sg", 1)], writes=[("xsrc", g, 1)])
            for m in range(2):
                tb = m * 8 + 2 * g
                S.dma("sp", lambda E, m=m, tb=tb: E.dma_start(
                    out=kmT[:, :, tb:tb + 2],
                    in_=mdst[g][m * 128:(m + 1) * 128, :].rearrange("p (a b) -> p a b", a=4)),
                    "rbm%d%d" % (g, m), reads=[("mdst", g)], writes=[("kmT", pc, m * 4 + g) for pc in range(4)])
            S.coll(lambda E: E.collective_compute("AllGather", ALU.bypass, replica_groups=NGRP,
                                                  ins=[xsrc[g].ap().opt()], outs=[xdst[g].ap().opt()]),
                   "cck%d" % g, reads=[("xsrc", g, 0), ("xsrc", g, 1)], writes=[("xdst", g)])
            for m in range(2):
                tb = m * 8 + 2 * g
                S.dma("sp", lambda E, m=m, tb=tb: E.dma_start(
                    out=KT[:, :, tb * 256:tb * 256 + 512],
                    in_=xdst[g][m * 128:(m + 1) * 128, 0:2048].rearrange("p (a b) -> p a b", a=4)),
                    "rbk%d%d" % (g, m), reads=[("xdst", g)], writes=[("KT", pc, m * 4 + g) for pc in range(4)])
                S.dma("sp", lambda E, m=m, tb=tb: E.dma_start(
                    out=VA[:, tb * 2:tb * 2 + 4, :, 0:DH],
                    in_=xdst[g][m * 128:(m + 1) * 128, 2048:4096].rearrange("p (t h d) -> p t h d", t=4, h=8)),
                    "rbv%d%d" % (g, m), reads=[("xdst", g)],
                    writes=[("VA", tb * 2 + i, hh) for i in range(4) for hh in range(2)])

        def gelu_from_psum(src, rkey, dst, wkey):
            S.act(dst, src, AF.Gelu_apprx_tanh, [rkey], [wkey])

        def qug_proj(g):
            kv_proj(g)
            for c in (0, 1):
                w, kw, iw = W.acquire(win_v[:, :, c * 256:(c + 1) * 256], (128, 8, 256))
                nrd = [("nT", i) for i in range(4)]
                for j in range(2):
                    pc = c * 2 + j
                    ba = nxt("pA")
                    for dc in range(8):
                        S.mm(pA[:, ba, :], w[:, dc, j * 128:(j + 1) * 128], nT[:, dc, 0:512],
                             dc == 0, dc == 7, [kw] + nrd, [("pA", ba)])
                    S.op("act", lambda E, pc=pc, ba=ba: E.mul(out=QTz[0:64, 2 * pc, :], in_=pA[0:64, ba, :], mul=0.125),
                         [("pA", ba)], [("QTz", pc)])
                    S.op("act", lambda E, pc=pc, ba=ba: E.mul(out=QTz[64:128, 2 * pc + 1, :], in_=pA[64:128, ba, :], mul=0.125),
                         [("pA", ba)], [("QTz", pc)])
                    S.stt(QTlo[0:64, pc, :], pA[0:64, ba, :], 0.125, QTz[0:64, 2 * pc, :], ALU.mult, ALU.subtract,
                          [("pA", ba), ("QTz", pc)], ["QA"])
                    S.stt(QTlo[64:128, pc, :], pA[64:128, ba, :], 0.125, QTz[64:128, 2 * pc + 1, :], ALU.mult, ALU.subtract,
                          [("pA", ba), ("QTz", pc)], ["QA"])
                W.release(iw)
            for c in (6, 7, 8, 9):
                w, kw, iw = W.acquire(win_v[:, :, c * 256:(c + 1) * 256], (128, 8, 256))
                for t in range(4):
                    by = nxt("pY")
                    for dc in range(8):
                        S.mm(pY[:, by, 0:256], nT[:, dc, t * 128:(t + 1) * 128], w[:, dc, :],
                             dc == 0, dc == 7, [kw, ("nT", t)], [("pY", by)])
                    if c < 8:
                        dst = ug[:, t, (c - 6) * 256:(c - 5) * 256]
                        wk = ("ug", t, c - 6)
                    else:
                        dst = vg[:, t, (c - 8) * 256:(c - 7) * 256]
                        wk = "VGM"
                    gelu_from_psum(pY[:, by, 0:256], ("pY", by), dst, wk)
                W.release(iw)

        def gate_part1(g):
            kmk = [("kmT", pc, i) for pc in range(4) for i in range(8)]
            S.copy("dve", kmh[:, 0, :, :], kmT[:], kmk, ["kmh0"])
            S.tt("dve", kmd[:], kmT[:], kmh[:, 0, :, :], ALU.subtract, kmk + ["kmh0"], ["kmd"])
            S.copy("dve", kmh[:, 1, :, :], kmd[:], ["kmd"], ["kmh1"])
            for pc in range(4):
                S.copy("dve", kmz[0:64, 2 * pc, :], kmh[0:64, 0, pc, :], ["kmh0"], ["kmz"])
                S.copy("dve", kmz[64:128, 2 * pc + 1, :], kmh[64:128, 0, pc, :], ["kmh0"], ["kmz"])
            for sub in range(4):
                s = 2 * g + sub // 2
                ba = nxt("pA")
                for h in range(NH):
                    pc = h // 2
                    qs = slice(sub * 128, (sub + 1) * 128)
                    terms = [(QTz[:, h, qs], kmh[:, 0, pc, :]), (QTz[:, h, qs], kmh[:, 1, pc, :]), (QTlo[:, pc, qs], kmz[:, h, :])]
                    for ti, (qq, kk) in enumerate(terms):
                        S.mm(pA[:, ba, h * 16:(h + 1) * 16], qq, kk,
                             ti == 0, ti == 2, [("QTz", pc), "QA", "kmh0", "kmh1", "kmz"], [("pA", ba)],
                             mark=(h == NH - 1 and ti == 2))
                S.tt("dve", gmb[:], pA[:, ba, 0:128].rearrange("p (h t) -> p h t", t=16),
                     pastt[:, s:s + 1, :].to_broadcast([128, 8, 16]), ALU.mult, [("pA", ba), "pastt"], ["gmb"])
                S.tt("dve", gmb[:], gmb[:], nbt[:, s:s + 1, :].to_broadcast([128, 8, 16]), ALU.add, ["gmb", "nbt"], ["gmb"])
                for h in range(NH):
                    S.op("dve", lambda E, h=h: E.max(out=mx8[:, h, :], in_=gmb[:, h, :]), ["gmb"], [("mx8", h)])
                S.ts("dve", thr[:], mx8[:, :, 3], -1.0e8, None, ALU.max, None, [("mx8", h) for h in range(NH)], ["thr"])
                S.tt("dve", gmb[:], gmb[:], thr[:, :, None].to_broadcast([128, 8, 16]), ALU.is_ge, ["gmb", "thr"], ["gmb"])
                S.ts("dve", biasq[:, sub, :, :], gmb[:], -NEG, NEG, ALU.mult, ALU.add, ["gmb"], [("biasq", sub)])
        def attention(g, filler=None):
            for hp in range(4):
                pb = nxt("pT")
                for hh in range(2):
                    h = hp * 2 + hh
                    for sub in range(4):
                        S.tr(pT[0:16, pb, hh * 512 + sub * 128: hh * 512 + (sub + 1) * 128], biasq[:, sub, h, :], idb[:],
                             [("biasq", sub), "idb"], [("pT", pb)], mark=(hh == 1 and sub == 3))
                S.copy("dve", biasT[0:16, hp * 2:hp * 2 + 2, :], pT[0:16, pb, :].rearrange("p (h q) -> p h q", q=512),
                       [("pT", pb)], [("biasT", hp)])
            nblk = 2 * g + 2
            visits = [(m * 8 + t, kt) for t in range(2 * g) for m in range(2) for kt in range(2)] + \
                     [(m * 8 + t, kt) for t in (2 * g, 2 * g + 1) for m in range(2) for kt in range(2)]
            nv = len(visits)
            sbufs = [(pA, 0), (pA, 1), (pB, 0), (pB, 1)]
            skeys = [("pA", 0), ("pA", 1), ("pB", 0), ("pB", 1)]
            bc = pT[:, 1, :].bitcast(F32)
            pk = pT[:, 0, :].bitcast(F32)
            recf = gtmp[:, 0, :]
            rhl = gtmp[:, 1, :].bitcast(BF16)
            ah = nbuf[:, :, 0:512]
            st_ = {"si": 0, "pti": 0}

            def epi_a(h):
                hb = h % 2
                S.copy("act", sg[0:65, hb, :], pY[0:65, hb, :], [("pY", hb)], [("sg", hb)])
                S.op("dve", lambda E: E.reciprocal(out=recf[64:65, :], in_=sg[64:65, hb, :]), [("sg", hb)], [("gtmp", 0)])
                S.copy("dve", rhl[64:65, 0:512], recf[64:65, :], [("gtmp", 0)], [("gtmp", 1)])
                S.tt("dve", recf[64:65, :], recf[64:65, :], rhl[64:65, 0:512], ALU.subtract, [("gtmp", 1)], [("gtmp", 0)])
                S.copy("dve", rhl[64:65, 512:1024], recf[64:65, :], [("gtmp", 0)], [("gtmp", 1)])

            def epi_b(h):
                hb = h % 2
                S.mm(bc[0:64, :], ones_b[64:65, :], rhl[64:65, 0:512], True, False, ["ones_b", ("gtmp", 1)], [("pT", 1)], mark=False)
                S.mm(bc[0:64, :], ones_b[64:65, :], rhl[64:65, 512:1024], False, True, ["ones_b", ("gtmp", 1)], [("pT", 1)], mark=True)
                S.tt("dve", ah[0:64, hb, :], sg[0:64, hb, :], bc[0:64, :], ALU.mult, [("sg", hb), ("pT", 1)], [("nbuf", hb)])

            def epi_c(h):
                hb = h % 2
                if hb == 0:
                    S.mm(pk, idb[0:64, :], ah[0:64, 0, :], True, False, ["idb", ("nbuf", 0)], [("pT", 0)], mark=True)
                else:
                    S.mm(pk, shodd[0:64, :], ah[0:64, 1, :], False, True, ["shodd", ("nbuf", 1)], [("pT", 0)], mark=True)
                    S.copy("act", attnT[:, h // 2, :], pk, [("pT", 0)], ["QA"])

            items = [(h, vi, t, kt) for h in range(NH) for vi, (t, kt) in enumerate(visits)]
            n_it = len(items)
            LA = 3
            pending = []

            def cols(t):
                return slice(256, 512) if t in (2 * g + 1, 8 + 2 * g + 1) else slice(0, 512)

            def score(i):
                h, vi, t, kt = items[i]
                pc = h // 2
                k0 = t * 256 + kt * 128
                ten, bi = sbufs[i % 4]
                sk = skeys[i % 4]
                cs = cols(t)
                diag = (t % 8) in (2 * g, 2 * g + 1)
                S.mm(ten[:, bi, cs], KT[:, pc, k0:k0 + 128], QTz[:, h, cs], True, False,
                     [("KT", pc, k0 // 512), ("QTz", pc)], [sk], mark=False)
                S.mm(ten[:, bi, cs], ejb[:, t, :], biasT[:, h, cs], False, not diag,
                     ["ejb", ("biasT", h // 2)], [sk], mark=(not diag))
                if diag:
                    qo = (t % 8 - 2 * g) * 256
                    S.mm(ten[:, bi, qo:qo + 256], idb[:], cmb[:, t // 8, kt, :], False, True, ["idb", "cmb"], [sk], mark=True)
                S.act(PT[:, i % NPT, cs], ten[:, bi, cs], AF.Exp, [sk], [("PT", i % NPT)])

            def pv(i):
                h, vi, t, kt = items[i]
                hb = h % 2
                cs = cols(t)
                S.mm(pY[0:65, hb, cs], VA[:, t * 2 + kt, h, :], PT[:, i % NPT, cs], vi == 0, vi == nv - 1,
                     [("PT", i % NPT), ("VA", t * 2 + kt, h // 4), "VAones"], [("pY", hb)], mark=True)
                if vi == nv - 1:
                    epi_a(h)
                    pending.append((i + min(10, nv - 2), epi_b, h))
                    pending.append((i + min(16, 2 * nv - 2), epi_c, h))

            for i in range(n_it + LA):
                if i < n_it:
                    score(i)
                j = i - LA
                if j >= 0:
                    for item in [x for x in pending if x[0] <= j]:
                        item[1](item[2])
                        pending.remove(item)
                    pv(j)
            if filler is not None:
                filler()
            for item in list(pending):
                item[1](item[2])

        def gmlp_and_transposes():
            for t in range(4):
                S.op("dve", lambda E, t=t: E.bn_stats(out=lnst[:, 0:6], in_=vg[:, t, :]), ["VGM"], ["lnst"])
                S.op("dve", lambda E, t=t: E.bn_aggr(out=lnmv4[:, t, 0:2], in_=lnst[:, 0:6]), ["lnst"], ["lnmv"])
            S.ts("dve", lnmv4[:, :, 2], lnmv4[:, :, 1], EPS, None, ALU.add, None, ["lnmv"], ["lnmv2"])
            S.act(lnmv4[:, :, 3], lnmv4[:, :, 2], AF.Sqrt, ["lnmv2"], ["lnmv3"])
            S.op("dve", lambda E: E.reciprocal(out=lnmv4[:, :, 2], in_=lnmv4[:, :, 3]), ["lnmv3"], ["lnmv2"])

            def b1(t):
                b = t % 2
                S.ts("dve", gtmp[:, b, :], vg[:, t, :], lnmv4[:, t, 0:1], lnmv4[:, t, 2:3], ALU.subtract, ALU.mult,
                     ["VGM", "lnmv", "lnmv2"], [("gtmp", b)])
                S.tt("dve", gtmp[:, b, :], gtmp[:, b, :], lng[:, 0, :], ALU.mult, [("gtmp", b), "lng"], [("gtmp", b)])
                S.tt("dve", nbuf[:, b, 0:512], gtmp[:, b, :], lng[:, 1, :], ALU.add, [("gtmp", b), "lng"], [("nbuf", b)])
                by = t % 2
                for gi in range(8):
                    S.mm(pY[:, by, gi * 64:(gi + 1) * 64], wsm[:, gi, :], nbuf[:, b, gi * 64:(gi + 1) * 64], True, True,
                         ["wsm", ("nbuf", b)], [("pY", by)], mark=(gi == 7))

            def b2(t):
                b = t % 2
                by = t % 2
                S.tt("dve", gtmp[:, b, :].rearrange("p (g d) -> p g d", d=64), pY[:, by, :].rearrange("p (g d) -> p g d", d=64),
                     bst[:, :, None].to_broadcast([128, 8, 64]), ALU.add, [("pY", by), "bst"], [("gtmp", b)])
                S.tt("dve", gm[:, t, :], gtmp[:, b, :], ug[:, t, :], ALU.mult, [("gtmp", b), ("ug", t, 0), ("ug", t, 1)], [("PT", t)])
                pb = nxt("pT")
                for kc in range(4):
                    S.tr(pT[:, pb, 512 + kc * 128:512 + (kc + 1) * 128], gm[:, t, kc * 128:(kc + 1) * 128], idb[:],
                         [("PT", t), "idb"], [("pT", pb)], mark=(kc == 3))
                S.copy("act", gmT[:, :, t * 128:(t + 1) * 128], pT[:, pb, 512:1024].rearrange("p (c k) -> p c k", k=128),
                       [("pT", pb)], ["QA"])

            b1(0)
            b1(1)
            b2(0)
            b1(2)
            b2(1)
            b1(3)
            b2(2)
            b2(3)

        def sig1_slot(d):
            if d < 6:
                return HT[:, d, :], [("HT", d, 0)]
            return nbuf[:, d - 6, 512:1024], [("nbuf", d - 6)]

        def early_g1():
            nrd = [("nT", i) for i in range(4)]
            for cpair in range(4):
                w1, k1, i1 = W.acquire(wgt_v[:, :, cpair * 256:(cpair + 1) * 256], (128, 8, 256))
                for j in range(2):
                    dchunk = cpair * 2 + j
                    for dc in range(8):
                        S.mm(pY[:, j, :], w1[:, dc, j * 128:(j + 1) * 128], nT[:, dc, 0:512], dc == 0, dc == 7, [k1] + nrd, [("pY", j)])
                    s1, s1k = sig1_slot(dchunk)
                    S.act(s1, pY[:, j, :], AF.Sigmoid, [("pY", j), "bgt"], s1k, bias=bgt[:, dchunk:dchunk + 1])
                W.release(i1)

        ugf = ug[:].rearrange("p a b -> p (a b)").bitcast(F32).rearrange("p (a b) -> p a b", a=2)

        def hoisted_gates(dqs=(0, 1), late=False):
            nrd = [("nT", i) for i in range(4)]
            grd = ["QA"]
            for dq in dqs:
                wb, kb, ib = W.acquire(wbg_v[:, :, dq * 512:(dq + 1) * 512], (128, 4, 512))
                for dp in range(2):
                    cpair = dq * 2 + dp
                    w2, k2, i2 = W.acquire(wgt_v[:, :, D + cpair * 256:D + (cpair + 1) * 256], (128, 8, 256))
                    for j in range(2):
                        dchunk = cpair * 2 + j
                        lo = (dp * 2 + j) * 128
                        for dc in range(8):
                            S.mm(pY[:, j, :], w2[:, dc, j * 128:(j + 1) * 128], nT[:, dc, 0:512], dc == 0, dc == 7, [k2] + nrd, [("pY", j)])
                        bb = nxt("pB")
                        for kc in range(4):
                            S.mm(pB[:, bb, :], wb[:, kc, lo:lo + 128], gmT[:, kc, :], kc == 0, kc == 3, [kb] + grd, [("pB", bb)])
                        if late:
                            stmp, sk_ = ugf[:, j, :], [("ug", 2 * j, 0), ("ug", 2 * j, 1), ("ug", 2 * j + 1, 0), ("ug", 2 * j + 1, 1)]
                        else:
                            stmp, sk_ = sg[:, j, :], [("sg", j)]
                        S.act(stmp, pY[:, j, :], AF.Sigmoid, [("pY", j), "bgt"], sk_, bias=bgt[:, 8 + dchunk:9 + dchunk])
                        S.tt("dve", mrgT[:, dchunk, :], stmp, pB[:, bb, :], ALU.mult, sk_ + [("pB", bb)], ["VGM"])
                    W.release(i2)
                W.release(ib)

        def merge_and_out():
            ard = ["QA"]
            for dq in range(2):
                wa, ka, ia = W.acquire(wba_v[:, :, dq * 512:(dq + 1) * 512], (128, 4, 512))
                for dd in range(4):
                    dchunk = dq * 4 + dd
                    lo = dd * 128
                    ba = nxt("pA")
                    for kc in range(4):
                        S.mm(pA[:, ba, :], wa[:, kc, lo:lo + 128], attnT[:, kc, :], kc == 0, kc == 3, [ka] + ard, [("pA", ba)])
                    s1, s1k = sig1_slot(dchunk)
                    gb = dchunk % 2
                    S.tt("dve", gtmp[:, gb, :], s1, pA[:, ba, :], ALU.mult, s1k + [("pA", ba)], [("gtmp", gb)])
                    S.tt("dve", mrgT[:, dchunk, :], gtmp[:, gb, :], mrgT[:, dchunk, :], ALU.add, [("gtmp", gb)], ["VGM"])
                W.release(ia)
            mrd = ["VGM"]
            for ch in range(4):
                w, kw, iw = W.acquire(wout_v[:, :, ch * 256:(ch + 1) * 256], (128, 8, 256))
                for t in range(4):
                    ba = nxt("pA")
                    for dc in range(8):
                        S.mm(pA[:, ba, 0:256], mrgT[:, dc, t * 128:(t + 1) * 128], w[:, dc, :], dc == 0, dc == 7, [kw] + mrd, [("pA", ba)])
                    S.tt("dve", xh[:, t, ch * 256:(ch + 1) * 256], pA[:, ba, 0:256], xh[:, t, ch * 256:(ch + 1) * 256], ALU.add,
                         [("pA", ba)], [("xh", t)])
                    if ch == 3:
                        rms_stats_tile(t)
                W.release(iw)
            rms_rstd(4, tiles_done=True)

        def final_norm_store(g):
            rms_rstd(4)
            for t in range(4):
                S.stt(xh[:, t, :], xh[:, t, :], stat[:, 3, t:t + 1], normg[:, 3, :], ALU.mult, ALU.mult,
                      [("stat", 3), "normg"], [("xh", t)])
            S.dma("sp", lambda E: E.dma_start(out=out_v[g], in_=xh[:]), "out%d" % g,
                  reads=[("xh", t) for t in range(4)], writes=[("out", g)])

        def load_x(g):
            S.dma("sp", lambda E: E.dma_start(out=xp0, in_=xown_v[g][:, 0:2, :]), "xin0", writes=["VGM"])
            S.dma("sp", lambda E: E.dma_start(out=xp1, in_=xown_v[g][:, 2:4, :]), "xin1", writes=["QA"])

        def body():
            load_x(0)
            rms_rstd(4, src_xpre)
            for g in range(4):
                rms_to_nT(4, 0, src_xpre, do_stats=False)
                if g > 0:
                    final_norm_store(g - 1)
                ffn(4, 0, w1g_v, w1u_v, w1d_v, src=src_xpre, do_norm=False, tail_stats=True)
                rms_to_nT(4, 1, do_stats=False)
                qug_proj(g)
                early_g1()
                gmlp_and_transposes()
                gate_part1(g)
                hoisted_gates((0,))
                attention(g, filler=lambda: hoisted_gates((1,), late=True))
                merge_and_out()
                nxt_hook = None
                if g < 3:
                    load_x(g + 1)
                    nxt_hook = lambda: rms_rstd(4, src_xpre)
                ffn(4, 2, w2g_v, w2u_v, w2d_v, hook=nxt_hook, do_stats=False)
            final_norm_store(3)

        S.dry = True
        body()
        S.dry = False
        for k in cnt:
            cnt[k] = 0
        W.reset()
        setup()
        body()
        S.final_wait("sp", [("out", g) for g in range(4)])
        S.emit()
    return nc


def _tables(p):
    seqblk = OWN[0] + OWN[1]
    past = np.zeros((8, 16), np.float32)
    nb = np.zeros((8, 16), np.float32)
    for s in range(8):
        i = OWN[p][s]
        for t in range(16):
            if seqblk[t] < i:
                past[s, t] = 1.0
            elif t == p * 8 + s:
                nb[s, t] = 1.0e9
            else:
                nb[s, t] = -1.0e9
    past = np.ascontiguousarray(np.broadcast_to(past.reshape(1, 128), (128, 128)))
    nb = np.ascontiguousarray(np.broadcast_to(nb.reshape(1, 128), (128, 128)))
    return past, nb


def make_in_maps(inputs):
    f = lambda a: np.ascontiguousarray(np.asarray(a, dtype=np.float32))
    x = f(inputs["x"])
    rep = lambda v, n: np.ascontiguousarray(np.broadcast_to(f(v).reshape(1, n), (128, n)))
    norms = np.stack([rep(inputs["ffn1_norm"], D), rep(inputs["mix_norm"], D),
                      rep(inputs["ffn2_norm"], D), rep(inputs["final_norm"], D)], axis=1)
    lngb = np.stack([rep(inputs["gmlp_ln_g"], 512), rep(inputs["gmlp_ln_b"], 512)], axis=1)
    ws = f(inputs["gmlp_w_s"]).reshape(8, 128, 128)
    wsT = np.ascontiguousarray(ws.transpose(2, 0, 1))
    jj, ii = np.meshgrid(np.arange(128), np.arange(128), indexing="ij")
    trilT = (jj <= ii).astype(np.float32)
    bsT = np.ascontiguousarray(f(inputs["gmlp_b_s"]).reshape(8, 128).T)
    bgate = np.ascontiguousarray(f(inputs["b_gate"]).reshape(16, 128).T)
    ident = np.eye(128, dtype=np.float32)
    shodd = np.zeros((64, 128), np.float32)
    shodd[np.arange(64), 64 + np.arange(64)] = 1.0
    ej = np.zeros((16, 16, 128), np.float32)
    for j in range(16):
        ej[j, j, :] = 1.0
    ej = ej.reshape(16, 16 * 128)
    cm = np.zeros((128, 2, 256), np.float32)
    for kt in range(2):
        kpos = kt * 128 + np.arange(128)[:, None]
        qpos = np.arange(256)[None, :]
        cm[:, kt, :] = np.where(kpos <= qpos, 0.0, NEG)
    shared = {
        "ffn1_w_gate": f(inputs["ffn1_w_gate"]).reshape(D, FF), "ffn1_w_up": f(inputs["ffn1_w_up"]).reshape(D, FF),
        "ffn1_w_down": f(inputs["ffn1_w_down"]).reshape(FF, D),
        "ffn2_w_gate": f(inputs["ffn2_w_gate"]).reshape(D, FF), "ffn2_w_up": f(inputs["ffn2_w_up"]).reshape(D, FF),
        "ffn2_w_down": f(inputs["ffn2_w_down"]).reshape(FF, D),
        "w_in": f(inputs["w_in"]).reshape(D, INW),
        "w_branch_attn": f(inputs["w_branch_attn"]).reshape(512, D),
        "w_branch_gmlp": f(inputs["w_branch_gmlp"]).reshape(512, D),
        "w_gate": f(inputs["w_gate"]).reshape(D, 2 * D), "w_out": f(inputs["w_out"]).reshape(D, D),
        "norms": np.ascontiguousarray(norms), "lngb": np.ascontiguousarray(lngb), "wsT": wsT, "trilT": trilT,
        "bsT": bsT, "bgate": bgate, "ident": ident, "shodd": shodd, "ej": ej,
    }
    in_maps = []
    for c in range(8):
        b, p = c // 2, c % 2
        xb = x[b].reshape(16, BLK, D)
        m = dict(shared)
        m["x_own"] = np.ascontiguousarray(xb[OWN[p]].reshape(TOK, D))
        cmm = np.zeros((128, 2, 2, 256), np.float32)
        cmm[:, p] = cm
        m["cmask"] = cmm.reshape(128, 1024)
        m["past"], m["nb"] = _tables(p)
        in_maps.append(m)
    return in_maps


def assemble(outs):
    y = np.zeros((BATCH, 16, BLK, D), np.float32)
    for c in range(8):
        b, p = c // 2, c % 2
        y[b, OWN[p]] = np.asarray(outs[c], dtype=np.float32).reshape(8, BLK, D)
    return y.reshape(BATCH, SEQ, D)


_NC = None


def kernel(**inputs):
    global _NC
    if _NC is None:
        _NC = build_program()
    in_maps = make_in_maps(inputs)
    res = run_bass_kernel_spmd(_NC, in_maps, core_ids=list(range(8)))
    return assemble([r["out"] for r in res.results])
```

```python
import numpy as np
from collections import deque
from contextlib import ExitStack
import concourse.bass as bass
import concourse.mybir as mybir
from concourse.bass_utils import run_bass_kernel_spmd

F32 = mybir.dt.float32
BF16 = mybir.dt.bfloat16
AF = mybir.ActivationFunctionType
ALU = mybir.AluOpType
AX = mybir.AxisListType

D = 1024
FF = 2816
NFC = FF // 128
NH = 8
DH = 64
INW = 2560
SEQ = 4096
BATCH = 4
BLK = 256
TOK = 2048
EPS = 1e-6
OWN = [[0, 3, 4, 7, 8, 11, 12, 15], [1, 2, 5, 6, 9, 10, 13, 14]]
NEG = -30000.0
RING = 7
NPT = 4
FPH = [[0, 1, 2], [3, 4, 5], [6, 7, 8], [9, 10]]


class Sched:
    ENG = ("pe", "act", "dve", "pool", "sp")

    def __init__(self, nc, stack):
        self.nc = nc
        self.stack = stack
        self.dry = False
        self.prog = {e: [] for e in self.ENG}
        self.sem = {}
        self.cnt = {}
        for e in ("pe", "act", "dve", "pool"):
            self.sem[e] = stack.enter_context(nc.semaphore("sem_" + e))
            self.cnt[e] = 0
        self.waited = {}
        self.last_w = {}
        self.readers = {}
        self.dma_sems = {}

    def dma_sem(self, name):
        if name not in self.dma_sems:
            s = self.stack.enter_context(self.nc.semaphore("dsem_" + name))
            self.dma_sems[name] = [s, 0]
            self.sem[name] = s
        return self.dma_sems[name]

    def _wait(self, eng, semkey, value):
        if value <= 0:
            return
        if eng == "pe" and semkey == "pe":
            return
        k = (eng, semkey)
        if self.waited.get(k, 0) >= value:
            return
        self.waited[k] = value
        s = self.sem[semkey]
        self.prog[eng].append(lambda E: E.wait_ge(s, value))

    def _deps(self, eng, reads, writes):
        for r in reads:
            lw = self.last_w.get(r)
            if lw is not None:
                self._wait(eng, lw[0], lw[1])
        for w in writes:
            lw = self.last_w.get(w)
            if lw is not None:
                self._wait(eng, lw[0], lw[1])
            for rd in self.readers.get(w, ()):
                self._wait(eng, rd[0], rd[1])

    def _record(self, token, reads, writes):
        for w in writes:
            self.last_w[w] = token
            self.readers[w] = []
        for r in reads:
            if r in writes:
                continue
            lst = self.readers.setdefault(r, [])
            if not lst or lst[-1] != token:
                lst.append(token)
                if len(lst) > 24:
                    best = {}
                    for t in lst:
                        if t[0] not in best or best[t[0]][1] < t[1]:
                            best[t[0]] = t
                    lst[:] = list(best.values())

    PSUM_KEYS = ("pA", "pB", "pY", "pT")

    def op(self, eng, fn, reads=(), writes=(), mark=True):
        if self.dry:
            return
        pr = [r for r in reads if isinstance(r, tuple) and r[0] in self.PSUM_KEYS and r not in writes]
        if pr:
            reads = [r for r in reads if r not in pr]
            writes = list(writes) + pr
        self._deps(eng, reads, writes)
        if mark:
            self.cnt[eng] += 1
            s = self.sem[eng]
            self.prog[eng].append(lambda E: fn(E).then_inc(s, 1))
            tok = (eng, self.cnt[eng])
        else:
            self.prog[eng].append(fn)
            tok = (eng, self.cnt[eng] + 1)
        self._record(tok, reads, writes)

    def dma(self, queue, fn, semname, reads=(), writes=()):
        if self.dry:
            return
        self._deps(queue, reads, writes)
        ds = self.dma_sem(semname)
        ds[1] += 16
        s = ds[0]
        self.prog[queue].append(lambda E: fn(E).then_inc(s, 16))
        self._record((semname, ds[1]), reads, writes)

    def coll(self, fn, semname, reads=(), writes=()):
        if self.dry:
            return
        self._deps("pool", reads, writes)
        ds = self.dma_sem(semname)
        ds[1] += 1
        s = ds[0]
        self.prog["pool"].append(lambda E: fn(E).then_inc(s, 1))
        self._record((semname, ds[1]), reads, writes)

    def barrier(self):
        if self.dry:
            return
        toks = {}
        for d in (self.last_w,):
            for t in d.values():
                if t[0] not in toks or toks[t[0]] < t[1]:
                    toks[t[0]] = t[1]
        for lst in self.readers.values():
            for t in lst:
                if t[0] not in toks or toks[t[0]] < t[1]:
                    toks[t[0]] = t[1]
        for e in ("pe", "act", "dve", "pool", "sp"):
            for k, v in toks.items():
                if k == "pe" and v > self.cnt["pe"]:
                    continue
                self._wait(e, k, v)

    def final_wait(self, eng, regions):
        for r in regions:
            lw = self.last_w.get(r)
            if lw is not None:
                self._wait(eng, lw[0], lw[1])

    def mm(self, out, lhsT, rhs, start, stop, reads, writes, mark=None):
        if mark is None:
            mark = stop
        self.op("pe", lambda E: E.matmul(out, lhsT=lhsT, rhs=rhs, start=start, stop=stop),
                reads, writes, mark)

    def tr(self, out, in_, ident, reads, writes, mark=True):
        self.op("pe", lambda E: E.transpose(out=out, in_=in_, identity=ident), reads, writes, mark)

    def act(self, out, in_, func, reads, writes, bias=None, scale=None, accum_out=None):
        kw = {}
        if bias is not None:
            kw["bias"] = bias
        if scale is not None:
            kw["scale"] = scale
        if accum_out is not None:
            kw["accum_out"] = accum_out
        self.op("act", lambda E: E.activation(out=out, in_=in_, func=func, **kw), reads, writes)

    def tt(self, eng, out, in0, in1, op, reads, writes):
        self.op(eng, lambda E: E.tensor_tensor(out=out, in0=in0, in1=in1, op=op), reads, writes)

    def ts(self, eng, out, in0, s1, s2, op0, op1, reads, writes):
        if op1 is None:
            self.op(eng, lambda E: E.tensor_scalar(out=out, in0=in0, scalar1=s1, scalar2=None, op0=op0),
                    reads, writes)
        else:
            self.op(eng, lambda E: E.tensor_scalar(out=out, in0=in0, scalar1=s1, scalar2=s2, op0=op0, op1=op1),
                    reads, writes)

    def stt(self, out, in0, scalar, in1, op0, op1, reads, writes):
        self.op("dve", lambda E: E.scalar_tensor_tensor(out=out, in0=in0, scalar=scalar, in1=in1, op0=op0, op1=op1),
                reads, writes)

    def copy(self, eng, out, in_, reads, writes):
        if eng == "act":
            self.op("act", lambda E: E.copy(out=out, in_=in_), reads, writes)
        else:
            self.op(eng, lambda E: E.tensor_copy(out=out, in_=in_), reads, writes)

    def emit(self):
        nc = self.nc
        prog = self.prog
        with nc.Block() as block:
            @block.tensor
            def _(E):
                for f in prog["pe"]:
                    f(E)

            @block.scalar
            def _(E):
                for f in prog["act"]:
                    f(E)

            @block.vector
            def _(E):
                for f in prog["dve"]:
                    f(E)

            @block.gpsimd
            def _(E):
                for f in prog["pool"]:
                    f(E)

            @block.sync
            def _(E):
                for f in prog["sp"]:
                    f(E)


class WStream:
    def __init__(self, S, ring, R):
        self.S = S
        self.ring = ring
        self.R = R
        self.plan = []
        self.reset()

    def reset(self):
        self.idx = 0
        self.next_load = 0
        self.live = deque()

    def view(self, slot, shape):
        P, a, b = shape
        return self.ring[0:P, slot, 0:a * b].rearrange("p (a b) -> p a b", a=a)

    def acquire(self, src, shape):
        if self.S.dry:
            self.plan.append((src, shape))
            return self.view(0, shape), ("wr", 0), -1
        i = self.idx
        self.idx += 1
        assert self.plan[i][1] == shape
        self.live.append(i)
        self.pump()
        assert i < self.next_load
        return self.view(i % self.R, shape), ("wr", i % self.R), i

    def release(self, i):
        if self.S.dry:
            return
        self.live.remove(i)
        self.pump()

    def pump(self):
        oldest = self.live[0] if self.live else self.idx
        while self.next_load < len(self.plan) and self.next_load < oldest + self.R:
            j = self.next_load
            slot = j % self.R
            src, shape = self.plan[j]
            dst = self.view(slot, shape)
            self.S.dma("pool", lambda E, dst=dst, src=src: E.dma_start(out=dst, in_=src),
                       "wr%d" % slot, writes=[("wr", slot)])
            self.next_load += 1


def build_program(NPAIR=4):
    nc = bass.Bass("TRN2", target_bir_lowering=False)

    def din(name, shape):
        return nc.dram_tensor(name, list(shape), F32, kind="ExternalInput").ap()

    x_own = din("x_own", [TOK, D])
    w1g = din("ffn1_w_gate", [D, FF]); w1u = din("ffn1_w_up", [D, FF]); w1d = din("ffn1_w_down", [FF, D])
    w2g = din("ffn2_w_gate", [D, FF]); w2u = din("ffn2_w_up", [D, FF]); w2d = din("ffn2_w_down", [FF, D])
    w_in = din("w_in", [D, INW])
    w_ba = din("w_branch_attn", [512, D]); w_bg = din("w_branch_gmlp", [512, D])
    w_gt = din("w_gate", [D, 2 * D]); w_out = din("w_out", [D, D])
    norms = din("norms", [128, 4, D])
    lngb = din("lngb", [128, 2, 512])
    wsT = din("wsT", [128, 8, 128])
    trilT = din("trilT", [128, 128])
    bsT = din("bsT", [128, 8])
    bgate = din("bgate", [128, 16])
    ident_d = din("ident", [128, 128])
    shodd_d = din("shodd", [64, 128])
    ej_d = din("ej", [16, 16 * 128])
    cmask_d = din("cmask", [128, 2 * 2 * 256])
    past_d = din("past", [128, 8 * 16])
    nb_d = din("nb", [128, 8 * 16])
    out_d = nc.dram_tensor("out", [TOK, D], F32, kind="ExternalOutput").ap()

    def wview(w):
        return w.rearrange("(dc p) f -> p dc f", p=128)

    w1g_v, w1u_v, w2g_v, w2u_v = wview(w1g), wview(w1u), wview(w2g), wview(w2u)
    w1d_v = w1d.rearrange("(fc p) d -> p fc d", p=128)
    w2d_v = w2d.rearrange("(fc p) d -> p fc d", p=128)
    win_v = wview(w_in)
    wgt_v = wview(w_gt)
    wout_v = wview(w_out)
    wba_v = w_ba.rearrange("(kc p) d -> p kc d", p=128)
    wbg_v = w_bg.rearrange("(kc p) d -> p kc d", p=128)
    xown_v = x_own.rearrange("(g t p) d -> g p t d", p=128, t=4)
    NGRP = [[2 * i, 2 * i + 1] for i in range(NPAIR)]
    xsrc = [nc.dram_tensor("xsrc%d" % g, [128, 4096], BF16) for g in range(4)]
    xdst = [nc.dram_tensor("xdst%d" % g, [256, 4096], BF16) for g in range(4)]
    msrc = [nc.dram_tensor("msrc%d" % g, [128, 8], F32) for g in range(4)]
    mdst = [nc.dram_tensor("mdst%d" % g, [256, 8], F32) for g in range(4)]
    out_v = out_d.rearrange("(g t p) d -> g p t d", p=128, t=4)

    with ExitStack() as st:
        S = Sched(nc, st)

        def sb(name, shape, dt):
            return st.enter_context(nc.sbuf_tensor(name, list(shape), dt))

        def ps(name, shape, dt):
            return st.enter_context(nc.psum_tensor(name, list(shape), dt))

        ring = sb("wring", [128, RING, 2048], BF16)
        W = WStream(S, ring, RING)
        KT = sb("KT", [128, 4, SEQ], BF16)
        VA = sb("VA", [128, 32, NH, DH + 1], BF16)
        kmT = sb("kmT", [128, 4, 16], F32)
        normg = sb("normg", [128, 4, D], F32)
        lng = sb("lng", [128, 2, 512], F32)
        wsm = sb("wsm", [128, 8, 128], BF16)
        trl = sb("trl", [128, 128], BF16)
        bst = sb("bst", [128, 8], F32)
        bgt = sb("bgt", [128, 16], F32)
        idb = sb("idb", [128, 128], BF16)
        ejb = sb("ejb", [128, 16, 128], BF16)
        cmb = sb("cmb", [128, 2, 2, 256], BF16)
        kms = sb("kms", [128, 4, 2], F32)
        pastt = sb("pastt", [128, 8, 16], F32)
        nbt = sb("nbt", [128, 8, 16], F32)
        xh = sb("xh", [128, 4, D], F32)
        nT = sb("nT", [128, 8, 512], BF16)
        HT = sb("HT", [128, 6, 512], BF16)
        nbuf = sb("nbuf", [128, 2, D], BF16)
        sg = sb("sg", [128, 2, 512], F32)
        st6 = sb("st6", [128, 12], F32)
        mv = sb("mv", [128, 8, 2], F32)
        kmv = sb("kmv", [128, 2], F32)
        stat = sb("stat", [128, 4, 8], F32)
        QTz = sb("QTz", [128, 8, 512], BF16)
        kmz = sb("kmz", [128, 8, 16], BF16)
        qa = sb("qa", [128, 8, 512], BF16)
        QTlo = qa[:, 0:4, :]
        kmh = sb("kmh", [128, 2, 4, 16], BF16)
        kmd = sb("kmd", [128, 4, 16], F32)
        attnT = qa[:, 0:4, :]
        gmT = qa[:, 4:8, :]
        ug = sb("ug", [128, 4, 512], BF16)
        vgm = sb("vgm", [128, 8, 512], BF16)
        vg = vgm[:].rearrange("p a b -> p (a b)").bitcast(F32).rearrange("p (a b) -> p a b", a=4)
        mrgT = vgm
        gtmp = sb("gtmp", [128, 2, 512], F32)
        gmb = sb("gmb", [128, 8, 16], F32)
        mx8 = sb("mx8", [128, 8, 8], F32)
        thr = sb("thr", [128, 8], F32)
        biasq = sb("biasq", [128, 4, 8, 16], BF16)
        biasT = sb("biasT", [128, 8, 512], BF16)
        PT = sb("PT", [128, NPT, 512], BF16)
        shodd = sb("shodd_sb", [128, 128], BF16)
        ones_b = sb("ones_b", [128, 64], BF16)
        gm = PT
        lnst = sb("lnst", [128, 8], F32)
        lnmv4 = sb("lnmv4", [128, 4, 4], F32)
        pA = ps("pA", [128, 2, 512], F32)
        pB = ps("pB", [128, 2, 512], F32)
        pY = ps("pY", [128, 2, 512], F32)
        pT = ps("pT", [128, 2, 1024], BF16)
        cnt = {"pA": 0, "pB": 0, "pY": 0, "pT": 0}

        def nxt(name):
            cnt[name] += 1
            return cnt[name] % 2

        def setup():
            S.dma("sp", lambda E: E.dma_start(out=normg[:], in_=norms), "c0", writes=["normg"])
            S.dma("sp", lambda E: E.dma_start(out=lng[:], in_=lngb), "c1", writes=["lng"])
            S.dma("pool", lambda E: E.dma_start(out=wsm[:], in_=wsT), "c2", writes=["wsm"])
            S.dma("pool", lambda E: E.dma_start(out=trl[:], in_=trilT), "c3", writes=["trl"])
            S.dma("sp", lambda E: E.dma_start(out=bst[:], in_=bsT), "c4", writes=["bst"])
            S.dma("sp", lambda E: E.dma_start(out=bgt[:], in_=bgate), "c5", writes=["bgt"])
            S.dma("pool", lambda E: E.dma_start(out=idb[:], in_=ident_d), "c6", writes=["idb"])
            S.dma("pool", lambda E: E.dma_start(out=shodd[0:64, :], in_=shodd_d), "c11", writes=["shodd"])
            S.op("dve", lambda E: E.memset(ones_b[:], 1.0), [], ["ones_b"])
            S.op("dve", lambda E: E.memset(ejb[:], 0.0), [], ["ejb"])
            S.op("dve", lambda E: E.memset(biasT[:], 0.0), [], [("biasT", hp) for hp in range(4)])
            S.op("dve", lambda E: E.memset(QTz[:], 0.0), [], [("QTz", pc) for pc in range(4)])
            S.op("dve", lambda E: E.memset(kmz[:], 0.0), [], ["kmz"])
            S.dma("pool", lambda E: E.dma_start(out=ejb[0:16, :, :].rearrange("p a b -> p (a b)"), in_=ej_d), "c7", writes=["ejb"])
            S.dma("pool", lambda E: E.dma_start(out=cmb[:].rearrange("p m a b -> p (m a b)"), in_=cmask_d), "c8", writes=["cmb"])
            S.dma("sp", lambda E: E.dma_start(out=pastt[:].rearrange("p a b -> p (a b)"), in_=past_d), "c9", writes=["pastt"])
            S.dma("sp", lambda E: E.dma_start(out=nbt[:].rearrange("p a b -> p (a b)"), in_=nb_d), "c10", writes=["nbt"])
            S.tt("dve", wsm[:], wsm[:], trl[:, None, :].to_broadcast([128, 8, 128]), ALU.mult, ["trl"], ["wsm"])
            S.op("dve", lambda E: E.memset(VA[:, :, :, DH:DH + 1], 1.0), [], ["VAones"])
            S.op("dve", lambda E: E.memset(kmT[:], 0.0), [], [("kmT", pc, i) for pc in range(4) for i in range(8)])

        def src_xh(t):
            return xh[:, t, :], ("xh", t)

        xp0 = vgm[:].rearrange("p a b -> p (a b)").bitcast(F32).rearrange("p (t d) -> p t d", t=2)
        xp1 = qa[:].rearrange("p a b -> p (a b)").bitcast(F32).rearrange("p (t d) -> p t d", t=2)

        def src_xpre(t):
            return (xp0[:, t, :], "VGM") if t < 2 else (xp1[:, t - 2, :], "QA")

        def rms_stats_tile(t, src=src_xh):
            xa, xk = src(t)
            S.op("dve", lambda E, xa=xa: E.bn_stats(out=st6[:, 0:6], in_=xa[:, 0:512]), [xk], ["st6a"])
            S.op("dve", lambda E, xa=xa: E.bn_stats(out=st6[:, 6:12], in_=xa[:, 512:1024]), [xk], ["st6b"])
            S.op("dve", lambda E, t=t: E.bn_aggr(out=mv[:, t, :], in_=st6[:, 0:12]), ["st6a", "st6b"], [("mv", t)])

        def rms_rstd(NT, src=src_xh, tiles_done=False):
            if not tiles_done:
                for t in range(NT):
                    rms_stats_tile(t, src)
            rd = [("mv", t) for t in range(NT)]
            S.tt("dve", stat[:, 0, 0:NT], mv[:, 0:NT, 0], mv[:, 0:NT, 0], ALU.mult, rd, [("stat", 0)])
            S.tt("dve", stat[:, 1, 0:NT], stat[:, 0, 0:NT], mv[:, 0:NT, 1], ALU.add, rd + [("stat", 0)], [("stat", 1)])
            S.ts("dve", stat[:, 1, 0:NT], stat[:, 1, 0:NT], EPS, None, ALU.add, None, [("stat", 1)], [("stat", 1)])
            S.act(stat[:, 2, 0:NT], stat[:, 1, 0:NT], AF.Sqrt, [("stat", 1)], [("stat", 2)])
            S.op("dve", lambda E: E.reciprocal(out=stat[:, 3, 0:NT], in_=stat[:, 2, 0:NT]), [("stat", 2)], [("stat", 3)])

        def rms_to_nT(NT, which, src=src_xh, do_stats=True):
            if do_stats:
                rms_rstd(NT, src)
            for t in range(NT):
                b = t % 2
                xa, xk = src(t)
                S.stt(nbuf[:, b, :], xa, stat[:, 3, t:t + 1], normg[:, which, :], ALU.mult, ALU.mult,
                      [xk, ("stat", 3), "normg"], [("nbuf", b)])
                pb = nxt("pT")
                for dc in range(8):
                    S.tr(pT[:, pb, dc * 128:(dc + 1) * 128], nbuf[:, b, dc * 128:(dc + 1) * 128], idb[:],
                         [("nbuf", b), "idb"], [("pT", pb)], mark=(dc == 7))
                S.copy("act", nT[:, :, t * 128:(t + 1) * 128], pT[:, pb, :].rearrange("p (dc k) -> p dc k", k=128),
                       [("pT", pb)], [("nT", t)])

        def ffn(NT, which, wg_v, wu_v, wd_v, src=src_xh, do_norm=True, hook=None, do_stats=True, tail_stats=False):
            if do_norm:
                rms_to_nT(NT, which, src, do_stats=do_stats)
            nhalf = NT // 4
            for phi, ph in enumerate(FPH):
                if hook is not None and phi == len(FPH) - 1:
                    hook()
                for pi, c in enumerate(ph):
                    wg, kg, ig = W.acquire(wg_v[:, :, c * 256:(c + 1) * 256], (128, 8, 256))
                    wu, ku, iu = W.acquire(wu_v[:, :, c * 256:(c + 1) * 256], (128, 8, 256))
                    for half in range(nhalf):
                        nrd = [("nT", half * 4 + i) for i in range(4)]
                        for j in range(2):
                            fl = pi * 2 + j
                            ba = nxt("pA")
                            for dc in range(8):
                                S.mm(pA[:, ba, :], wg[:, dc, j * 128:(j + 1) * 128], nT[:, dc, half * 512:(half + 1) * 512],
                                     dc == 0, dc == 7, [kg] + nrd, [("pA", ba)])
                            bb = nxt("pB")
                            for dc in range(8):
                                S.mm(pB[:, bb, :], wu[:, dc, j * 128:(j + 1) * 128], nT[:, dc, half * 512:(half + 1) * 512],
                                     dc == 0, dc == 7, [ku] + nrd, [("pB", bb)])
                            S.act(sg[:, ba, :], pA[:, ba, :], AF.Silu, [("pA", ba)], [("sg", ba)])
                            S.tt("dve", HT[:, fl, half * 512:(half + 1) * 512], sg[:, ba, :], pB[:, bb, :], ALU.mult,
                                 [("sg", ba), ("pB", bb)], [("HT", fl, half)])
                    W.release(ig)
                    W.release(iu)
                wds = [W.acquire(wd_v[:, 2 * c:2 * c + 2, :], (128, 2, 1024)) for c in ph]
                nf = 2 * len(ph)
                for t in range(NT):
                    for hf in range(2):
                        by = nxt("pY")
                        for fl in range(nf):
                            wd, kd, _ = wds[fl // 2]
                            S.mm(pY[:, by, :], HT[:, fl, t * 128:(t + 1) * 128], wd[:, fl % 2, hf * 512:(hf + 1) * 512],
                                 fl == 0, fl == nf - 1, [kd, ("HT", fl, t // 4)], [("pY", by)])
                        if phi == 0:
                            xa, xk = src(t)
                        else:
                            xa, xk = src_xh(t)
                        S.stt(xh[:, t, hf * 512:(hf + 1) * 512], pY[:, by, :], 0.5, xa[:, hf * 512:(hf + 1) * 512],
                              ALU.mult, ALU.add, [("pY", by), xk], [("xh", t)])
                    if tail_stats and phi == len(FPH) - 1:
                        rms_stats_tile(t)
                for _, _, i in wds:
                    W.release(i)
            if tail_stats:
                rms_rstd(NT, tiles_done=True)

        def kv_proj(g):
            kst = gtmp[:].rearrange("p a b -> p (a b)").bitcast(BF16).rearrange("p (a b) -> p a b", a=4)
            vst = sg[:].rearrange("p a b -> p (a b)").bitcast(BF16).rearrange("p (t h d) -> p t h d", t=4, h=8)
            nrd = [("nT", i) for i in range(4)]
            for c in (2, 3):
                w, kw, iw = W.acquire(win_v[:, :, c * 256:(c + 1) * 256], (128, 8, 256))
                for j in range(2):
                    pc = (c - 2) * 2 + j
                    ba = nxt("pA")
                    for dc in range(8):
                        S.mm(pA[:, ba, :], w[:, dc, j * 128:(j + 1) * 128], nT[:, dc, 0:512],
                             dc == 0, dc == 7, [kw] + nrd, [("pA", ba)])
                    S.copy("act", kst[:, pc, :], pA[:, ba, :], [("pA", ba)], [("gtmp", pc // 2)])
                    for bl in range(2):
                        S.op("dve", lambda E, ba=ba, bl=bl: E.bn_stats(out=st6[:, 0:6], in_=pA[:, ba, bl * 256:(bl + 1) * 256]),
                             [("pA", ba)], ["st6a"])
                        S.op("dve", lambda E: E.bn_aggr(out=kmv[:, 0:2], in_=st6[:, 0:6]), ["st6a"], ["kmv"])
                        S.copy("dve", kms[:, pc, bl:bl + 1], kmv[:, 0:1], ["kmv"], ["kms"])
                W.release(iw)
            S.dma("sp", lambda E: E.dma_start(out=msrc[g][:, :], in_=kms[:].rearrange("p a b -> p (a b)")), "stm%d" % g,
                  reads=["kms"], writes=[("msrc", g)])
            S.coll(lambda E: E.collective_compute("AllGather", ALU.bypass, replica_groups=NGRP,
                                                  ins=[msrc[g].ap().opt()], outs=[mdst[g].ap().opt()]),
                   "ccm%d" % g, reads=[("msrc", g)], writes=[("mdst", g)])
            for c in (4, 5):
                w, kw, iw = W.acquire(win_v[:, :, c * 256:(c + 1) * 256], (128, 8, 256))
                for t in range(4):
                    by = nxt("pY")
                    for dc in range(8):
                        S.mm(pY[:, by, 0:256], nT[:, dc, t * 128:(t + 1) * 128], w[:, dc, :],
                             dc == 0, dc == 7, [kw, ("nT", t)], [("pY", by)])
                    h0 = (c - 4) * 4
                    S.copy("act", vst[:, t, h0:h0 + 4, :], pY[:, by, 0:256].rearrange("p (h d) -> p h d", d=DH),
                           [("pY", by)], [("sg", t // 2)])
                W.release(iw)
            S.dma("sp", lambda E: E.dma_start(out=xsrc[g][:, 0:2048], in_=kst.rearrange("p a b -> p (a b)")), "stk%d" % g,
                  reads=[("gtmp", 0), ("gtmp", 1)], writes=[("xsrc", g, 0)])
            S.dma("sp", lambda E: E.dma_start(out=xsrc[g][:, 2048:4096], in_=vst.rearrange("p t h d -> p (t h d)")), "stv%d" % g,
                  reads=[("sg", 0), ("sg", 1)], writes=[("xsrc", g, 1)])
            for m in range(2):
                tb = m * 8 + 2 * g
                S.dma("sp", lambda E, m=m, tb=tb: E.dma_start(
                    out=kmT[:, :, tb:tb + 2],
                    in_=mdst[g][m * 128:(m + 1) * 128, :].rearrange("p (a b) -> p a b", a=4)),
                    "rbm%d%d" % (g, m), reads=[("mdst", g)], writes=[("kmT", pc, m * 4 + g) for pc in range(4)])
            S.coll(lambda E: E.collective_compute("AllGather", ALU.bypass, replica_groups=NGRP,
                                                  ins=[xsrc[g].ap().opt()], outs=[xdst[g].ap().opt()]),
                   "cck%d" % g, reads=[("xsrc", g, 0), ("xsrc", g, 1)], writes=[("xdst", g)])
            for m in range(2):
                tb = m * 8 + 2 * g
                S.dma("sp", lambda E, m=m, tb=tb: E.dma_start(
                    out=KT[:, :, tb * 256:tb * 256 + 512],
                    in_=xdst[g][m * 128:(m + 1) * 128, 0:2048].rearrange("p (a b) -> p a b", a=4)),
                    "rbk%d%d" % (g, m), reads=[("xdst", g)], writes=[("KT", pc, m * 4 + g) for pc in range(4)])
                S.dma("sp", lambda E, m=m, tb=tb: E.dma_start(
                    out=VA[:, tb * 2:tb * 2 + 4, :, 0:DH],
                    in_=xdst[g][m * 128:(m + 1) * 128, 2048:4096].rearrange("p (t h d) -> p t h d", t=4, h=8)),
                    "rbv%d%d" % (g, m), reads=[("xdst", g)],
                    writes=[("VA", tb * 2 + i, hh) for i in range(4) for hh in range(2)])

        def gelu_from_psum(src, rkey, dst, wkey):
            S.act(dst, src, AF.Gelu_apprx_tanh, [rkey], [wkey])

        def qug_proj(g):
            kv_proj(g)
            for c in (0, 1):
                w, kw, iw = W.acquire(win_v[:, :, c * 256:(c + 1) * 256], (128, 8, 256))
                nrd = [("nT", i) for i in range(4)]
                for j in range(2):
                    pc = c * 2 + j
                    ba = nxt("pA")
                    for dc in range(8):
                        S.mm(pA[:, ba, :], w[:, dc, j * 128:(j + 1) * 128], nT[:, dc, 0:512],
                             dc == 0, dc == 7, [kw] + nrd, [("pA", ba)])
                    S.op("act", lambda E, pc=pc, ba=ba: E.mul(out=QTz[0:64, 2 * pc, :], in_=pA[0:64, ba, :], mul=0.125),
                         [("pA", ba)], [("QTz", pc)])
                    S.op("act", lambda E, pc=pc, ba=ba: E.mul(out=QTz[64:128, 2 * pc + 1, :], in_=pA[64:128, ba, :], mul=0.125),
                         [("pA", ba)], [("QTz", pc)])
                    S.stt(QTlo[0:64, pc, :], pA[0:64, ba, :], 0.125, QTz[0:64, 2 * pc, :], ALU.mult, ALU.subtract,
                          [("pA", ba), ("QTz", pc)], ["QA"])
                    S.stt(QTlo[64:128, pc, :], pA[64:128, ba, :], 0.125, QTz[64:128, 2 * pc + 1, :], ALU.mult, ALU.subtract,
                          [("pA", ba), ("QTz", pc)], ["QA"])
                W.release(iw)
            for c in (6, 7, 8, 9):
                w, kw, iw = W.acquire(win_v[:, :, c * 256:(c + 1) * 256], (128, 8, 256))
                for t in range(4):
                    by = nxt("pY")
                    for dc in range(8):
                        S.mm(pY[:, by, 0:256], nT[:, dc, t * 128:(t + 1) * 128], w[:, dc, :],
                             dc == 0, dc == 7, [kw, ("nT", t)], [("pY", by)])
                    if c < 8:
                        dst = ug[:, t, (c - 6) * 256:(c - 5) * 256]
                        wk = ("ug", t, c - 6)
                    else:
                        dst = vg[:, t, (c - 8) * 256:(c - 7) * 256]
                        wk = "VGM"
                    gelu_from_psum(pY[:, by, 0:256], ("pY", by), dst, wk)
                W.release(iw)

        def gate_part1(g):
            kmk = [("kmT", pc, i) for pc in range(4) for i in range(8)]
            S.copy("dve", kmh[:, 0, :, :], kmT[:], kmk, ["kmh0"])
            S.tt("dve", kmd[:], kmT[:], kmh[:, 0, :, :], ALU.subtract, kmk + ["kmh0"], ["kmd"])
            S.copy("dve", kmh[:, 1, :, :], kmd[:], ["kmd"], ["kmh1"])
            for pc in range(4):
                S.copy("dve", kmz[0:64, 2 * pc, :], kmh[0:64, 0, pc, :], ["kmh0"], ["kmz"])
                S.copy("dve", kmz[64:128, 2 * pc + 1, :], kmh[64:128, 0, pc, :], ["kmh0"], ["kmz"])
            for sub in range(4):
                s = 2 * g + sub // 2
                ba = nxt("pA")
                for h in range(NH):
                    pc = h // 2
                    qs = slice(sub * 128, (sub + 1) * 128)
                    terms = [(QTz[:, h, qs], kmh[:, 0, pc, :]), (QTz[:, h, qs], kmh[:, 1, pc, :]), (QTlo[:, pc, qs], kmz[:, h, :])]
                    for ti, (qq, kk) in enumerate(terms):
                        S.mm(pA[:, ba, h * 16:(h + 1) * 16], qq, kk,
                             ti == 0, ti == 2, [("QTz", pc), "QA", "kmh0", "kmh1", "kmz"], [("pA", ba)],
                             mark=(h == NH - 1 and ti == 2))
                S.tt("dve", gmb[:], pA[:, ba, 0:128].rearrange("p (h t) -> p h t", t=16),
                     pastt[:, s:s + 1, :].to_broadcast([128, 8, 16]), ALU.mult, [("pA", ba), "pastt"], ["gmb"])
                S.tt("dve", gmb[:], gmb[:], nbt[:, s:s + 1, :].to_broadcast([128, 8, 16]), ALU.add, ["gmb", "nbt"], ["gmb"])
                for h in range(NH):
                    S.op("dve", lambda E, h=h: E.max(out=mx8[:, h, :], in_=gmb[:, h, :]), ["gmb"], [("mx8", h)])
                S.ts("dve", thr[:], mx8[:, :, 3], -1.0e8, None, ALU.max, None, [("mx8", h) for h in range(NH)], ["thr"])
                S.tt("dve", gmb[:], gmb[:], thr[:, :, None].to_broadcast([128, 8, 16]), ALU.is_ge, ["gmb", "thr"], ["gmb"])
                S.ts("dve", biasq[:, sub, :, :], gmb[:], -NEG, NEG, ALU.mult, ALU.add, ["gmb"], [("biasq", sub)])
        def attention(g, filler=None):
            for hp in range(4):
                pb = nxt("pT")
                for hh in range(2):
                    h = hp * 2 + hh
                    for sub in range(4):
                        S.tr(pT[0:16, pb, hh * 512 + sub * 128: hh * 512 + (sub + 1) * 128], biasq[:, sub, h, :], idb[:],
                             [("biasq", sub), "idb"], [("pT", pb)], mark=(hh == 1 and sub == 3))
                S.copy("dve", biasT[0:16, hp * 2:hp * 2 + 2, :], pT[0:16, pb, :].rearrange("p (h q) -> p h q", q=512),
                       [("pT", pb)], [("biasT", hp)])
            nblk = 2 * g + 2
            visits = [(m * 8 + t, kt) for t in range(2 * g) for m in range(2) for kt in range(2)] + \
                     [(m * 8 + t, kt) for t in (2 * g, 2 * g + 1) for m in range(2) for kt in range(2)]
            nv = len(visits)
            sbufs = [(pA, 0), (pA, 1), (pB, 0), (pB, 1)]
            skeys = [("pA", 0), ("pA", 1), ("pB", 0), ("pB", 1)]
            bc = pT[:, 1, :].bitcast(F32)
            pk = pT[:, 0, :].bitcast(F32)
            recf = gtmp[:, 0, :]
            rhl = gtmp[:, 1, :].bitcast(BF16)
            ah = nbuf[:, :, 0:512]
            st_ = {"si": 0, "pti": 0}

            def epi_a(h):
                hb = h % 2
                S.copy("act", sg[0:65, hb, :], pY[0:65, hb, :], [("pY", hb)], [("sg", hb)])
                S.op("dve", lambda E: E.reciprocal(out=recf[64:65, :], in_=sg[64:65, hb, :]), [("sg", hb)], [("gtmp", 0)])
                S.copy("dve", rhl[64:65, 0:512], recf[64:65, :], [("gtmp", 0)], [("gtmp", 1)])
                S.tt("dve", recf[64:65, :], recf[64:65, :], rhl[64:65, 0:512], ALU.subtract, [("gtmp", 1)], [("gtmp", 0)])
                S.copy("dve", rhl[64:65, 512:1024], recf[64:65, :], [("gtmp", 0)], [("gtmp", 1)])

            def epi_b(h):
                hb = h % 2
                S.mm(bc[0:64, :], ones_b[64:65, :], rhl[64:65, 0:512], True, False, ["ones_b", ("gtmp", 1)], [("pT", 1)], mark=False)
                S.mm(bc[0:64, :], ones_b[64:65, :], rhl[64:65, 512:1024], False, True, ["ones_b", ("gtmp", 1)], [("pT", 1)], mark=True)
                S.tt("dve", ah[0:64, hb, :], sg[0:64, hb, :], bc[0:64, :], ALU.mult, [("sg", hb), ("pT", 1)], [("nbuf", hb)])

            def epi_c(h):
                hb = h % 2
                if hb == 0:
                    S.mm(pk, idb[0:64, :], ah[0:64, 0, :], True, False, ["idb", ("nbuf", 0)], [("pT", 0)], mark=True)
                else:
                    S.mm(pk, shodd[0:64, :], ah[0:64, 1, :], False, True, ["shodd", ("nbuf", 1)], [("pT", 0)], mark=True)
                    S.copy("act", attnT[:, h // 2, :], pk, [("pT", 0)], ["QA"])

            items = [(h, vi, t, kt) for h in range(NH) for vi, (t, kt) in enumerate(visits)]
            n_it = len(items)
            LA = 3
            pending = []

            def cols(t):
                return slice(256, 512) if t in (2 * g + 1, 8 + 2 * g + 1) else slice(0, 512)

            def score(i):
                h, vi, t, kt = items[i]
                pc = h // 2
                k0 = t * 256 + kt * 128
                ten, bi = sbufs[i % 4]
                sk = skeys[i % 4]
                cs = cols(t)
                diag = (t % 8) in (2 * g, 2 * g + 1)
                S.mm(ten[:, bi, cs], KT[:, pc, k0:k0 + 128], QTz[:, h, cs], True, False,
                     [("KT", pc, k0 // 512), ("QTz", pc)], [sk], mark=False)
                S.mm(ten[:, bi, cs], ejb[:, t, :], biasT[:, h, cs], False, not diag,
                     ["ejb", ("biasT", h // 2)], [sk], mark=(not diag))
                if diag:
                    qo = (t % 8 - 2 * g) * 256
                    S.mm(ten[:, bi, qo:qo + 256], idb[:], cmb[:, t // 8, kt, :], False, True, ["idb", "cmb"], [sk], mark=True)
                S.act(PT[:, i % NPT, cs], ten[:, bi, cs], AF.Exp, [sk], [("PT", i % NPT)])

            def pv(i):
                h, vi, t, kt = items[i]
                hb = h % 2
                cs = cols(t)
                S.mm(pY[0:65, hb, cs], VA[:, t * 2 + kt, h, :], PT[:, i % NPT, cs], vi == 0, vi == nv - 1,
                     [("PT", i % NPT), ("VA", t * 2 + kt, h // 4), "VAones"], [("pY", hb)], mark=True)
                if vi == nv - 1:
                    epi_a(h)
                    pending.append((i + min(10, nv - 2), epi_b, h))
                    pending.append((i + min(16, 2 * nv - 2), epi_c, h))

            for i in range(n_it + LA):
                if i < n_it:
                    score(i)
                j = i - LA
                if j >= 0:
                    for item in [x for x in pending if x[0] <= j]:
                        item[1](item[2])
                        pending.remove(item)
                    pv(j)
            if filler is not None:
                filler()
            for item in list(pending):
                item[1](item[2])

        def gmlp_and_transposes():
            for t in range(4):
                S.op("dve", lambda E, t=t: E.bn_stats(out=lnst[:, 0:6], in_=vg[:, t, :]), ["VGM"], ["lnst"])
                S.op("dve", lambda E, t=t: E.bn_aggr(out=lnmv4[:, t, 0:2], in_=lnst[:, 0:6]), ["lnst"], ["lnmv"])
            S.ts("dve", lnmv4[:, :, 2], lnmv4[:, :, 1], EPS, None, ALU.add, None, ["lnmv"], ["lnmv2"])
            S.act(lnmv4[:, :, 3], lnmv4[:, :, 2], AF.Sqrt, ["lnmv2"], ["lnmv3"])
            S.op("dve", lambda E: E.reciprocal(out=lnmv4[:, :, 2], in_=lnmv4[:, :, 3]), ["lnmv3"], ["lnmv2"])
            S.stt(lnmv4[:, :, 1], lnmv4[:, :, 0], -1.0, lnmv4[:, :, 2], ALU.mult, ALU.mult, ["lnmv", "lnmv2"], ["lnnb"])

            def b1(t):
                b = t % 2
                S.act(gtmp[:, b, :], vg[:, t, :], AF.Identity, ["VGM", "lnmv2", "lnnb"], [("gtmp", b)],
                      scale=lnmv4[:, t, 2:3], bias=lnmv4[:, t, 1:2])
                S.tt("dve", gtmp[:, b, :], gtmp[:, b, :], lng[:, 0, :], ALU.mult, [("gtmp", b), "lng"], [("gtmp", b)])
                S.tt("dve", nbuf[:, b, 0:512], gtmp[:, b, :], lng[:, 1, :], ALU.add, [("gtmp", b), "lng"], [("nbuf", b)])
                by = t % 2
                for gi in range(8):
                    S.mm(pY[:, by, gi * 64:(gi + 1) * 64], wsm[:, gi, :], nbuf[:, b, gi * 64:(gi + 1) * 64], True, True,
                         ["wsm", ("nbuf", b)], [("pY", by)], mark=(gi == 7))

            def b2(t):
                b = t % 2
                by = t % 2
                S.tt("dve", gtmp[:, b, :].rearrange("p (g d) -> p g d", d=64), pY[:, by, :].rearrange("p (g d) -> p g d", d=64),
                     bst[:, :, None].to_broadcast([128, 8, 64]), ALU.add, [("pY", by), "bst"], [("gtmp", b)])
                S.tt("dve", gm[:, t, :], gtmp[:, b, :], ug[:, t, :], ALU.mult, [("gtmp", b), ("ug", t, 0), ("ug", t, 1)], [("PT", t)])
                pb = nxt("pT")
                for kc in range(4):
                    S.tr(pT[:, pb, 512 + kc * 128:512 + (kc + 1) * 128], gm[:, t, kc * 128:(kc + 1) * 128], idb[:],
                         [("PT", t), "idb"], [("pT", pb)], mark=(kc == 3))
                S.copy("act", gmT[:, :, t * 128:(t + 1) * 128], pT[:, pb, 512:1024].rearrange("p (c k) -> p c k", k=128),
                       [("pT", pb)], ["QA"])

            b1(0)
            b1(1)
            b2(0)
            b1(2)
            b2(1)
            b1(3)
            b2(2)
            b2(3)

        def sig1_slot(d):
            if d < 6:
                return HT[:, d, :], [("HT", d, 0)]
            return nbuf[:, d - 6, 512:1024], [("nbuf", d - 6)]

        def early_g1():
            nrd = [("nT", i) for i in range(4)]
            for cpair in range(4):
                w1, k1, i1 = W.acquire(wgt_v[:, :, cpair * 256:(cpair + 1) * 256], (128, 8, 256))
                for j in range(2):
                    dchunk = cpair * 2 + j
                    for dc in range(8):
                        S.mm(pY[:, j, :], w1[:, dc, j * 128:(j + 1) * 128], nT[:, dc, 0:512], dc == 0, dc == 7, [k1] + nrd, [("pY", j)])
                    s1, s1k = sig1_slot(dchunk)
                    S.act(s1, pY[:, j, :], AF.Sigmoid, [("pY", j), "bgt"], s1k, bias=bgt[:, dchunk:dchunk + 1])
                W.release(i1)

        ugf = ug[:].rearrange("p a b -> p (a b)").bitcast(F32).rearrange("p (a b) -> p a b", a=2)

        def hoisted_gates(dqs=(0, 1), late=False):
            nrd = [("nT", i) for i in range(4)]
            grd = ["QA"]
            for dq in dqs:
                wb, kb, ib = W.acquire(wbg_v[:, :, dq * 512:(dq + 1) * 512], (128, 4, 512))
                for dp in range(2):
                    cpair = dq * 2 + dp
                    w2, k2, i2 = W.acquire(wgt_v[:, :, D + cpair * 256:D + (cpair + 1) * 256], (128, 8, 256))
                    for j in range(2):
                        dchunk = cpair * 2 + j
                        lo = (dp * 2 + j) * 128
                        for dc in range(8):
                            S.mm(pY[:, j, :], w2[:, dc, j * 128:(j + 1) * 128], nT[:, dc, 0:512], dc == 0, dc == 7, [k2] + nrd, [("pY", j)])
                        bb = nxt("pB")
                        for kc in range(4):
                            S.mm(pB[:, bb, :], wb[:, kc, lo:lo + 128], gmT[:, kc, :], kc == 0, kc == 3, [kb] + grd, [("pB", bb)])
                        if late:
                            stmp, sk_ = ugf[:, j, :], [("ug", 2 * j, 0), ("ug", 2 * j, 1), ("ug", 2 * j + 1, 0), ("ug", 2 * j + 1, 1)]
                        else:
                            stmp, sk_ = sg[:, j, :], [("sg", j)]
                        S.act(stmp, pY[:, j, :], AF.Sigmoid, [("pY", j), "bgt"], sk_, bias=bgt[:, 8 + dchunk:9 + dchunk])
                        S.tt("dve", mrgT[:, dchunk, :], stmp, pB[:, bb, :], ALU.mult, sk_ + [("pB", bb)], ["VGM"])
                    W.release(i2)
                W.release(ib)

        def merge_and_out():
            ard = ["QA"]
            for dq in range(2):
                wa, ka, ia = W.acquire(wba_v[:, :, dq * 512:(dq + 1) * 512], (128, 4, 512))
                for dd in range(4):
                    dchunk = dq * 4 + dd
                    lo = dd * 128
                    ba = nxt("pA")
                    for kc in range(4):
                        S.mm(pA[:, ba, :], wa[:, kc, lo:lo + 128], attnT[:, kc, :], kc == 0, kc == 3, [ka] + ard, [("pA", ba)])
                    s1, s1k = sig1_slot(dchunk)
                    gb = dchunk % 2
                    S.tt("dve", gtmp[:, gb, :], s1, pA[:, ba, :], ALU.mult, s1k + [("pA", ba)], [("gtmp", gb)])
                    S.tt("dve", mrgT[:, dchunk, :], gtmp[:, gb, :], mrgT[:, dchunk, :], ALU.add, [("gtmp", gb)], ["VGM"])
                W.release(ia)
            mrd = ["VGM"]
            for ch in range(4):
                w, kw, iw = W.acquire(wout_v[:, :, ch * 256:(ch + 1) * 256], (128, 8, 256))
                for t in range(4):
                    ba = nxt("pA")
                    for dc in range(8):
                        S.mm(pA[:, ba, 0:256], mrgT[:, dc, t * 128:(t + 1) * 128], w[:, dc, :], dc == 0, dc == 7, [kw] + mrd, [("pA", ba)])
                    S.tt("dve", xh[:, t, ch * 256:(ch + 1) * 256], pA[:, ba, 0:256], xh[:, t, ch * 256:(ch + 1) * 256], ALU.add,
                         [("pA", ba)], [("xh", t)])
                    if ch == 3:
                        rms_stats_tile(t)
                W.release(iw)
            rms_rstd(4, tiles_done=True)

        def final_norm_store(g):
            rms_rstd(4)
            for t in range(4):
                S.stt(xh[:, t, :], xh[:, t, :], stat[:, 3, t:t + 1], normg[:, 3, :], ALU.mult, ALU.mult,
                      [("stat", 3), "normg"], [("xh", t)])
            S.dma("sp", lambda E: E.dma_start(out=out_v[g], in_=xh[:]), "out%d" % g,
                  reads=[("xh", t) for t in range(4)], writes=[("out", g)])

        def load_x(g):
            S.dma("sp", lambda E: E.dma_start(out=xp0, in_=xown_v[g][:, 0:2, :]), "xin0", writes=["VGM"])
            S.dma("sp", lambda E: E.dma_start(out=xp1, in_=xown_v[g][:, 2:4, :]), "xin1", writes=["QA"])

        def body():
            load_x(0)
            rms_rstd(4, src_xpre)
            for g in range(4):
                rms_to_nT(4, 0, src_xpre, do_stats=False)
                if g > 0:
                    final_norm_store(g - 1)
                ffn(4, 0, w1g_v, w1u_v, w1d_v, src=src_xpre, do_norm=False, tail_stats=True)
                rms_to_nT(4, 1, do_stats=False)
                qug_proj(g)
                early_g1()
                gmlp_and_transposes()
                gate_part1(g)
                hoisted_gates((0,))
                attention(g, filler=lambda: hoisted_gates((1,), late=True))
                merge_and_out()
                nxt_hook = None
                if g < 3:
                    load_x(g + 1)
                    nxt_hook = lambda: rms_rstd(4, src_xpre)
                ffn(4, 2, w2g_v, w2u_v, w2d_v, hook=nxt_hook, do_stats=False)
            final_norm_store(3)

        S.dry = True
        body()
        S.dry = False
        for k in cnt:
            cnt[k] = 0
        W.reset()
        setup()
        body()
        S.final_wait("sp", [("out", g) for g in range(4)])
        S.emit()
    return nc


def _tables(p):
    seqblk = OWN[0] + OWN[1]
    past = np.zeros((8, 16), np.float32)
    nb = np.zeros((8, 16), np.float32)
    for s in range(8):
        i = OWN[p][s]
        for t in range(16):
            if seqblk[t] < i:
                past[s, t] = 1.0
            elif t == p * 8 + s:
                nb[s, t] = 1.0e9
            else:
                nb[s, t] = -1.0e9
    past = np.ascontiguousarray(np.broadcast_to(past.reshape(1, 128), (128, 128)))
    nb = np.ascontiguousarray(np.broadcast_to(nb.reshape(1, 128), (128, 128)))
    return past, nb


def make_in_maps(inputs):
    f = lambda a: np.ascontiguousarray(np.asarray(a, dtype=np.float32))
    x = f(inputs["x"])
    rep = lambda v, n: np.ascontiguousarray(np.broadcast_to(f(v).reshape(1, n), (128, n)))
    norms = np.stack([rep(inputs["ffn1_norm"], D), rep(inputs["mix_norm"], D),
                      rep(inputs["ffn2_norm"], D), rep(inputs["final_norm"], D)], axis=1)
    lngb = np.stack([rep(inputs["gmlp_ln_g"], 512), rep(inputs["gmlp_ln_b"], 512)], axis=1)
    ws = f(inputs["gmlp_w_s"]).reshape(8, 128, 128)
    wsT = np.ascontiguousarray(ws.transpose(2, 0, 1))
    jj, ii = np.meshgrid(np.arange(128), np.arange(128), indexing="ij")
    trilT = (jj <= ii).astype(np.float32)
    bsT = np.ascontiguousarray(f(inputs["gmlp_b_s"]).reshape(8, 128).T)
    bgate = np.ascontiguousarray(f(inputs["b_gate"]).reshape(16, 128).T)
    ident = np.eye(128, dtype=np.float32)
    shodd = np.zeros((64, 128), np.float32)
    shodd[np.arange(64), 64 + np.arange(64)] = 1.0
    ej = np.zeros((16, 16, 128), np.float32)
    for j in range(16):
        ej[j, j, :] = 1.0
    ej = ej.reshape(16, 16 * 128)
    cm = np.zeros((128, 2, 256), np.float32)
    for kt in range(2):
        kpos = kt * 128 + np.arange(128)[:, None]
        qpos = np.arange(256)[None, :]
        cm[:, kt, :] = np.where(kpos <= qpos, 0.0, NEG)
    shared = {
        "ffn1_w_gate": f(inputs["ffn1_w_gate"]).reshape(D, FF), "ffn1_w_up": f(inputs["ffn1_w_up"]).reshape(D, FF),
        "ffn1_w_down": f(inputs["ffn1_w_down"]).reshape(FF, D),
        "ffn2_w_gate": f(inputs["ffn2_w_gate"]).reshape(D, FF), "ffn2_w_up": f(inputs["ffn2_w_up"]).reshape(D, FF),
        "ffn2_w_down": f(inputs["ffn2_w_down"]).reshape(FF, D),
        "w_in": f(inputs["w_in"]).reshape(D, INW),
        "w_branch_attn": f(inputs["w_branch_attn"]).reshape(512, D),
        "w_branch_gmlp": f(inputs["w_branch_gmlp"]).reshape(512, D),
        "w_gate": f(inputs["w_gate"]).reshape(D, 2 * D), "w_out": f(inputs["w_out"]).reshape(D, D),
        "norms": np.ascontiguousarray(norms), "lngb": np.ascontiguousarray(lngb), "wsT": wsT, "trilT": trilT,
        "bsT": bsT, "bgate": bgate, "ident": ident, "shodd": shodd, "ej": ej,
    }
    in_maps = []
    for c in range(8):
        b, p = c // 2, c % 2
        xb = x[b].reshape(16, BLK, D)
        m = dict(shared)
        m["x_own"] = np.ascontiguousarray(xb[OWN[p]].reshape(TOK, D))
        cmm = np.zeros((128, 2, 2, 256), np.float32)
        cmm[:, p] = cm
        m["cmask"] = cmm.reshape(128, 1024)
        m["past"], m["nb"] = _tables(p)
        in_maps.append(m)
    return in_maps


def assemble(outs):
    y = np.zeros((BATCH, 16, BLK, D), np.float32)
    for c in range(8):
        b, p = c // 2, c % 2
        y[b, OWN[p]] = np.asarray(outs[c], dtype=np.float32).reshape(8, BLK, D)
    return y.reshape(BATCH, SEQ, D)


_NC = None


def kernel(**inputs):
    global _NC
    if _NC is None:
        _NC = build_program()
    in_maps = make_in_maps(inputs)
    res = run_bass_kernel_spmd(_NC, in_maps, core_ids=list(range(8)))
    return assemble([r["out"] for r in res.results])
```

```python
import numpy as np
from collections import deque
from contextlib import ExitStack
import concourse.bass as bass
import concourse.mybir as mybir
from concourse.bass_utils import run_bass_kernel_spmd

F32 = mybir.dt.float32
BF16 = mybir.dt.bfloat16
AF = mybir.ActivationFunctionType
ALU = mybir.AluOpType
AX = mybir.AxisListType

D = 1024
FF = 2816
NFC = FF // 128
NH = 8
DH = 64
INW = 2560
SEQ = 4096
BATCH = 4
BLK = 256
TOK = 2048
EPS = 1e-6
OWN = [[0, 3, 4, 7, 8, 11, 12, 15], [1, 2, 5, 6, 9, 10, 13, 14]]
NEG = -30000.0
RING = 7
NPT = 4
FPH = [[0, 1, 2], [3, 4, 5], [6, 7, 8], [9, 10]]


class Sched:
    ENG = ("pe", "act", "dve", "pool", "sp")

    def __init__(self, nc, stack):
        self.nc = nc
        self.stack = stack
        self.dry = False
        self.prog = {e: [] for e in self.ENG}
        self.sem = {}
        self.cnt = {}
        for e in ("pe", "act", "dve", "pool"):
            self.sem[e] = stack.enter_context(nc.semaphore("sem_" + e))
            self.cnt[e] = 0
        self.waited = {}
        self.last_w = {}
        self.readers = {}
        self.dma_sems = {}

    def dma_sem(self, name):
        if name not in self.dma_sems:
            s = self.stack.enter_context(self.nc.semaphore("dsem_" + name))
            self.dma_sems[name] = [s, 0]
            self.sem[name] = s
        return self.dma_sems[name]

    def _wait(self, eng, semkey, value):
        if value <= 0:
            return
        if eng == "pe" and semkey == "pe":
            return
        k = (eng, semkey)
        if self.waited.get(k, 0) >= value:
            return
        self.waited[k] = value
        s = self.sem[semkey]
        self.prog[eng].append(lambda E: E.wait_ge(s, value))

    def _deps(self, eng, reads, writes):
        for r in reads:
            lw = self.last_w.get(r)
            if lw is not None:
                self._wait(eng, lw[0], lw[1])
        for w in writes:
            lw = self.last_w.get(w)
            if lw is not None:
                self._wait(eng, lw[0], lw[1])
            for rd in self.readers.get(w, ()):
                self._wait(eng, rd[0], rd[1])

    def _record(self, token, reads, writes):
        for w in writes:
            self.last_w[w] = token
            self.readers[w] = []
        for r in reads:
            if r in writes:
                continue
            lst = self.readers.setdefault(r, [])
            if not lst or lst[-1] != token:
                lst.append(token)
                if len(lst) > 24:
                    best = {}
                    for t in lst:
                        if t[0] not in best or best[t[0]][1] < t[1]:
                            best[t[0]] = t
                    lst[:] = list(best.values())

    PSUM_KEYS = ("pA", "pB", "pY", "pT")

    def op(self, eng, fn, reads=(), writes=(), mark=True):
        if self.dry:
            return
        pr = [r for r in reads if isinstance(r, tuple) and r[0] in self.PSUM_KEYS and r not in writes]
        if pr:
            reads = [r for r in reads if r not in pr]
            writes = list(writes) + pr
        self._deps(eng, reads, writes)
        if mark:
            self.cnt[eng] += 1
            s = self.sem[eng]
            self.prog[eng].append(lambda E: fn(E).then_inc(s, 1))
            tok = (eng, self.cnt[eng])
        else:
            self.prog[eng].append(fn)
            tok = (eng, self.cnt[eng] + 1)
        self._record(tok, reads, writes)

    def dma(self, queue, fn, semname, reads=(), writes=()):
        if self.dry:
            return
        self._deps(queue, reads, writes)
        ds = self.dma_sem(semname)
        ds[1] += 16
        s = ds[0]
        self.prog[queue].append(lambda E: fn(E).then_inc(s, 16))
        self._record((semname, ds[1]), reads, writes)

    def coll(self, fn, semname, reads=(), writes=()):
        if self.dry:
            return
        self._deps("pool", reads, writes)
        ds = self.dma_sem(semname)
        ds[1] += 1
        s = ds[0]
        self.prog["pool"].append(lambda E: fn(E).then_inc(s, 1))
        self._record((semname, ds[1]), reads, writes)

    def barrier(self):
        if self.dry:
            return
        toks = {}
        for d in (self.last_w,):
            for t in d.values():
                if t[0] not in toks or toks[t[0]] < t[1]:
                    toks[t[0]] = t[1]
        for lst in self.readers.values():
            for t in lst:
                if t[0] not in toks or toks[t[0]] < t[1]:
                    toks[t[0]] = t[1]
        for e in ("pe", "act", "dve", "pool", "sp"):
            for k, v in toks.items():
                if k == "pe" and v > self.cnt["pe"]:
                    continue
                self._wait(e, k, v)

    def final_wait(self, eng, regions):
        for r in regions:
            lw = self.last_w.get(r)
            if lw is not None:
                self._wait(eng, lw[0], lw[1])

    def mm(self, out, lhsT, rhs, start, stop, reads, writes, mark=None):
        if mark is None:
            mark = stop
        self.op("pe", lambda E: E.matmul(out, lhsT=lhsT, rhs=rhs, start=start, stop=stop),
                reads, writes, mark)

    def tr(self, out, in_, ident, reads, writes, mark=True):
        self.op("pe", lambda E: E.transpose(out=out, in_=in_, identity=ident), reads, writes, mark)

    def act(self, out, in_, func, reads, writes, bias=None, scale=None, accum_out=None):
        kw = {}
        if bias is not None:
            kw["bias"] = bias
        if scale is not None:
            kw["scale"] = scale
        if accum_out is not None:
            kw["accum_out"] = accum_out
        self.op("act", lambda E: E.activation(out=out, in_=in_, func=func, **kw), reads, writes)

    def tt(self, eng, out, in0, in1, op, reads, writes):
        self.op(eng, lambda E: E.tensor_tensor(out=out, in0=in0, in1=in1, op=op), reads, writes)

    def ts(self, eng, out, in0, s1, s2, op0, op1, reads, writes):
        if op1 is None:
            self.op(eng, lambda E: E.tensor_scalar(out=out, in0=in0, scalar1=s1, scalar2=None, op0=op0),
                    reads, writes)
        else:
            self.op(eng, lambda E: E.tensor_scalar(out=out, in0=in0, scalar1=s1, scalar2=s2, op0=op0, op1=op1),
                    reads, writes)

    def stt(self, out, in0, scalar, in1, op0, op1, reads, writes):
        self.op("dve", lambda E: E.scalar_tensor_tensor(out=out, in0=in0, scalar=scalar, in1=in1, op0=op0, op1=op1),
                reads, writes)

    def copy(self, eng, out, in_, reads, writes):
        if eng == "act":
            self.op("act", lambda E: E.copy(out=out, in_=in_), reads, writes)
        else:
            self.op(eng, lambda E: E.tensor_copy(out=out, in_=in_), reads, writes)

    def emit(self):
        nc = self.nc
        prog = self.prog
        with nc.Block() as block:
            @block.tensor
            def _(E):
                for f in prog["pe"]:
                    f(E)

            @block.scalar
            def _(E):
                for f in prog["act"]:
                    f(E)

            @block.vector
            def _(E):
                for f in prog["dve"]:
                    f(E)

            @block.gpsimd
            def _(E):
                for f in prog["pool"]:
                    f(E)

            @block.sync
            def _(E):
                for f in prog["sp"]:
                    f(E)


class WStream:
    def __init__(self, S, ring, R):
        self.S = S
        self.ring = ring
        self.R = R
        self.plan = []
        self.reset()

    def reset(self):
        self.idx = 0
        self.next_load = 0
        self.live = deque()

    def view(self, slot, shape):
        P, a, b = shape
        return self.ring[0:P, slot, 0:a * b].rearrange("p (a b) -> p a b", a=a)

    def acquire(self, src, shape):
        if self.S.dry:
            self.plan.append((src, shape))
            return self.view(0, shape), ("wr", 0), -1
        i = self.idx
        self.idx += 1
        assert self.plan[i][1] == shape
        self.live.append(i)
        self.pump()
        assert i < self.next_load
        return self.view(i % self.R, shape), ("wr", i % self.R), i

    def release(self, i):
        if self.S.dry:
            return
        self.live.remove(i)
        self.pump()

    def pump(self):
        oldest = self.live[0] if self.live else self.idx
        while self.next_load < len(self.plan) and self.next_load < oldest + self.R:
            j = self.next_load
            slot = j % self.R
            src, shape = self.plan[j]
            dst = self.view(slot, shape)
            self.S.dma("pool", lambda E, dst=dst, src=src: E.dma_start(out=dst, in_=src),
                       "wr%d" % slot, writes=[("wr", slot)])
            self.next_load += 1


def build_program(NPAIR=4):
    nc = bass.Bass("TRN2", target_bir_lowering=False)

    def din(name, shape):
        return nc.dram_tensor(name, list(shape), F32, kind="ExternalInput").ap()

    x_own = din("x_own", [TOK, D])
    w1g = din("ffn1_w_gate", [D, FF]); w1u = din("ffn1_w_up", [D, FF]); w1d = din("ffn1_w_down", [FF, D])
    w2g = din("ffn2_w_gate", [D, FF]); w2u = din("ffn2_w_up", [D, FF]); w2d = din("ffn2_w_down", [FF, D])
    w_in = din("w_in", [D, INW])
    w_ba = din("w_branch_attn", [512, D]); w_bg = din("w_branch_gmlp", [512, D])
    w_gt = din("w_gate", [D, 2 * D]); w_out = din("w_out", [D, D])
    norms = din("norms", [128, 4, D])
    lngb = din("lngb", [128, 2, 512])
    wsT = din("wsT", [128, 8, 128])
    trilT = din("trilT", [128, 128])
    bsT = din("bsT", [128, 8])
    bgate = din("bgate", [128, 16])
    ident_d = din("ident", [128, 128])
    shodd_d = din("shodd", [64, 128])
    ej_d = din("ej", [16, 16 * 128])
    cmask_d = din("cmask", [128, 2 * 2 * 256])
    past_d = din("past", [128, 8 * 16])
    nb_d = din("nb", [128, 8 * 16])
    out_d = nc.dram_tensor("out", [TOK, D], F32, kind="ExternalOutput").ap()

    def wview(w):
        return w.rearrange("(dc p) f -> p dc f", p=128)

    w1g_v, w1u_v, w2g_v, w2u_v = wview(w1g), wview(w1u), wview(w2g), wview(w2u)
    w1d_v = w1d.rearrange("(fc p) d -> p fc d", p=128)
    w2d_v = w2d.rearrange("(fc p) d -> p fc d", p=128)
    win_v = wview(w_in)
    wgt_v = wview(w_gt)
    wout_v = wview(w_out)
    wba_v = w_ba.rearrange("(kc p) d -> p kc d", p=128)
    wbg_v = w_bg.rearrange("(kc p) d -> p kc d", p=128)
    xown_v = x_own.rearrange("(g t p) d -> g p t d", p=128, t=4)
    NGRP = [[2 * i, 2 * i + 1] for i in range(NPAIR)]
    xsrc = [nc.dram_tensor("xsrc%d" % g, [128, 4096], BF16) for g in range(4)]
    xdst = [nc.dram_tensor("xdst%d" % g, [256, 4096], BF16) for g in range(4)]
    msrc = [nc.dram_tensor("msrc%d" % g, [128, 8], F32) for g in range(4)]
    mdst = [nc.dram_tensor("mdst%d" % g, [256, 8], F32) for g in range(4)]
    out_v = out_d.rearrange("(g t p) d -> g p t d", p=128, t=4)

    with ExitStack() as st:
        S = Sched(nc, st)

        def sb(name, shape, dt):
            return st.enter_context(nc.sbuf_tensor(name, list(shape), dt))

        def ps(name, shape, dt):
            return st.enter_context(nc.psum_tensor(name, list(shape), dt))

        ring = sb("wring", [128, RING, 2048], BF16)
        W = WStream(S, ring, RING)
        KT = sb("KT", [128, 4, SEQ], BF16)
        VA = sb("VA", [128, 32, NH, DH + 1], BF16)
        kmT = sb("kmT", [128, 4, 16], F32)
        normg = sb("normg", [128, 4, D], F32)
        lng = sb("lng", [128, 2, 512], F32)
        wsm = sb("wsm", [128, 8, 128], BF16)
        trl = sb("trl", [128, 128], BF16)
        bst = sb("bst", [128, 8], F32)
        bgt = sb("bgt", [128, 16], F32)
        idb = sb("idb", [128, 128], BF16)
        ejb = sb("ejb", [128, 16, 128], BF16)
        cmb = sb("cmb", [128, 2, 2, 256], BF16)
        kms = sb("kms", [128, 4, 2], F32)
        pastt = sb("pastt", [128, 8, 16], F32)
        nbt = sb("nbt", [128, 8, 16], F32)
        xh = sb("xh", [128, 4, D], F32)
        nT = sb("nT", [128, 8, 512], BF16)
        HT = sb("HT", [128, 6, 512], BF16)
        nbuf = sb("nbuf", [128, 2, D], BF16)
        sg = sb("sg", [128, 2, 512], F32)
        st6 = sb("st6", [128, 12], F32)
        mv = sb("mv", [128, 8, 2], F32)
        kmv = sb("kmv", [128, 2], F32)
        stat = sb("stat", [128, 4, 8], F32)
        QTz = sb("QTz", [128, 8, 512], BF16)
        kmz = sb("kmz", [128, 8, 16], BF16)
        qa = sb("qa", [128, 8, 512], BF16)
        QTlo = qa[:, 0:4, :]
        kmh = sb("kmh", [128, 2, 4, 16], BF16)
        kmd = sb("kmd", [128, 4, 16], F32)
        attnT = qa[:, 0:4, :]
        gmT = qa[:, 4:8, :]
        ug = sb("ug", [128, 4, 512], BF16)
        vgm = sb("vgm", [128, 8, 512], BF16)
        vg = vgm[:].rearrange("p a b -> p (a b)").bitcast(F32).rearrange("p (a b) -> p a b", a=4)
        mrgT = vgm
        gtmp = sb("gtmp", [128, 2, 512], F32)
        gmb = sb("gmb", [128, 8, 16], F32)
        mx8 = sb("mx8", [128, 8, 8], F32)
        thr = sb("thr", [128, 8], F32)
        biasq = sb("biasq", [128, 4, 8, 16], BF16)
        biasT = sb("biasT", [128, 8, 512], BF16)
        PT = sb("PT", [128, NPT, 512], BF16)
        shodd = sb("shodd_sb", [128, 128], BF16)
        ones_b = sb("ones_b", [128, 64], BF16)
        gm = PT
        lnst = sb("lnst", [128, 8], F32)
        lnmv4 = sb("lnmv4", [128, 4, 4], F32)
        pA = ps("pA", [128, 2, 512], F32)
        pB = ps("pB", [128, 2, 512], F32)
        pY = ps("pY", [128, 2, 512], F32)
        pT = ps("pT", [128, 2, 1024], BF16)
        cnt = {"pA": 0, "pB": 0, "pY": 0, "pT": 0}

        def nxt(name):
            cnt[name] += 1
            return cnt[name] % 2

        def setup():
            S.dma("sp", lambda E: E.dma_start(out=normg[:], in_=norms), "c0", writes=["normg"])
            S.dma("sp", lambda E: E.dma_start(out=lng[:], in_=lngb), "c1", writes=["lng"])
            S.dma("pool", lambda E: E.dma_start(out=wsm[:], in_=wsT), "c2", writes=["wsm"])
            S.dma("pool", lambda E: E.dma_start(out=trl[:], in_=trilT), "c3", writes=["trl"])
            S.dma("sp", lambda E: E.dma_start(out=bst[:], in_=bsT), "c4", writes=["bst"])
            S.dma("sp", lambda E: E.dma_start(out=bgt[:], in_=bgate), "c5", writes=["bgt"])
            S.dma("pool", lambda E: E.dma_start(out=idb[:], in_=ident_d), "c6", writes=["idb"])
            S.dma("pool", lambda E: E.dma_start(out=shodd[0:64, :], in_=shodd_d), "c11", writes=["shodd"])
            S.op("dve", lambda E: E.memset(ones_b[:], 1.0), [], ["ones_b"])
            S.op("dve", lambda E: E.memset(ejb[:], 0.0), [], ["ejb"])
            S.op("dve", lambda E: E.memset(biasT[:], 0.0), [], [("biasT", hp) for hp in range(4)])
            S.op("dve", lambda E: E.memset(QTz[:], 0.0), [], [("QTz", pc) for pc in range(4)])
            S.op("dve", lambda E: E.memset(kmz[:], 0.0), [], ["kmz"])
            S.dma("pool", lambda E: E.dma_start(out=ejb[0:16, :, :].rearrange("p a b -> p (a b)"), in_=ej_d), "c7", writes=["ejb"])
            S.dma("pool", lambda E: E.dma_start(out=cmb[:].rearrange("p m a b -> p (m a b)"), in_=cmask_d), "c8", writes=["cmb"])
            S.dma("sp", lambda E: E.dma_start(out=pastt[:].rearrange("p a b -> p (a b)"), in_=past_d), "c9", writes=["pastt"])
            S.dma("sp", lambda E: E.dma_start(out=nbt[:].rearrange("p a b -> p (a b)"), in_=nb_d), "c10", writes=["nbt"])
            S.tt("dve", wsm[:], wsm[:], trl[:, None, :].to_broadcast([128, 8, 128]), ALU.mult, ["trl"], ["wsm"])
            S.op("dve", lambda E: E.memset(VA[:, :, :, DH:DH + 1], 1.0), [], ["VAones"])
            S.op("dve", lambda E: E.memset(kmT[:], 0.0), [], [("kmT", pc, i) for pc in range(4) for i in range(8)])

        def src_xh(t):
            return xh[:, t, :], ("xh", t)

        xp0 = vgm[:].rearrange("p a b -> p (a b)").bitcast(F32).rearrange("p (t d) -> p t d", t=2)
        xp1 = qa[:].rearrange("p a b -> p (a b)").bitcast(F32).rearrange("p (t d) -> p t d", t=2)

        def src_xpre(t):
            return (xp0[:, t, :], "VGM") if t < 2 else (xp1[:, t - 2, :], "QA")

        def rms_stats_tile(t, src=src_xh):
            xa, xk = src(t)
            S.op("dve", lambda E, xa=xa: E.bn_stats(out=st6[:, 0:6], in_=xa[:, 0:512]), [xk], ["st6a"])
            S.op("dve", lambda E, xa=xa: E.bn_stats(out=st6[:, 6:12], in_=xa[:, 512:1024]), [xk], ["st6b"])
            S.op("dve", lambda E, t=t: E.bn_aggr(out=mv[:, t, :], in_=st6[:, 0:12]), ["st6a", "st6b"], [("mv", t)])

        def rms_rstd(NT, src=src_xh, tiles_done=False):
            if not tiles_done:
                for t in range(NT):
                    rms_stats_tile(t, src)
            rd = [("mv", t) for t in range(NT)]
            S.tt("dve", stat[:, 0, 0:NT], mv[:, 0:NT, 0], mv[:, 0:NT, 0], ALU.mult, rd, [("stat", 0)])
            S.tt("dve", stat[:, 1, 0:NT], stat[:, 0, 0:NT], mv[:, 0:NT, 1], ALU.add, rd + [("stat", 0)], [("stat", 1)])
            S.ts("dve", stat[:, 1, 0:NT], stat[:, 1, 0:NT], EPS, None, ALU.add, None, [("stat", 1)], [("stat", 1)])
            S.act(stat[:, 2, 0:NT], stat[:, 1, 0:NT], AF.Sqrt, [("stat", 1)], [("stat", 2)])
            S.op("dve", lambda E: E.reciprocal(out=stat[:, 3, 0:NT], in_=stat[:, 2, 0:NT]), [("stat", 2)], [("stat", 3)])

        def rms_to_nT(NT, which, src=src_xh, do_stats=True):
            if do_stats:
                rms_rstd(NT, src)
            for t in range(NT):
                b = t % 2
                xa, xk = src(t)
                S.stt(nbuf[:, b, :], xa, stat[:, 3, t:t + 1], normg[:, which, :], ALU.mult, ALU.mult,
                      [xk, ("stat", 3), "normg"], [("nbuf", b)])
                pb = nxt("pT")
                for dc in range(8):
                    S.tr(pT[:, pb, dc * 128:(dc + 1) * 128], nbuf[:, b, dc * 128:(dc + 1) * 128], idb[:],
                         [("nbuf", b), "idb"], [("pT", pb)], mark=(dc == 7))
                S.copy("act", nT[:, :, t * 128:(t + 1) * 128], pT[:, pb, :].rearrange("p (dc k) -> p dc k", k=128),
                       [("pT", pb)], [("nT", t)])

        def ffn(NT, which, wg_v, wu_v, wd_v, src=src_xh, do_norm=True, hook=None, do_stats=True, tail_stats=False):
            if do_norm:
                rms_to_nT(NT, which, src, do_stats=do_stats)
            nhalf = NT // 4
            for phi, ph in enumerate(FPH):
                if hook is not None and phi == len(FPH) - 1:
                    hook()
                for pi, c in enumerate(ph):
                    wg, kg, ig = W.acquire(wg_v[:, :, c * 256:(c + 1) * 256], (128, 8, 256))
                    wu, ku, iu = W.acquire(wu_v[:, :, c * 256:(c + 1) * 256], (128, 8, 256))
                    for half in range(nhalf):
                        nrd = [("nT", half * 4 + i) for i in range(4)]
                        for j in range(2):
                            fl = pi * 2 + j
                            ba = nxt("pA")
                            for dc in range(8):
                                S.mm(pA[:, ba, :], wg[:, dc, j * 128:(j + 1) * 128], nT[:, dc, half * 512:(half + 1) * 512],
                                     dc == 0, dc == 7, [kg] + nrd, [("pA", ba)])
                            bb = nxt("pB")
                            for dc in range(8):
                                S.mm(pB[:, bb, :], wu[:, dc, j * 128:(j + 1) * 128], nT[:, dc, half * 512:(half + 1) * 512],
                                     dc == 0, dc == 7, [ku] + nrd, [("pB", bb)])
                            S.act(sg[:, ba, :], pA[:, ba, :], AF.Silu, [("pA", ba)], [("sg", ba)])
                            S.tt("dve", HT[:, fl, half * 512:(half + 1) * 512], sg[:, ba, :], pB[:, bb, :], ALU.mult,
                                 [("sg", ba), ("pB", bb)], [("HT", fl, half)])
                    W.release(ig)
                    W.release(iu)
                wds = [W.acquire(wd_v[:, 2 * c:2 * c + 2, :], (128, 2, 1024)) for c in ph]
                nf = 2 * len(ph)
                for t in range(NT):
                    for hf in range(2):
                        by = nxt("pY")
                        for fl in range(nf):
                            wd, kd, _ = wds[fl // 2]
                            S.mm(pY[:, by, :], HT[:, fl, t * 128:(t + 1) * 128], wd[:, fl % 2, hf * 512:(hf + 1) * 512],
                                 fl == 0, fl == nf - 1, [kd, ("HT", fl, t // 4)], [("pY", by)])
                        if phi == 0:
                            xa, xk = src(t)
                        else:
                            xa, xk = src_xh(t)
                        S.stt(xh[:, t, hf * 512:(hf + 1) * 512], pY[:, by, :], 0.5, xa[:, hf * 512:(hf + 1) * 512],
                              ALU.mult, ALU.add, [("pY", by), xk], [("xh", t)])
                    if tail_stats and phi == len(FPH) - 1:
                        rms_stats_tile(t)
                for _, _, i in wds:
                    W.release(i)
            if tail_stats:
                rms_rstd(NT, tiles_done=True)

        def kv_proj(g):
            kst = gtmp[:].rearrange("p a b -> p (a b)").bitcast(BF16).rearrange("p (a b) -> p a b", a=4)
            vst = sg[:].rearrange("p a b -> p (a b)").bitcast(BF16).rearrange("p (t h d) -> p t h d", t=4, h=8)
            nrd = [("nT", i) for i in range(4)]
            for c in (2, 3):
                w, kw, iw = W.acquire(win_v[:, :, c * 256:(c + 1) * 256], (128, 8, 256))
                for j in range(2):
                    pc = (c - 2) * 2 + j
                    ba = nxt("pA")
                    for dc in range(8):
                        S.mm(pA[:, ba, :], w[:, dc, j * 128:(j + 1) * 128], nT[:, dc, 0:512],
                             dc == 0, dc == 7, [kw] + nrd, [("pA", ba)])
                    S.copy("act", kst[:, pc, :], pA[:, ba, :], [("pA", ba)], [("gtmp", pc // 2)])
                    for bl in range(2):
                        S.op("dve", lambda E, ba=ba, bl=bl: E.bn_stats(out=st6[:, 0:6], in_=pA[:, ba, bl * 256:(bl + 1) * 256]),
                             [("pA", ba)], ["st6a"])
                        S.op("dve", lambda E: E.bn_aggr(out=kmv[:, 0:2], in_=st6[:, 0:6]), ["st6a"], ["kmv"])
                        S.copy("dve", kms[:, pc, bl:bl + 1], kmv[:, 0:1], ["kmv"], ["kms"])
                W.release(iw)
            S.dma("sp", lambda E: E.dma_start(out=msrc[g][:, :], in_=kms[:].rearrange("p a b -> p (a b)")), "stm%d" % g,
                  reads=["kms"], writes=[("msrc", g)])
            S.coll(lambda E: E.collective_compute("AllGather", ALU.bypass, replica_groups=NGRP,
                                                  ins=[msrc[g].ap().opt()], outs=[mdst[g].ap().opt()]),
                   "ccm%d" % g, reads=[("msrc", g)], writes=[("mdst", g)])
            for c in (4, 5):
                w, kw, iw = W.acquire(win_v[:, :, c * 256:(c + 1) * 256], (128, 8, 256))
                for t in range(4):
                    by = nxt("pY")
                    for dc in range(8):
                        S.mm(pY[:, by, 0:256], nT[:, dc, t * 128:(t + 1) * 128], w[:, dc, :],
                             dc == 0, dc == 7, [kw, ("nT", t)], [("pY", by)])
                    h0 = (c - 4) * 4
                    S.copy("act", vst[:, t, h0:h0 + 4, :], pY[:, by, 0:256].rearrange("p (h d) -> p h d", d=DH),
                           [("pY", by)], [("sg", t // 2)])
                W.release(iw)
            S.dma("sp", lambda E: E.dma_start(out=xsrc[g][:, 0:2048], in_=kst.rearrange("p a b -> p (a b)")), "stk%d" % g,
                  reads=[("gtmp", 0), ("gtmp", 1)], writes=[("xsrc", g, 0)])
            S.dma("sp", lambda E: E.dma_start(out=xsrc[g][:, 2048:4096], in_=vst.rearrange("p t h d -> p (t h d)")), "stv%d" % g,
                  reads=[("sg", 0), ("sg", 1)], writes=[("xsrc", g, 1)])
            for m in range(2):
                tb = m * 8 + 2 * g
                S.dma("sp", lambda E, m=m, tb=tb: E.dma_start(
                    out=kmT[:, :, tb:tb + 2],
                    in_=mdst[g][m * 128:(m + 1) * 128, :].rearrange("p (a b) -> p a b", a=4)),
                    "rbm%d%d" % (g, m), reads=[("mdst", g)], writes=[("kmT", pc, m * 4 + g) for pc in range(4)])
            S.coll(lambda E: E.collective_compute("AllGather", ALU.bypass, replica_groups=NGRP,
                                                  ins=[xsrc[g].ap().opt()], outs=[xdst[g].ap().opt()]),
                   "cck%d" % g, reads=[("xsrc", g, 0), ("xsrc", g, 1)], writes=[("xdst", g)])
            for m in range(2):
                tb = m * 8 + 2 * g
                S.dma("sp", lambda E, m=m, tb=tb: E.dma_start(
                    out=KT[:, :, tb * 256:tb * 256 + 512],
                    in_=xdst[g][m * 128:(m + 1) * 128, 0:2048].rearrange("p (a b) -> p a b", a=4)),
                    "rbk%d%d" % (g, m), reads=[("xdst", g)], writes=[("KT", pc, m * 4 + g) for pc in range(4)])
                S.dma("sp", lambda E, m=m, tb=tb: E.dma_start(
                    out=VA[:, tb * 2:tb * 2 + 4, :, 0:DH],
                    in_=xdst[g][m * 128:(m + 1) * 128, 2048:4096].rearrange("p (t h d) -> p t h d", t=4, h=8)),
                    "rbv%d%d" % (g, m), reads=[("xdst", g)],
                    writes=[("VA", tb * 2 + i, hh) for i in range(4) for hh in range(2)])

        def gelu_from_psum(src, rkey, dst, wkey):
            S.act(dst, src, AF.Gelu_apprx_tanh, [rkey], [wkey])

        def qug_proj(g):
            kv_proj(g)
            for c in (0, 1):
                w, kw, iw = W.acquire(win_v[:, :, c * 256:(c + 1) * 256], (128, 8, 256))
                nrd = [("nT", i) for i in range(4)]
                for j in range(2):
                    pc = c * 2 + j
                    ba = nxt("pA")
                    for dc in range(8):
                        S.mm(pA[:, ba, :], w[:, dc, j * 128:(j + 1) * 128], nT[:, dc, 0:512],
                             dc == 0, dc == 7, [kw] + nrd, [("pA", ba)])
                    S.op("act", lambda E, pc=pc, ba=ba: E.mul(out=QTz[0:64, 2 * pc, :], in_=pA[0:64, ba, :], mul=0.125),
                         [("pA", ba)], [("QTz", pc)])
                    S.op("act", lambda E, pc=pc, ba=ba: E.mul(out=QTz[64:128, 2 * pc + 1, :], in_=pA[64:128, ba, :], mul=0.125),
                         [("pA", ba)], [("QTz", pc)])
                    S.stt(QTlo[0:64, pc, :], pA[0:64, ba, :], 0.125, QTz[0:64, 2 * pc, :], ALU.mult, ALU.subtract,
                          [("pA", ba), ("QTz", pc)], ["QA"])
                    S.stt(QTlo[64:128, pc, :], pA[64:128, ba, :], 0.125, QTz[64:128, 2 * pc + 1, :], ALU.mult, ALU.subtract,
                          [("pA", ba), ("QTz", pc)], ["QA"])
                W.release(iw)
            for c in (6, 7, 8, 9):
                w, kw, iw = W.acquire(win_v[:, :, c * 256:(c + 1) * 256], (128, 8, 256))
                for t in range(4):
                    by = nxt("pY")
                    for dc in range(8):
                        S.mm(pY[:, by, 0:256], nT[:, dc, t * 128:(t + 1) * 128], w[:, dc, :],
                             dc == 0, dc == 7, [kw, ("nT", t)], [("pY", by)])
                    if c < 8:
                        dst = ug[:, t, (c - 6) * 256:(c - 5) * 256]
                        wk = ("ug", t, c - 6)
                    else:
                        dst = vg[:, t, (c - 8) * 256:(c - 7) * 256]
                        wk = "VGM"
                    gelu_from_psum(pY[:, by, 0:256], ("pY", by), dst, wk)
                W.release(iw)

        def gate_part1(g):
            kmk = [("kmT", pc, i) for pc in range(4) for i in range(8)]
            S.copy("dve", kmh[:, 0, :, :], kmT[:], kmk, ["kmh0"])
            S.tt("dve", kmd[:], kmT[:], kmh[:, 0, :, :], ALU.subtract, kmk + ["kmh0"], ["kmd"])
            S.copy("dve", kmh[:, 1, :, :], kmd[:], ["kmd"], ["kmh1"])
            for pc in range(4):
                S.copy("dve", kmz[0:64, 2 * pc, :], kmh[0:64, 0, pc, :], ["kmh0"], ["kmz"])
                S.copy("dve", kmz[64:128, 2 * pc + 1, :], kmh[64:128, 0, pc, :], ["kmh0"], ["kmz"])
            for sub in range(4):
                s = 2 * g + sub // 2
                ba = nxt("pA")
                for h in range(NH):
                    pc = h // 2
                    qs = slice(sub * 128, (sub + 1) * 128)
                    terms = [(QTz[:, h, qs], kmh[:, 0, pc, :]), (QTz[:, h, qs], kmh[:, 1, pc, :]), (QTlo[:, pc, qs], kmz[:, h, :])]
                    for ti, (qq, kk) in enumerate(terms):
                        S.mm(pA[:, ba, h * 16:(h + 1) * 16], qq, kk,
                             ti == 0, ti == 2, [("QTz", pc), "QA", "kmh0", "kmh1", "kmz"], [("pA", ba)],
                             mark=(h == NH - 1 and ti == 2))
                S.tt("dve", gmb[:], pA[:, ba, 0:128].rearrange("p (h t) -> p h t", t=16),
                     pastt[:, s:s + 1, :].to_broadcast([128, 8, 16]), ALU.mult, [("pA", ba), "pastt"], ["gmb"])
                S.tt("dve", gmb[:], gmb[:], nbt[:, s:s + 1, :].to_broadcast([128, 8, 16]), ALU.add, ["gmb", "nbt"], ["gmb"])
                for h in range(NH):
                    S.op("dve", lambda E, h=h: E.max(out=mx8[:, h, :], in_=gmb[:, h, :]), ["gmb"], [("mx8", h)])
                S.ts("dve", thr[:], mx8[:, :, 3], -1.0e8, None, ALU.max, None, [("mx8", h) for h in range(NH)], ["thr"])
                S.tt("dve", gmb[:], gmb[:], thr[:, :, None].to_broadcast([128, 8, 16]), ALU.is_ge, ["gmb", "thr"], ["gmb"])
                S.ts("dve", biasq[:, sub, :, :], gmb[:], -NEG, NEG, ALU.mult, ALU.add, ["gmb"], [("biasq", sub)])
        def attention(g, filler=None):
            for hp in range(4):
                pb = nxt("pT")
                for hh in range(2):
                    h = hp * 2 + hh
                    for sub in range(4):
                        S.tr(pT[0:16, pb, hh * 512 + sub * 128: hh * 512 + (sub + 1) * 128], biasq[:, sub, h, :], idb[:],
                             [("biasq", sub), "idb"], [("pT", pb)], mark=(hh == 1 and sub == 3))
                S.copy("dve", biasT[0:16, hp * 2:hp * 2 + 2, :], pT[0:16, pb, :].rearrange("p (h q) -> p h q", q=512),
                       [("pT", pb)], [("biasT", hp)])
            nblk = 2 * g + 2
            visits = [(m * 8 + t, kt) for t in range(2 * g) for m in range(2) for kt in range(2)] + \
                     [(m * 8 + t, kt) for t in (2 * g, 2 * g + 1) for m in range(2) for kt in range(2)]
            nv = len(visits)
            sbufs = [(pA, 0), (pA, 1), (pB, 0), (pB, 1)]
            skeys = [("pA", 0), ("pA", 1), ("pB", 0), ("pB", 1)]
            bc = pT[:, 1, :].bitcast(F32)
            pk = pT[:, 0, :].bitcast(F32)
            recf = gtmp[:, 0, :]
            rhl = gtmp[:, 1, :].bitcast(BF16)
            ah = nbuf[:, :, 0:512]
            st_ = {"si": 0, "pti": 0}

            def epi_a(h):
                hb = h % 2
                S.copy("act", sg[0:65, hb, :], pY[0:65, hb, :], [("pY", hb)], [("sg", hb)])
                S.op("dve", lambda E: E.reciprocal(out=recf[64:65, :], in_=sg[64:65, hb, :]), [("sg", hb)], [("gtmp", 0)])
                S.copy("dve", rhl[64:65, 0:512], recf[64:65, :], [("gtmp", 0)], [("gtmp", 1)])
                S.tt("dve", recf[64:65, :], recf[64:65, :], rhl[64:65, 0:512], ALU.subtract, [("gtmp", 1)], [("gtmp", 0)])
                S.copy("dve", rhl[64:65, 512:1024], recf[64:65, :], [("gtmp", 0)], [("gtmp", 1)])

            def epi_b(h):
                hb = h % 2
                S.mm(bc[0:64, :], ones_b[64:65, :], rhl[64:65, 0:512], True, False, ["ones_b", ("gtmp", 1)], [("pT", 1)], mark=False)
                S.mm(bc[0:64, :], ones_b[64:65, :], rhl[64:65, 512:1024], False, True, ["ones_b", ("gtmp", 1)], [("pT", 1)], mark=True)
                S.tt("dve", ah[0:64, hb, :], sg[0:64, hb, :], bc[0:64, :], ALU.mult, [("sg", hb), ("pT", 1)], [("nbuf", hb)])

            def epi_c(h):
                hb = h % 2
                if hb == 0:
                    S.mm(pk, idb[0:64, :], ah[0:64, 0, :], True, False, ["idb", ("nbuf", 0)], [("pT", 0)], mark=True)
                else:
                    S.mm(pk, shodd[0:64, :], ah[0:64, 1, :], False, True, ["shodd", ("nbuf", 1)], [("pT", 0)], mark=True)
                    S.copy("act", attnT[:, h // 2, :], pk, [("pT", 0)], ["QA"])

            items = [(h, vi, t, kt) for h in range(NH) for vi, (t, kt) in enumerate(visits)]
            n_it = len(items)
            LA = 3
            pending = []

            def cols(t):
                return slice(256, 512) if t in (2 * g + 1, 8 + 2 * g + 1) else slice(0, 512)

            def score(i):
                h, vi, t, kt = items[i]
                pc = h // 2
                k0 = t * 256 + kt * 128
                ten, bi = sbufs[i % 4]
                sk = skeys[i % 4]
                cs = cols(t)
                diag = (t % 8) in (2 * g, 2 * g + 1)
                S.mm(ten[:, bi, cs], KT[:, pc, k0:k0 + 128], QTz[:, h, cs], True, False,
                     [("KT", pc, k0 // 512), ("QTz", pc)], [sk], mark=False)
                S.mm(ten[:, bi, cs], ejb[:, t, :], biasT[:, h, cs], False, not diag,
                     ["ejb", ("biasT", h // 2)], [sk], mark=(not diag))
                if diag:
                    qo = (t % 8 - 2 * g) * 256
                    S.mm(ten[:, bi, qo:qo + 256], idb[:], cmb[:, t // 8, kt, :], False, True, ["idb", "cmb"], [sk], mark=True)
                S.act(PT[:, i % NPT, cs], ten[:, bi, cs], AF.Exp, [sk], [("PT", i % NPT)])

            def pv(i):
                h, vi, t, kt = items[i]
                hb = h % 2
                cs = cols(t)
                S.mm(pY[0:65, hb, cs], VA[:, t * 2 + kt, h, :], PT[:, i % NPT, cs], vi == 0, vi == nv - 1,
                     [("PT", i % NPT), ("VA", t * 2 + kt, h // 4), "VAones"], [("pY", hb)], mark=True)
                if vi == nv - 1:
                    epi_a(h)
                    pending.append((i + min(10, nv - 2), epi_b, h))
                    pending.append((i + min(16, 2 * nv - 2), epi_c, h))

            for i in range(n_it + LA):
                if i < n_it:
                    score(i)
                j = i - LA
                if j >= 0:
                    for item in [x for x in pending if x[0] <= j]:
                        item[1](item[2])
                        pending.remove(item)
                    pv(j)
            if filler is not None:
                filler()
            for item in list(pending):
                item[1](item[2])

        def gmlp_and_transposes():
            for t in range(4):
                S.op("dve", lambda E, t=t: E.bn_stats(out=lnst[:, 0:6], in_=vg[:, t, :]), ["VGM"], ["lnst"])
                S.op("dve", lambda E, t=t: E.bn_aggr(out=lnmv4[:, t, 0:2], in_=lnst[:, 0:6]), ["lnst"], ["lnmv"])
            S.ts("dve", lnmv4[:, :, 2], lnmv4[:, :, 1], EPS, None, ALU.add, None, ["lnmv"], ["lnmv2"])
            S.act(lnmv4[:, :, 3], lnmv4[:, :, 2], AF.Sqrt, ["lnmv2"], ["lnmv3"])
            S.op("dve", lambda E: E.reciprocal(out=lnmv4[:, :, 2], in_=lnmv4[:, :, 3]), ["lnmv3"], ["lnmv2"])

            def b1(t):
                b = t % 2
                S.ts("dve", gtmp[:, b, :], vg[:, t, :], lnmv4[:, t, 0:1], lnmv4[:, t, 2:3], ALU.subtract, ALU.mult,
                     ["VGM", "lnmv", "lnmv2"], [("gtmp", b)])
                S.tt("dve", gtmp[:, b, :], gtmp[:, b, :], lng[:, 0, :], ALU.mult, [("gtmp", b), "lng"], [("gtmp", b)])
                S.tt("dve", nbuf[:, b, 0:512], gtmp[:, b, :], lng[:, 1, :], ALU.add, [("gtmp", b), "lng"], [("nbuf", b)])
                by = t % 2
                for gi in range(8):
                    S.mm(pY[:, by, gi * 64:(gi + 1) * 64], wsm[:, gi, :], nbuf[:, b, gi * 64:(gi + 1) * 64], True, True,
                         ["wsm", ("nbuf", b)], [("pY", by)], mark=(gi == 7))

            def b2(t):
                b = t % 2
                by = t % 2
                S.tt("dve", gtmp[:, b, :].rearrange("p (g d) -> p g d", d=64), pY[:, by, :].rearrange("p (g d) -> p g d", d=64),
                     bst[:, :, None].to_broadcast([128, 8, 64]), ALU.add, [("pY", by), "bst"], [("gtmp", b)])
                S.tt("dve", gm[:, t, :], gtmp[:, b, :], ug[:, t, :], ALU.mult, [("gtmp", b), ("ug", t, 0), ("ug", t, 1)], [("PT", t)])
                pb = nxt("pT")
                for kc in range(4):
                    S.tr(pT[:, pb, 512 + kc * 128:512 + (kc + 1) * 128], gm[:, t, kc * 128:(kc + 1) * 128], idb[:],
                         [("PT", t), "idb"], [("pT", pb)], mark=(kc == 3))
                S.copy("act", gmT[:, :, t * 128:(t + 1) * 128], pT[:, pb, 512:1024].rearrange("p (c k) -> p c k", k=128),
                       [("pT", pb)], ["QA"])

            b1(0)
            b1(1)
            b2(0)
            b1(2)
            b2(1)
            b1(3)
            b2(2)
            b2(3)

        def sig1_slot(d):
            if d < 6:
                return HT[:, d, :], [("HT", d, 0)]
            return nbuf[:, d - 6, 512:1024], [("nbuf", d - 6)]

        def early_g1():
            nrd = [("nT", i) for i in range(4)]
            for cpair in range(4):
                w1, k1, i1 = W.acquire(wgt_v[:, :, cpair * 256:(cpair + 1) * 256], (128, 8, 256))
                for j in range(2):
                    dchunk = cpair * 2 + j
                    for dc in range(8):
                        S.mm(pY[:, j, :], w1[:, dc, j * 128:(j + 1) * 128], nT[:, dc, 0:512], dc == 0, dc == 7, [k1] + nrd, [("pY", j)])
                    s1, s1k = sig1_slot(dchunk)
                    S.act(s1, pY[:, j, :], AF.Sigmoid, [("pY", j), "bgt"], s1k, bias=bgt[:, dchunk:dchunk + 1])
                W.release(i1)

        ugf = ug[:].rearrange("p a b -> p (a b)").bitcast(F32).rearrange("p (a b) -> p a b", a=2)

        sgb = sg[:].rearrange("p a b -> p (a b)").bitcast(BF16).rearrange("p (a b) -> p a b", a=4)

        def early_g2():
            nrd = [("nT", i) for i in range(4)]
            for cpair in range(2):
                w2, k2, i2 = W.acquire(wgt_v[:, :, D + cpair * 256:D + (cpair + 1) * 256], (128, 8, 256))
                for j in range(2):
                    dchunk = cpair * 2 + j
                    for dc in range(8):
                        S.mm(pY[:, j, :], w2[:, dc, j * 128:(j + 1) * 128], nT[:, dc, 0:512], dc == 0, dc == 7, [k2] + nrd, [("pY", j)])
                    S.act(sgb[:, dchunk, :], pY[:, j, :], AF.Sigmoid, [("pY", j), "bgt"], [("sg", dchunk // 2)],
                          bias=bgt[:, 8 + dchunk:9 + dchunk])
                W.release(i2)

        def hoisted_gates(dqs=(0, 1), late=False, g2_done=False):
            nrd = [("nT", i) for i in range(4)]
            grd = ["QA"]
            for dq in dqs:
                wb, kb, ib = W.acquire(wbg_v[:, :, dq * 512:(dq + 1) * 512], (128, 4, 512))
                for dp in range(2):
                    cpair = dq * 2 + dp
                    if not g2_done:
                        w2, k2, i2 = W.acquire(wgt_v[:, :, D + cpair * 256:D + (cpair + 1) * 256], (128, 8, 256))
                    for j in range(2):
                        dchunk = cpair * 2 + j
                        lo = (dp * 2 + j) * 128
                        if not g2_done:
                            for dc in range(8):
                                S.mm(pY[:, j, :], w2[:, dc, j * 128:(j + 1) * 128], nT[:, dc, 0:512], dc == 0, dc == 7, [k2] + nrd, [("pY", j)])
                        bb = nxt("pB")
                        for kc in range(4):
                            S.mm(pB[:, bb, :], wb[:, kc, lo:lo + 128], gmT[:, kc, :], kc == 0, kc == 3, [kb] + grd, [("pB", bb)])
                        if g2_done:
                            stmp, sk_ = sgb[:, dchunk, :], [("sg", dchunk // 2)]
                        elif late:
                            stmp, sk_ = ugf[:, j, :], [("ug", 2 * j, 0), ("ug", 2 * j, 1), ("ug", 2 * j + 1, 0), ("ug", 2 * j + 1, 1)]
                        else:
                            stmp, sk_ = sg[:, j, :], [("sg", j)]
                        if not g2_done:
                            S.act(stmp, pY[:, j, :], AF.Sigmoid, [("pY", j), "bgt"], sk_, bias=bgt[:, 8 + dchunk:9 + dchunk])
                        S.tt("dve", mrgT[:, dchunk, :], stmp, pB[:, bb, :], ALU.mult, sk_ + [("pB", bb)], ["VGM"])
                    if not g2_done:
                        W.release(i2)
                W.release(ib)

        def merge_and_out():
            ard = ["QA"]
            for dq in range(2):
                wa, ka, ia = W.acquire(wba_v[:, :, dq * 512:(dq + 1) * 512], (128, 4, 512))
                for dd in range(4):
                    dchunk = dq * 4 + dd
                    lo = dd * 128
                    ba = nxt("pA")
                    for kc in range(4):
                        S.mm(pA[:, ba, :], wa[:, kc, lo:lo + 128], attnT[:, kc, :], kc == 0, kc == 3, [ka] + ard, [("pA", ba)])
                    s1, s1k = sig1_slot(dchunk)
                    gb = dchunk % 2
                    S.tt("dve", gtmp[:, gb, :], s1, pA[:, ba, :], ALU.mult, s1k + [("pA", ba)], [("gtmp", gb)])
                    S.tt("dve", mrgT[:, dchunk, :], gtmp[:, gb, :], mrgT[:, dchunk, :], ALU.add, [("gtmp", gb)], ["VGM"])
                W.release(ia)
            mrd = ["VGM"]
            for ch in range(4):
                w, kw, iw = W.acquire(wout_v[:, :, ch * 256:(ch + 1) * 256], (128, 8, 256))
                for t in range(4):
                    ba = nxt("pA")
                    for dc in range(8):
                        S.mm(pA[:, ba, 0:256], mrgT[:, dc, t * 128:(t + 1) * 128], w[:, dc, :], dc == 0, dc == 7, [kw] + mrd, [("pA", ba)])
                    S.tt("dve", xh[:, t, ch * 256:(ch + 1) * 256], pA[:, ba, 0:256], xh[:, t, ch * 256:(ch + 1) * 256], ALU.add,
                         [("pA", ba)], [("xh", t)])
                    if ch == 3:
                        rms_stats_tile(t)
                W.release(iw)
            rms_rstd(4, tiles_done=True)

        def final_norm_store(g):
            rms_rstd(4)
            for t in range(4):
                S.stt(xh[:, t, :], xh[:, t, :], stat[:, 3, t:t + 1], normg[:, 3, :], ALU.mult, ALU.mult,
                      [("stat", 3), "normg"], [("xh", t)])
            S.dma("sp", lambda E: E.dma_start(out=out_v[g], in_=xh[:]), "out%d" % g,
                  reads=[("xh", t) for t in range(4)], writes=[("out", g)])

        def load_x(g):
            S.dma("sp", lambda E: E.dma_start(out=xp0, in_=xown_v[g][:, 0:2, :]), "xin0", writes=["VGM"])
            S.dma("sp", lambda E: E.dma_start(out=xp1, in_=xown_v[g][:, 2:4, :]), "xin1", writes=["QA"])

        def body():
            load_x(0)
            rms_rstd(4, src_xpre)
            for g in range(4):
                rms_to_nT(4, 0, src_xpre, do_stats=False)
                if g > 0:
                    final_norm_store(g - 1)
                ffn(4, 0, w1g_v, w1u_v, w1d_v, src=src_xpre, do_norm=False, tail_stats=True)
                rms_to_nT(4, 1, do_stats=False)
                qug_proj(g)
                early_g1()
                early_g2()
                gmlp_and_transposes()
                gate_part1(g)
                hoisted_gates((0,), g2_done=True)
                attention(g, filler=lambda: hoisted_gates((1,), late=True))
                merge_and_out()
                nxt_hook = None
                if g < 3:
                    load_x(g + 1)
                    nxt_hook = lambda: rms_rstd(4, src_xpre)
                ffn(4, 2, w2g_v, w2u_v, w2d_v, hook=nxt_hook, do_stats=False)
            final_norm_store(3)

        S.dry = True
        body()
        S.dry = False
        for k in cnt:
            cnt[k] = 0
        W.reset()
        setup()
        body()
        S.final_wait("sp", [("out", g) for g in range(4)])
        S.emit()
    return nc


def _tables(p):
    seqblk = OWN[0] + OWN[1]
    past = np.zeros((8, 16), np.float32)
    nb = np.zeros((8, 16), np.float32)
    for s in range(8):
        i = OWN[p][s]
        for t in range(16):
            if seqblk[t] < i:
                past[s, t] = 1.0
            elif t == p * 8 + s:
                nb[s, t] = 1.0e9
            else:
                nb[s, t] = -1.0e9
    past = np.ascontiguousarray(np.broadcast_to(past.reshape(1, 128), (128, 128)))
    nb = np.ascontiguousarray(np.broadcast_to(nb.reshape(1, 128), (128, 128)))
    return past, nb


def make_in_maps(inputs):
    f = lambda a: np.ascontiguousarray(np.asarray(a, dtype=np.float32))
    x = f(inputs["x"])
    rep = lambda v, n: np.ascontiguousarray(np.broadcast_to(f(v).reshape(1, n), (128, n)))
    norms = np.stack([rep(inputs["ffn1_norm"], D), rep(inputs["mix_norm"], D),
                      rep(inputs["ffn2_norm"], D), rep(inputs["final_norm"], D)], axis=1)
    lngb = np.stack([rep(inputs["gmlp_ln_g"], 512), rep(inputs["gmlp_ln_b"], 512)], axis=1)
    ws = f(inputs["gmlp_w_s"]).reshape(8, 128, 128)
    wsT = np.ascontiguousarray(ws.transpose(2, 0, 1))
    jj, ii = np.meshgrid(np.arange(128), np.arange(128), indexing="ij")
    trilT = (jj <= ii).astype(np.float32)
    bsT = np.ascontiguousarray(f(inputs["gmlp_b_s"]).reshape(8, 128).T)
    bgate = np.ascontiguousarray(f(inputs["b_gate"]).reshape(16, 128).T)
    ident = np.eye(128, dtype=np.float32)
    shodd = np.zeros((64, 128), np.float32)
    shodd[np.arange(64), 64 + np.arange(64)] = 1.0
    ej = np.zeros((16, 16, 128), np.float32)
    for j in range(16):
        ej[j, j, :] = 1.0
    ej = ej.reshape(16, 16 * 128)
    cm = np.zeros((128, 2, 256), np.float32)
    for kt in range(2):
        kpos = kt * 128 + np.arange(128)[:, None]
        qpos = np.arange(256)[None, :]
        cm[:, kt, :] = np.where(kpos <= qpos, 0.0, NEG)
    shared = {
        "ffn1_w_gate": f(inputs["ffn1_w_gate"]).reshape(D, FF), "ffn1_w_up": f(inputs["ffn1_w_up"]).reshape(D, FF),
        "ffn1_w_down": f(inputs["ffn1_w_down"]).reshape(FF, D),
        "ffn2_w_gate": f(inputs["ffn2_w_gate"]).reshape(D, FF), "ffn2_w_up": f(inputs["ffn2_w_up"]).reshape(D, FF),
        "ffn2_w_down": f(inputs["ffn2_w_down"]).reshape(FF, D),
        "w_in": f(inputs["w_in"]).reshape(D, INW),
        "w_branch_attn": f(inputs["w_branch_attn"]).reshape(512, D),
        "w_branch_gmlp": f(inputs["w_branch_gmlp"]).reshape(512, D),
        "w_gate": f(inputs["w_gate"]).reshape(D, 2 * D), "w_out": f(inputs["w_out"]).reshape(D, D),
        "norms": np.ascontiguousarray(norms), "lngb": np.ascontiguousarray(lngb), "wsT": wsT, "trilT": trilT,
        "bsT": bsT, "bgate": bgate, "ident": ident, "shodd": shodd, "ej": ej,
    }
    in_maps = []
    for c in range(8):
        b, p = c // 2, c % 2
        xb = x[b].reshape(16, BLK, D)
        m = dict(shared)
        m["x_own"] = np.ascontiguousarray(xb[OWN[p]].reshape(TOK, D))
        cmm = np.zeros((128, 2, 2, 256), np.float32)
        cmm[:, p] = cm
        m["cmask"] = cmm.reshape(128, 1024)
        m["past"], m["nb"] = _tables(p)
        in_maps.append(m)
    return in_maps


def assemble(outs):
    y = np.zeros((BATCH, 16, BLK, D), np.float32)
    for c in range(8):
        b, p = c // 2, c % 2
        y[b, OWN[p]] = np.asarray(outs[c], dtype=np.float32).reshape(8, BLK, D)
    return y.reshape(BATCH, SEQ, D)


_NC = None


def kernel(**inputs):
    global _NC
    if _NC is None:
        _NC = build_program()
    in_maps = make_in_maps(inputs)
    res = run_bass_kernel_spmd(_NC, in_maps, core_ids=list(range(8)))
    return assemble([r["out"] for r in res.results])
```
